# Optimizing a Trainium2 kernel written in Bass

```python
import math
import jax, jax.numpy as jnp
from jax import lax
import numpy as np

D_MODEL = 1024
BATCH = 4
SEQ = 8192
DEPTH = 4

N_CONV_LAYERS = DEPTH // 2
N_ATTN_LAYERS = DEPTH - N_CONV_LAYERS
N_HEADS = 8
HEAD_DIM = D_MODEL // (2 * N_HEADS)
V_HEAD_DIM = 2 * HEAD_DIM
CONV_WIDTH = 31
FFN_CONV_WIDTH = 3
D_FF = ((8 * D_MODEL // 3 + 255) // 256) * 256
N_BUCKETS = 32
MAX_DISTANCE = 128
MAX_EXACT = N_BUCKETS // 2
Q_BLOCK = 128
EPS = 1e-6

kernel_name = "yoco_conformer_diffattn_hybrid"


def rms_norm(x, g):
    x32 = x.astype(jnp.float32)
    y = x32 * lax.rsqrt(jnp.mean(x32 * x32, axis=-1, keepdims=True) + EPS)
    return (y * g.astype(jnp.float32)).astype(x.dtype)


def layer_norm(x, g, b):
    x32 = x.astype(jnp.float32)
    mu = jnp.mean(x32, axis=-1, keepdims=True)
    var = jnp.mean(jnp.square(x32 - mu), axis=-1, keepdims=True)
    y = (x32 - mu) * lax.rsqrt(var + EPS) * g.astype(jnp.float32) + b.astype(jnp.float32)
    return y.astype(x.dtype)


def causal_dwconv(x, w, b):
    width = w.shape[0]
    y = lax.conv_general_dilated(
        x, w[:, None, :].astype(x.dtype), window_strides=(1,), padding=[(width - 1, 0)],
        dimension_numbers=("NWC", "WIO", "NWC"), feature_group_count=x.shape[-1])
    return y + b


def conformer_conv(h, w1, b1, dw, dwb, ln_g, ln_b, w2, b2):
    u = h @ w1 + b1
    a, gt = jnp.split(u, 2, axis=-1)
    u = a * jax.nn.sigmoid(gt)
    u = causal_dwconv(u, dw, dwb)
    u = jax.nn.silu(layer_norm(u, ln_g, ln_b))
    return u @ w2 + b2


def conv_ffn(h, w_in, dw, dwb, w_out):
    u = causal_dwconv(h @ w_in, dw, dwb)
    g, v = jnp.split(u, 2, axis=-1)
    return (jax.nn.silu(g) * v) @ w_out


def t5_bucket(n):
    n_f = jnp.maximum(n, 1).astype(jnp.float32)
    large = MAX_EXACT + (jnp.log(n_f / MAX_EXACT) / math.log(MAX_DISTANCE / MAX_EXACT)
                         * (N_BUCKETS - MAX_EXACT)).astype(jnp.int32)
    large = jnp.minimum(large, N_BUCKETS - 1)
    return jnp.where(n < MAX_EXACT, n, large)


def diff_attention(h, k, v, w_q, lam_vec, subln_g, w_o, rel_bias, lambda_init):
    bsz, seq, _ = h.shape
    n_blocks = seq // Q_BLOCK
    q = (h @ w_q).reshape(bsz, n_blocks, Q_BLOCK, N_HEADS, 2, HEAD_DIM)
    q = jnp.moveaxis(q, 1, 0)
    lv = lam_vec.astype(jnp.float32)
    lam = jnp.exp(jnp.sum(lv[0] * lv[1])) - jnp.exp(jnp.sum(lv[2] * lv[3])) + lambda_init
    starts = jnp.arange(n_blocks, dtype=jnp.int32) * Q_BLOCK
    k_pos = jnp.arange(seq, dtype=jnp.int32)
    scale = HEAD_DIM ** -0.5

    def one_block(args):
        q_blk, start = args
        q_pos = start + jnp.arange(Q_BLOCK, dtype=jnp.int32)
        rel = q_pos[:, None] - k_pos[None, :]
        bias = jnp.transpose(rel_bias[t5_bucket(jnp.maximum(rel, 0))], (2, 0, 1)).astype(jnp.float32)
        s = jnp.einsum("bqhcd,bkhcd->bhcqk", q_blk, k).astype(jnp.float32) * scale + bias[None, :, None]
        s = jnp.where(rel >= 0, s, -jnp.inf)
        p = jax.nn.softmax(s, axis=-1)
        a = (p[:, :, 0] - lam * p[:, :, 1]).astype(v.dtype)
        return jnp.einsum("bhqk,bkhe->bqhe", a, v)

    o = lax.map(one_block, (q, starts))
    o = jnp.moveaxis(o, 0, 1).reshape(bsz, seq, N_HEADS, V_HEAD_DIM)
    o = rms_norm(o, subln_g) * (1.0 - lambda_init)
    return o.reshape(bsz, seq, N_HEADS * V_HEAD_DIM) @ w_o


def setup_inputs(seed: int = 0) -> dict:
    key = jax.random.key(seed)
    ks = jax.random.split(key, 25)
    f32 = jnp.float32

    def nrm(k, shape, s):
        return jax.random.normal(k, shape, f32) * s

    D, F, H, A, Bn = D_MODEL, D_FF, N_HEADS, N_CONV_LAYERS, N_ATTN_LAYERS
    return {
        "x": nrm(ks[0], (BATCH, SEQ, D), 1.0),
        "c": nrm(ks[1], (BATCH, D), 1.0),
        "mod_w": nrm(ks[2], (DEPTH, D, 6 * D), D ** -0.5),
        "mod_b": nrm(ks[3], (DEPTH, 6 * D), 0.02),
        "norm_g": 1.0 + nrm(ks[4], (DEPTH, 4, D), 0.05),
        "cm_w1": nrm(ks[5], (A, D, 2 * D), D ** -0.5),
        "cm_b1": nrm(ks[6], (A, 2 * D), 0.02),
        "cm_dw": nrm(ks[7], (A, CONV_WIDTH, D), CONV_WIDTH ** -0.5),
        "cm_dwb": nrm(ks[8], (A, D), 0.02),
        "cm_ln_g": 1.0 + nrm(ks[9], (A, D), 0.05),
        "cm_ln_b": nrm(ks[10], (A, D), 0.02),
        "cm_w2": nrm(ks[11], (A, D, D), D ** -0.5),
        "cm_b2": nrm(ks[12], (A, D), 0.02),
        "kv_norm_g": 1.0 + nrm(ks[13], (D,), 0.05),
        "w_k": nrm(ks[14], (D, H * 2 * HEAD_DIM), D ** -0.5),
        "w_v": nrm(ks[15], (D, H * V_HEAD_DIM), D ** -0.5),
        "w_q": nrm(ks[16], (Bn, D, H * 2 * HEAD_DIM), D ** -0.5),
        "lam": nrm(ks[17], (Bn, 4, HEAD_DIM), 0.1),
        "subln_g": 1.0 + nrm(ks[18], (Bn, V_HEAD_DIM), 0.05),
        "w_o": nrm(ks[19], (Bn, H * V_HEAD_DIM, D), (H * V_HEAD_DIM) ** -0.5),
        "rel_bias": nrm(ks[20], (N_BUCKETS, H), 0.5),
        "ffn_w_in": nrm(ks[21], (DEPTH, D, 2 * F), D ** -0.5),
        "ffn_dw": nrm(ks[22], (DEPTH, FFN_CONV_WIDTH, 2 * F), FFN_CONV_WIDTH ** -0.5),
        "ffn_dwb": nrm(ks[23], (DEPTH, 2 * F), 0.02),
        "ffn_w_out": nrm(ks[24], (DEPTH, F, D), F ** -0.5),
    }


def reference(x, c, mod_w, mod_b, norm_g, cm_w1, cm_b1, cm_dw, cm_dwb, cm_ln_g, cm_ln_b, cm_w2, cm_b2,
              kv_norm_g, w_k, w_v, w_q, lam, subln_g, w_o, rel_bias, ffn_w_in, ffn_dw, ffn_dwb, ffn_w_out):
    bsz, seq, _ = x.shape
    c_act = jax.nn.silu(c)
    k_shared = None
    v_shared = None
    for l in range(DEPTH):
        mod = c_act @ mod_w[l] + mod_b[l]
        sh_m, sc_m, g_m, sh_f, sc_f, g_f = [m[:, None, :] for m in jnp.split(mod, 6, axis=-1)]

        h = rms_norm(x, norm_g[l, 0]) * (1.0 + sc_m) + sh_m
        if l < N_CONV_LAYERS:
            y = conformer_conv(h, cm_w1[l], cm_b1[l], cm_dw[l], cm_dwb[l], cm_ln_g[l], cm_ln_b[l],
                               cm_w2[l], cm_b2[l])
        else:
            if l == N_CONV_LAYERS:
                hkv = rms_norm(x, kv_norm_g)
                k_shared = (hkv @ w_k).reshape(bsz, seq, N_HEADS, 2, HEAD_DIM)
                v_shared = (hkv @ w_v).reshape(bsz, seq, N_HEADS, V_HEAD_DIM)
            j = l - N_CONV_LAYERS
            lambda_init = 0.8 - 0.6 * math.exp(-0.3 * l)
            y = diff_attention(h, k_shared, v_shared, w_q[j], lam[j], subln_g[j], w_o[j], rel_bias, lambda_init)
        x = x + g_m * rms_norm(y, norm_g[l, 1])

        h = rms_norm(x, norm_g[l, 2]) * (1.0 + sc_f) + sh_f
        y = conv_ffn(h, ffn_w_in[l], ffn_dw[l], ffn_dwb[l], ffn_w_out[l])
        x = x + g_f * rms_norm(y, norm_g[l, 3])
    return x
```

```python
import math
from contextlib import ExitStack

import numpy as np
import concourse.bass as bass
import concourse.mybir as mybir
from concourse.bass_utils import run_bass_kernel_spmd

F32 = mybir.dt.float32
BF16 = mybir.dt.bfloat16
AF = mybir.ActivationFunctionType
ALU = mybir.AluOpType
AX = mybir.AxisListType

EPS = 1e-6
CONVW = 31
LG = 1151
NEG = -30000.0
SEM_LIMIT = 20000
NDMA = 24
CC_BYTES = 2 * 1024 * 1024


class Cfg:
    def __init__(self, D=1024, S=8192, F=2816, B=4, L=4):
        self.D, self.S, self.F, self.B, self.L = D, S, F, B, L
        self.KD = D // 128
        self.KF = F // 128
        self.NH = D // 128
        self.OH = self.NH // 2
        self.To = S // 2
        self.HL = 128
        self.NA = L // 2
        rc = CC_BYTES // (self.To * 2)
        rc = min(rc, D)
        rc = (rc // 128) * 128
        self.RC = rc
        self.NG = D // rc


class T:
    __slots__ = ("w", "r")

    def __init__(self):
        self.w = None
        self.r = {}


class Buf:
    def __init__(self, a):
        self.a = a
        self.t = T()


class K:
    def __init__(self, nc, es):
        self.nc = nc
        self.es = es
        self.eng = {"pe": nc.tensor, "act": nc.scalar, "dve": nc.vector, "pool": nc.gpsimd, "sp": nc.sync}
        self.gen = {e: 0 for e in ("pe", "act", "dve", "pool")}
        self.cnt = {e: 0 for e in ("pe", "act", "dve", "pool")}
        self.semh = {}
        self.final = {}
        for e in self.gen:
            self._newsem(e)
        self.seen = {e: {} for e in self.eng}
        self.dsem = [es.enter_context(nc.semaphore(f"dq{i}")) for i in range(NDMA)]
        for i in range(NDMA):
            self.semh[f"dq{i}"] = self.dsem[i]
        self.dcnt = [0] * NDMA
        self.dnext = 0
        self.ncc = 0
        self.nins = 0

    def _newsem(self, e):
        key = f"{e}{self.gen[e]}"
        self.semh[key] = self.es.enter_context(self.nc.semaphore("s_" + key))
        self.cnt[e] = 0

    def _key(self, e):
        return f"{e}{self.gen[e]}"

    def _wait(self, e, deps):
        best = {}
        for (k, v) in deps:
            if v > best.get(k, 0):
                best[k] = v
        for k, v in best.items():
            if e == "pe" and k.startswith("pe"):
                continue
            if self.seen[e].get(k, 0) >= v:
                continue
            self.eng[e].wait_ge(self.semh[k], v)
            self.seen[e][k] = v
            self.nins += 1

    @staticmethod
    def _deps(reads, writes):
        deps = []
        for t in reads:
            if t.w:
                deps.append(t.w)
        for t in writes:
            if t.w:
                deps.append(t.w)
            deps.extend(t.r.items())
        return deps

    @staticmethod
    def _mark(tok, reads, writes):
        k, v = tok
        for t in reads:
            if t.r.get(k, 0) < v:
                t.r[k] = v
        for t in writes:
            t.w = tok
            t.r = {}

    def op(self, e, fn, reads=(), writes=(), inc=True):
        reads = [b.t if isinstance(b, Buf) else b for b in reads]
        writes = [b.t if isinstance(b, Buf) else b for b in writes]
        self._wait(e, self._deps(reads, writes))
        ins = fn(self.eng[e])
        self.nins += 1
        if inc:
            self.cnt[e] += 1
            tok = (self._key(e), self.cnt[e])
            ins.then_inc(self.semh[tok[0]], 1)
            self._mark(tok, reads, writes)
            if self.cnt[e] >= SEM_LIMIT:
                self.final[tok[0]] = self.cnt[e]
                self.gen[e] += 1
                self._newsem(e)
        else:
            tok = (self._key(e), self.cnt[e] + 1)
            self._mark(tok, reads, writes)
        return tok

    def dma(self, e, out, in_, reads=(), writes=(), **kw):
        reads = [b.t if isinstance(b, Buf) else b for b in reads]
        writes = [b.t if isinstance(b, Buf) else b for b in writes]
        i = self.dnext
        self.dnext = (self.dnext + 1) % NDMA
        key = f"dq{i}"
        deps = self._deps(reads, writes)
        if self.dcnt[i] > 0:
            deps.append((key, 16 * self.dcnt[i]))
        self._wait(e, deps)
        ins = self.eng[e].dma_start(out=out, in_=in_, **kw)
        self.nins += 1
        self.dcnt[i] += 1
        tok = (key, 16 * self.dcnt[i])
        ins.then_inc(self.dsem[i], 16)
        self._mark(tok, reads, writes)
        return tok

    def coll(self, src, dst, reads=(), writes=()):
        reads = [b.t if isinstance(b, Buf) else b for b in reads]
        writes = [b.t if isinstance(b, Buf) else b for b in writes]
        key = f"cc{self.ncc}"
        self.ncc += 1
        sem = self.es.enter_context(self.nc.semaphore("s_" + key))
        self.semh[key] = sem
        self._wait("pool", self._deps(reads, writes))
        ins = self.nc.gpsimd.collective_compute("AllGather", ALU.bypass, replica_groups=[[0, 1], [2, 3], [4, 5], [6, 7]],
                                                ins=[src], outs=[dst])
        ins.then_inc(sem, 1)
        self.nins += 1
        tok = (key, 1)
        self.final[key] = 1
        self._mark(tok, reads, writes)
        return tok

    def all_tokens(self):
        toks = [(k, v) for k, v in self.final.items()]
        toks += [(self._key(e), self.cnt[e]) for e in self.cnt if self.cnt[e] > 0]
        toks += [(f"dq{i}", 16 * self.dcnt[i]) for i in range(NDMA) if self.dcnt[i] > 0]
        return toks

    def barrier(self):
        toks = self.all_tokens()
        for e in ("pe", "act", "dve", "pool", "sp"):
            self._wait(e, toks)


class WStream:
    def __init__(self, k, bufs, srcs):
        self.k, self.bufs, self.srcs = k, bufs, srcs
        self.issued = 0

    def _issue(self):
        i = self.issued
        b = self.bufs[i % len(self.bufs)]
        self.k.dma("pool", b.a[:].rearrange("p k j -> p (k j)"), self.srcs[i], writes=[b])
        self.issued += 1

    def get(self, i):
        while self.issued <= min(i + len(self.bufs) - 1, len(self.srcs) - 1):
            self._issue()
        return self.bufs[i % len(self.bufs)]


def split(total, maxn=512):
    n = -(-total // maxn)
    base = -(-total // n)
    out = []
    s = 0
    while s < total:
        w = min(base, total - s)
        out.append((s, w))
        s += w
    return out


def _fm(v):
    v = np.asarray(v, np.float32).reshape(-1, 128)
    return np.ascontiguousarray(v.T)


def t5_bucket_np(n):
    n = np.asarray(n)
    n_f = np.maximum(n, 1).astype(np.float32)
    large = 16 + (np.log(n_f / np.float32(16)) / np.float32(math.log(128 / 16)) * np.float32(16)).astype(np.int32)
    large = np.minimum(large, 31)
    return np.where(n < 16, n, large)


def _t5_bucket_jaxlike():
    n = np.arange(0, LG, dtype=np.int32)
    return t5_bucket_np(n)


class VecTable:
    def __init__(self):
        self.cols = []
        self.off = {}
        self.n = 0

    def add(self, name, arr):
        arr = np.asarray(arr, np.float32)
        assert arr.shape[0] == 128
        self.off[name] = self.n
        self.cols.append(arr)
        self.n += arr.shape[1]

    def build(self):
        return np.ascontiguousarray(np.concatenate(self.cols, axis=1))


def vec_layout(cfg):
    vt = {}
    n = 0

    def add(name, w):
        nonlocal n
        vt[name] = n
        n += w
    KD, KF, L, NA = cfg.KD, cfg.KF, cfg.L, cfg.NA
    add("cT", KD)
    add("modb", L * 6 * KD)
    for l in range(L):
        for i in range(4):
            add(f"ng{l}{i}", KD)
    for l in range(NA):
        add(f"b1a{l}", KD)
        add(f"b1g{l}", KD)
        add(f"dw{l}", KD * CONVW)
        add(f"dwb{l}", KD)
        add(f"lng{l}", KD)
        add(f"lnb{l}", KD)
        add(f"b2{l}", KD)
    add("kvg", KD)
    for j in range(L - NA):
        add(f"lam{j}", 256)
        add(f"subg{j}", 1)
    for l in range(L):
        add(f"fdw{l}", 3 * 2 * KF)
        add(f"fdwb{l}", 2 * KF)
    add("hm", 1)
    add("hmc", 1)
    return vt, n


def prepare_core(cfg, inp, c):
    D, S, F, L, NA, KD, KF, NH, OH, To, HL = cfg.D, cfg.S, cfg.F, cfg.L, cfg.NA, cfg.KD, cfg.KF, cfg.NH, cfg.OH, cfg.To, cfg.HL
    b, r = c // 2, c % 2
    f32 = np.float32
    m = {}
    x = inp["x"][b]
    x0 = np.zeros((D, HL + To), f32)
    lo = r * To - HL
    src_lo = max(lo, 0)
    x0[:, src_lo - lo:] = x[src_lo:r * To + To].T
    m["x0"] = x0
    vt = VecTable()
    vt.add("cT", _fm(inp["c"][b]))
    vt.add("modb", np.concatenate([_fm(inp["mod_b"][l]) for l in range(L)], 1))
    for l in range(L):
        for i in range(4):
            vt.add(f"ng{l}{i}", _fm(inp["norm_g"][l, i]))
    for l in range(NA):
        vt.add(f"b1a{l}", _fm(inp["cm_b1"][l][:D]))
        vt.add(f"b1g{l}", _fm(inp["cm_b1"][l][D:]))
        dw = inp["cm_dw"][l]
        vt.add(f"dw{l}", np.ascontiguousarray(dw.reshape(CONVW, KD, 128).transpose(2, 1, 0)).reshape(128, KD * CONVW))
        vt.add(f"dwb{l}", _fm(inp["cm_dwb"][l]))
        vt.add(f"lng{l}", _fm(inp["cm_ln_g"][l]))
        vt.add(f"lnb{l}", _fm(inp["cm_ln_b"][l]))
        vt.add(f"b2{l}", _fm(inp["cm_b2"][l]))
    vt.add("kvg", _fm(inp["kv_norm_g"]))
    for j in range(L - NA):
        vt.add(f"lam{j}", np.tile(np.asarray(inp["lam"][j], f32).reshape(1, 256), (128, 1)))
        vt.add(f"subg{j}", np.asarray(inp["subln_g"][j], f32).reshape(128, 1))
    for l in range(L):
        fdw = inp["ffn_dw"][l]
        vt.add(f"fdw{l}", np.ascontiguousarray(fdw.reshape(3, 2 * KF, 128).transpose(2, 0, 1)).reshape(128, 3 * 2 * KF))
        vt.add(f"fdwb{l}", _fm(inp["ffn_dwb"][l]))
    vt.add("hm", np.full((128, 1), float(r), f32))
    vt.add("hmc", np.full((128, 1), float(1 - r), f32))
    lay, n = vec_layout(cfg)
    assert lay == vt.off and n == vt.n
    m["vecs"] = vt.build()
    w1 = np.stack([inp["cm_w1"][l].reshape(KD, 128, 2, KD, 128).transpose(3, 1, 0, 2, 4).reshape(KD, 128, KD * 256)
                   for l in range(NA)])
    m["w1u"] = np.ascontiguousarray(w1, f32)
    w2 = np.stack([inp["cm_w2"][l].reshape(KD, 128, KD, 128).transpose(2, 1, 0, 3).reshape(KD, 128, KD * 128)
                   for l in range(NA)])
    m["w2u"] = np.ascontiguousarray(w2, f32)
    wi = np.stack([inp["ffn_w_in"][l].reshape(KD, 128, 2, KF, 128).transpose(3, 1, 0, 2, 4).reshape(KF, 128, KD * 256)
                   for l in range(L)])
    m["winu"] = np.ascontiguousarray(wi, f32)
    wo = np.stack([inp["ffn_w_out"][l].reshape(KF, 128, KD, 128).transpose(2, 1, 0, 3).reshape(KD, 128, KF * 128)
                   for l in range(L)])
    m["woutu"] = np.ascontiguousarray(wo, f32)
    hs = slice(r * OH * 128, (r + 1) * OH * 128)

    def own(w):
        return np.ascontiguousarray(w.reshape(KD, 128, NH * 128)[:, :, hs].transpose(1, 0, 2).reshape(128, KD * OH * 128), f32)
    m["wk"] = own(inp["w_k"])
    m["wv"] = own(inp["w_v"])
    m["wq"] = np.stack([own(inp["w_q"][j]) for j in range(L - NA)])
    wou = np.stack([inp["w_o"][j].reshape(NH, 128, KD, 128).transpose(2, 1, 0, 3).reshape(KD, 128, NH * 128)
                    for j in range(L - NA)])
    m["wou"] = np.ascontiguousarray(wou, f32)
    m["modw"] = np.ascontiguousarray(inp["mod_w"][:, :, r * 3 * D:(r + 1) * 3 * D], f32)
    rbx = np.ones((33, OH), f32)
    rbx[:32] = inp["rel_bias"][:, r * OH:(r + 1) * OH]
    m["rbx"] = rbx
    m["ident"] = np.eye(128, dtype=f32)
    m["antiid"] = np.ascontiguousarray(np.eye(128, dtype=f32)[::-1])
    ohc = np.zeros((33, LG), f32)
    nn = np.arange(LG) - 511
    bk = t5_bucket_np(np.maximum(nn, 0))
    pos = nn >= 0
    ohc[bk[pos], np.nonzero(pos)[0]] = 1.0
    ohc[31, pos] -= 1.0
    ohc[32, ~pos] = NEG
    m["ohc"] = ohc
    return m


def build(cfg, stop_after=None):
    D, S, F, L, NA, KD, KF, NH, OH, To, HL = cfg.D, cfg.S, cfg.F, cfg.L, cfg.NA, cfg.KD, cfg.KF, cfg.NH, cfg.OH, cfg.To, cfg.HL
    RC, NG = cfg.RC, cfg.NG
    SQD = math.sqrt(D)
    nc = bass.Bass("TRN2", target_bir_lowering=False)
    VO, NV = vec_layout(cfg)

    def din(name, shape, dt=F32):
        return nc.dram_tensor(name, list(shape), dt, kind="ExternalInput").ap()

    def dint(name, shape, dt):
        return nc.dram_tensor(name, list(shape), dt).ap()

    x0 = din("x0", [D, HL + To])
    vecs_d = din("vecs", [128, NV])
    w1u = din("w1u", [NA, KD, 128, KD * 256])
    w2u = din("w2u", [NA, KD, 128, KD * 128])
    winu = din("winu", [L, KF, 128, KD * 256])
    woutu = din("woutu", [L, KD, 128, KF * 128])
    wk_d = din("wk", [128, KD * OH * 128])
    wv_d = din("wv", [128, KD * OH * 128])
    wq_d = din("wq", [L - NA, 128, KD * OH * 128])
    wou = din("wou", [L - NA, KD, 128, NH * 128])
    modw = din("modw", [L, D, 3 * D])
    rbx_d = din("rbx", [33, OH])
    ident_d = din("ident", [128, 128])
    antiid_d = din("antiid", [128, 128])
    ohc_d = din("ohc", [33, LG])
    yout = nc.dram_tensor("yout", [D, To], F32, kind="ExternalOutput").ap()

    XA = dint("XA", [D, HL + To], F32)
    XB = dint("XB", [D, HL + To], F32)
    tXA, tXB, tX0, tY = T(), T(), T(), T()
    modsrc = dint("modsrc", [L, 3 * D], F32)
    moddst = dint("moddst", [2 * L, 3 * D], F32)
    gvec = dint("gvec", [OH, LG + 1], F32)
    xnsrc = [dint(f"xnsrc{i}", [RC, To], BF16) for i in range(NG)]
    xndst = [dint(f"xndst{i}", [2 * RC, To], BF16) for i in range(NG)]
    t_xnsrc = [T() for _ in range(NG)]
    t_xndst = [T() for _ in range(NG)]
    Kt = dint("Kt", [OH, 128, S], BF16)
    Qt = dint("Qt", [OH, 128, S], BF16)
    Vd = dint("Vd", [OH, 128, S // 128, 128], BF16)
    tKt, tQt, tVd = T(), T(), T()
    Osrc = [dint(f"Osrc{h}", [128, S], BF16) for h in range(OH)]
    Odst = [dint(f"Odst{h}", [256, S], BF16) for h in range(OH)]
    t_osrc = [T() for _ in range(OH)]
    t_odst = [T() for _ in range(OH)]

    with ExitStack() as es:
        k = K(nc, es)

        uid = [0]

        def sb(st, name, shape, dt):
            uid[0] += 1
            return Buf(st.enter_context(nc.sbuf_tensor(f"sb{uid[0]}_{name}", list(shape), dt)))

        def pst(st, name, shape):
            uid[0] += 1
            return Buf(st.enter_context(nc.psum_tensor(f"ps{uid[0]}_{name}", list(shape), F32)))

        block = es.enter_context(nc.Block())

        V = sb(es, "V", [128, NV], F32)
        SC = sb(es, "SC", [128, L * 6 * KD + KD + 8], F32)
        modT = sb(es, "modT", [128, L * 6 * KD], F32)
        ident_bf = sb(es, "ident_bf", [128, 128], BF16)
        ones_bf = sb(es, "ones_bf", [128, 128], BF16)
        antif = sb(es, "antif", [128, 128], F32)

        def vcol(name, i=0, w=1):
            o = VO[name] + i
            return V.a[:, o:o + w]

        def sc_off(l, which):
            return (l * 4 + which) * KD
        KVG_O = L * 4 * KD
        NLAM_O = KVG_O + KD
        SG_O = NLAM_O + 2

        def scv(off, i=0, w=1):
            return SC.a[:, off + i:off + i + w]

        def modcol(l, which, kk):
            o = l * 6 * KD + which * KD + kk
            return modT.a[:, o:o + 1]

        @block.sync
        def _(sync):
            k.dma("sp", V.a[:], vecs_d, writes=[V])
            with ExitStack() as ps_:
                identf = sb(ps_, "identf", [128, 128], F32)
                cact = sb(ps_, "cact", [128, KD], F32)
                wt = [sb(ps_, f"mwt{i}", [128, KD, 512], F32) for i in range(2)]
                rowbuf = sb(ps_, "rowbuf", [1, L * 3 * D], F32)
                rbx = sb(ps_, "rbx", [33, OH], F32)
                ohc = sb(ps_, "ohc", [33, LG], F32)
                grow = sb(ps_, "grow", [OH, LG + 1], F32)
                lt = sb(ps_, "lt", [128, 128], F32)
                ls = sb(ps_, "ls", [128, 4], F32)
                pp = [pst(ps_, f"pp{i}", [128, 512]) for i in range(2)]
                k.dma("sp", identf.a[:], ident_d, writes=[identf])
                k.dma("sp", antif.a[:], antiid_d, writes=[antif])
                k.op("act", lambda e: e.activation(out=ident_bf.a[:], in_=identf.a[:], func=AF.Copy), [identf], [ident_bf])
                k.op("dve", lambda e: e.memset(ones_bf.a[:], 1.0), [], [ones_bf])
                k.op("act", lambda e: e.activation(out=cact.a[:], in_=vcol("cT", 0, KD), func=AF.Silu), [V], [cact])
                it = 0
                for l in range(L):
                    for (n0, n) in split(3 * D, 512):
                        w_ = wt[it % 2]
                        p_ = pp[it % 2]
                        it += 1
                        k.dma("sp", w_.a[:, :, 0:n], modw[l].rearrange("(k p) n -> p k n", p=128)[:, :, n0:n0 + n], writes=[w_])
                        for kk in range(KD):
                            k.op("pe", lambda e: e.matmul(p_.a[0:1, 0:n], lhsT=cact.a[:, kk:kk + 1], rhs=w_.a[:, kk, 0:n],
                                                          start=(kk == 0), stop=(kk == KD - 1)),
                                 [cact, w_], [p_], inc=(kk == KD - 1))
                        o = l * 3 * D + n0
                        k.op("act", lambda e: e.activation(out=rowbuf.a[0:1, o:o + n], in_=p_.a[0:1, 0:n], func=AF.Copy),
                             [p_], [rowbuf])
                tms, tmd = T(), T()
                k.dma("sp", modsrc.rearrange("(o l) n -> o (l n)", o=1), rowbuf.a[0:1, :], reads=[rowbuf], writes=[tms])
                k.coll(modsrc, moddst, reads=[tms], writes=[tmd])
                for l in range(L):
                    for rr in range(2):
                        o = l * 6 * KD + rr * 3 * KD
                        k.dma("sp", modT.a[:, o:o + 3 * KD], moddst[rr * L + l].rearrange("(j p) -> p j", p=128),
                              reads=[tmd], writes=[modT], allow_slow_non_contiguous=True)
                k.op("dve", lambda e: e.tensor_tensor(out=modT.a[:], in0=modT.a[:], in1=vcol("modb", 0, L * 6 * KD), op=ALU.add),
                     [modT, V], [modT])
                for l in range(L):
                    for (which, scw, gw, ngA, ngG) in ((0, 1, 2, 0, 1), (2, 4, 5, 2, 3)):
                        oA = sc_off(l, which)
                        oG = sc_off(l, which + 1)
                        mo = l * 6 * KD
                        k.op("dve", lambda e: e.tensor_scalar(out=SC.a[:, oA:oA + KD], in0=modT.a[:, mo + scw * KD:mo + (scw + 1) * KD],
                                                              scalar1=1.0, scalar2=SQD, op0=ALU.add, op1=ALU.mult), [modT], [SC])
                        k.op("dve", lambda e: e.tensor_tensor(out=SC.a[:, oA:oA + KD], in0=SC.a[:, oA:oA + KD],
                                                              in1=vcol(f"ng{l}{ngA}", 0, KD), op=ALU.mult), [SC, V], [SC])
                        k.op("dve", lambda e: e.tensor_scalar(out=SC.a[:, oG:oG + KD], in0=modT.a[:, mo + gw * KD:mo + (gw + 1) * KD],
                                                              scalar1=SQD, scalar2=None, op0=ALU.mult), [modT], [SC])
                        k.op("dve", lambda e: e.tensor_tensor(out=SC.a[:, oG:oG + KD], in0=SC.a[:, oG:oG + KD],
                                                              in1=vcol(f"ng{l}{ngG}", 0, KD), op=ALU.mult), [SC, V], [SC])
                k.op("dve", lambda e: e.tensor_scalar(out=SC.a[:, KVG_O:KVG_O + KD], in0=vcol("kvg", 0, KD), scalar1=SQD, scalar2=None,
                                                      op0=ALU.mult), [V], [SC])
                for j in range(L - NA):
                    l = NA + j
                    lam_init = 0.8 - 0.6 * math.exp(-0.3 * l)
                    lo = VO[f"lam{j}"]
                    k.op("dve", lambda e: e.tensor_tensor(out=lt.a[:, 0:64], in0=V.a[:, lo:lo + 64], in1=V.a[:, lo + 64:lo + 128], op=ALU.mult),
                         [V], [lt])
                    k.op("dve", lambda e: e.tensor_tensor(out=lt.a[:, 64:128], in0=V.a[:, lo + 128:lo + 192], in1=V.a[:, lo + 192:lo + 256],
                                                          op=ALU.mult), [V, lt], [lt])
                    k.op("dve", lambda e: e.tensor_reduce(out=ls.a[:, 0:1], in_=lt.a[:, 0:64], axis=AX.X, op=ALU.add), [lt], [ls])
                    k.op("dve", lambda e: e.tensor_reduce(out=ls.a[:, 1:2], in_=lt.a[:, 64:128], axis=AX.X, op=ALU.add), [lt, ls], [ls])
                    k.op("act", lambda e: e.activation(out=ls.a[:, 2:4], in_=ls.a[:, 0:2], func=AF.Exp), [ls], [ls])
                    k.op("dve", lambda e: e.tensor_tensor(out=ls.a[:, 0:1], in0=ls.a[:, 3:4], in1=ls.a[:, 2:3], op=ALU.subtract), [ls], [ls])
                    k.op("dve", lambda e: e.tensor_scalar(out=SC.a[:, NLAM_O + j:NLAM_O + j + 1], in0=ls.a[:, 0:1], scalar1=-lam_init,
                                                          scalar2=None, op0=ALU.add), [ls], [SC])
                    k.op("dve", lambda e: e.tensor_scalar(out=SC.a[:, SG_O + j:SG_O + j + 1], in0=vcol(f"subg{j}"),
                                                          scalar1=(1.0 - lam_init) * math.sqrt(128.0), scalar2=None, op0=ALU.mult), [V], [SC])
                k.dma("sp", rbx.a[:], rbx_d, writes=[rbx])
                k.dma("sp", ohc.a[:], ohc_d, writes=[ohc])
                for (n0, n) in split(LG, 512):
                    p_ = pp[it % 2]
                    it += 1
                    k.op("pe", lambda e: e.matmul(p_.a[0:OH, 0:n], lhsT=rbx.a[:, :], rhs=ohc.a[:, n0:n0 + n], start=True, stop=True),
                         [rbx, ohc], [p_])
                    k.op("act", lambda e: e.activation(out=grow.a[:, n0:n0 + n], in_=p_.a[0:OH, 0:n], func=AF.Copy), [p_], [grow])
                tgv = T()
                k.dma("sp", gvec[:, 0:LG], grow.a[:, 0:LG], reads=[grow], writes=[tgv])
                k.barrier()

            def rms_rstd(st_sq, W, rstd, psb, extra_eps=D * EPS):
                for i, (n0, n) in enumerate(split(W, 512)):
                    p_ = psb[i % len(psb)]
                    for kk in range(KD):
                        k.op("pe", lambda e: e.matmul(p_.a[:, 0:n], lhsT=ones_bf.a[:], rhs=st_sq.a[:, kk, n0:n0 + n],
                                                      start=(kk == 0), stop=(kk == KD - 1)), [ones_bf, st_sq], [p_], inc=(kk == KD - 1))
                    k.op("act", lambda e: e.activation(out=rstd.a[:, n0:n0 + n], in_=p_.a[:, 0:n], func=AF.Ln, bias=extra_eps, scale=1.0),
                         [p_], [rstd])
                    k.op("act", lambda e: e.activation(out=rstd.a[:, n0:n0 + n], in_=rstd.a[:, n0:n0 + n], func=AF.Exp, scale=-0.5),
                         [rstd], [rstd])

            def xview(xd, c0, c1):
                return xd.rearrange("(k p) t -> p k t", p=128)[:, :, c0:c1]

            def tiles_for(Hs, Ts):
                nt = -(-To // Ts)
                step = -(-(To // nt) // 64) * 64
                bounds = [-Hs] + [min(To, step * i) for i in range(1, nt)] + [To]
                return [(bounds[i], bounds[i + 1]) for i in range(len(bounds) - 1)]

            def tile_w(Hs, Ts):
                return max(b - a for (a, b) in tiles_for(Hs, Ts))

            def norm_in(st, xin, txin, a, H, b, A_off, B_l, B_which, xt, sq, rstd, tmp, h, psb, mask_h):
                W = b - a + H
                k.dma("sp", xt.a[:, :, 0:W], xview(xin, HL + a - H, HL + b), reads=[txin], writes=[xt])
                k.op("act", lambda e: e.activation(out=sq.a[:, :, 0:W], in_=xt.a[:, :, 0:W], func=AF.Square), [xt], [sq])
                rms_rstd(sq, W, rstd, psb)
                for kk in range(KD):
                    t_ = tmp[kk % 2]
                    k.op("dve", lambda e: e.scalar_tensor_tensor(out=t_.a[:, 0:W], in0=xt.a[:, kk, 0:W], scalar=scv(A_off, kk),
                                                                 in1=rstd.a[:, 0:W], op0=ALU.mult, op1=ALU.mult), [xt, SC, rstd], [t_])
                    k.op("act", lambda e: e.activation(out=h.a[:, kk, 0:W], in_=t_.a[:, 0:W], func=AF.Identity,
                                                       bias=modcol(B_l, B_which, kk), scale=1.0), [t_, modT], [h])
                if mask_h > 0:
                    k.op("dve", lambda e: e.tensor_scalar(out=h.a[:, :, 0:mask_h], in0=h.a[:, :, 0:mask_h], scalar1=vcol("hm"), scalar2=None,
                                                          op0=ALU.mult), [h, V], [h])

            def resid_out(y, Wo, G_off, xin, txin, xout, txout, a, b, sq, rstd, tmp, xres, psb, emit_xn=None, is_final=False):
                k.op("act", lambda e: e.activation(out=sq.a[:, :, 0:Wo], in_=y.a[:, :, 0:Wo], func=AF.Square), [y], [sq])
                rms_rstd(sq, Wo, rstd, psb)
                for kk in range(KD):
                    xr = xres[kk % 2]
                    t_ = tmp[kk % 2]
                    k.dma("sp", xr.a[:, 0:Wo], xin[kk * 128:(kk + 1) * 128, HL + a:HL + b], reads=[txin], writes=[xr])
                    k.op("dve", lambda e: e.tensor_tensor(out=t_.a[:, 0:Wo], in0=y.a[:, kk, 0:Wo], in1=rstd.a[:, 0:Wo], op=ALU.mult),
                         [y, rstd], [t_])
                    k.op("act", lambda e: e.activation(out=t_.a[:, 0:Wo], in_=t_.a[:, 0:Wo], func=AF.Identity, scale=scv(G_off, kk)),
                         [t_, SC], [t_])
                    k.op("pool", lambda e: e.tensor_tensor(out=y.a[:, kk, 0:Wo], in0=t_.a[:, 0:Wo], in1=xr.a[:, 0:Wo], op=ALU.add),
                         [t_, xr], [y])
                if is_final:
                    o0 = max(0, -a)
                    k.dma("sp", xview(xout, a + o0, b), y.a[:, :, o0:Wo], reads=[y], writes=[txout])
                else:
                    k.dma("sp", xview(xout, HL + a, HL + b), y.a[:, :, 0:Wo], reads=[y], writes=[txout])
                if emit_xn is not None:
                    hbuf = emit_xn
                    k.op("act", lambda e: e.activation(out=sq.a[:, :, 0:Wo], in_=y.a[:, :, 0:Wo], func=AF.Square), [y], [sq])
                    rms_rstd(sq, Wo, rstd, psb)
                    o0 = max(0, -a)
                    for kk in range(KD):
                        k.op("dve", lambda e: e.tensor_tensor(out=hbuf.a[:, kk, 0:Wo], in0=y.a[:, kk, 0:Wo], in1=rstd.a[:, 0:Wo], op=ALU.mult),
                             [y, rstd], [hbuf])
                    per = RC // 128
                    for i in range(NG):
                        k.dma("sp", xnsrc[i].rearrange("(k p) t -> p k t", p=128)[:, :, a + o0:b],
                              hbuf.a[:, i * per:(i + 1) * per, o0:Wo], reads=[hbuf], writes=[t_xnsrc[i]])

            def exchange_xn():
                for i in range(NG):
                    k.coll(xnsrc[i], xndst[i], reads=[t_xnsrc[i]], writes=[t_xndst[i]])

            def mixer_sublayer(l, Hs, xin, txin, xout, txout):
                H = CONVW - 1
                Ts = min(512, To)
                WO = tile_w(Hs, Ts)
                WM = WO + H
                with ExitStack() as st:
                    xt = sb(st, "m_xt", [128, KD, WM], F32)
                    ybuf = sb(st, "m_y", [128, KD, WO], F32)
                    sq = sb(st, "m_sq", [128, KD, WM], BF16)
                    rstd = sb(st, "m_rstd", [128, WM], F32)
                    r2 = sb(st, "m_r2", [128, 3, 512], F32)
                    tmp = [sb(st, f"m_tmp{i}", [128, WM], F32) for i in range(2)]
                    xres = [sb(st, f"m_xr{i}", [128, WO], F32) for i in range(2)]
                    h = sb(st, "m_h", [128, KD, WM], BF16)
                    u = sb(st, "m_u", [128, KD, WM], BF16)
                    v32 = sb(st, "m_v", [128, KD, WO], F32)
                    vb = sb(st, "m_vb", [128, KD, WO], BF16)
                    z = sb(st, "m_z", [128, KD, WO], BF16)
                    sig = [sb(st, f"m_sig{i}", [128, 512], F32) for i in range(2)]
                    w1b = [sb(st, f"m_w1{i}", [128, KD, 256], BF16) for i in range(2)]
                    w2b = [sb(st, f"m_w2{i}", [128, KD, 128], BF16) for i in range(2)]
                    dg = [sb(st, f"m_dg{i}", [128, CONVW, 128], BF16) for i in range(2)]
                    psA = [pst(st, f"m_psA{i}", [128, 512]) for i in range(2)]
                    psG = [pst(st, f"m_psG{i}", [128, 512]) for i in range(2)]
                    psC = [pst(st, f"m_psC{i}", [128, 512]) for i in range(2)]
                    psS = [pst(st, f"m_psS{i}", [128, 512]) for i in range(2)]
                    wi = 0
                    mtiles = tiles_for(Hs, Ts)
                    w1s = WStream(k, w1b, [w1u[l, oc] for _ in mtiles for oc in range(KD)])
                    w2s = WStream(k, w2b, [w2u[l, oc] for _ in mtiles for oc in range(KD)])
                    w1s.get(0)
                    w2s.get(0)
                    def nin(ti):
                        a, b = mtiles[ti]
                        norm_in(st, xin, txin, a, H, b, sc_off(l, 0), l, 0, xt, sq, rstd, tmp, h, psS, 0)
                    nin(0)
                    for ti, (a, b) in enumerate(mtiles):
                        W = b - a + H
                        Wo = b - a
                        it = 0
                        for oc in range(KD):
                            wb_ = w1s.get(ti * KD + oc)
                            for (n0, n) in split(W, 512):
                                pa, pg, sg_ = psA[it % 2], psG[it % 2], sig[it % 2]
                                it += 1
                                for kk in range(KD):
                                    k.op("pe", lambda e: e.matmul(pa.a[:, 0:n], lhsT=wb_.a[:, kk, 0:128], rhs=h.a[:, kk, n0:n0 + n],
                                                                  start=(kk == 0), stop=(kk == KD - 1)), [wb_, h], [pa], inc=(kk == KD - 1))
                                for kk in range(KD):
                                    k.op("pe", lambda e: e.matmul(pg.a[:, 0:n], lhsT=wb_.a[:, kk, 128:256], rhs=h.a[:, kk, n0:n0 + n],
                                                                  start=(kk == 0), stop=(kk == KD - 1)), [wb_, h], [pg], inc=(kk == KD - 1))
                                k.op("act", lambda e: e.activation(out=sg_.a[:, 0:n], in_=pg.a[:, 0:n], func=AF.Sigmoid,
                                                                   bias=vcol(f"b1g{l}", oc), scale=1.0), [pg, V], [sg_])
                                k.op("dve", lambda e: e.scalar_tensor_tensor(out=u.a[:, oc, n0:n0 + n], in0=pa.a[:, 0:n], scalar=vcol(f"b1a{l}", oc),
                                                                             in1=sg_.a[:, 0:n], op0=ALU.add, op1=ALU.mult), [pa, sg_, V], [u])
                        if ti == 0:
                            nn = Hs + H
                            k.op("dve", lambda e: e.tensor_scalar(out=u.a[:, :, 0:nn], in0=u.a[:, :, 0:nn], scalar1=vcol("hm"), scalar2=None,
                                                                  op0=ALU.mult), [u, V], [u])
                        if ti + 1 < len(mtiles):
                            nin(ti + 1)
                        it = 0
                        for c in range(KD):
                            d_ = dg[c % 2]
                            do = VO[f"dw{l}"] + c * CONVW
                            k.op("pool", lambda e: e.tensor_tensor(out=d_.a[:], in0=ident_bf.a[:].unsqueeze(1).broadcast_to([128, CONVW, 128]),
                                                                   in1=V.a[:, do:do + CONVW].unsqueeze(2).broadcast_to([128, CONVW, 128]),
                                                                   op=ALU.mult), [ident_bf, V], [d_])
                            for (n0, n) in split(Wo, 512):
                                pc = psC[it % 2]
                                it += 1
                                for j in range(CONVW):
                                    k.op("pe", lambda e: e.matmul(pc.a[:, 0:n], lhsT=d_.a[:, j, :], rhs=u.a[:, c, n0 + j:n0 + j + n],
                                                                  start=(j == 0), stop=(j == CONVW - 1)), [d_, u], [pc], inc=(j == CONVW - 1))
                                k.op("act", lambda e: e.activation(out=v32.a[:, c, n0:n0 + n], in_=pc.a[:, 0:n], func=AF.Identity,
                                                                   bias=vcol(f"dwb{l}", c), scale=1.0), [pc, V], [v32])
                        k.op("pool", lambda e: e.tensor_copy(out=vb.a[:, :, 0:Wo], in_=v32.a[:, :, 0:Wo]), [v32], [vb])
                        k.op("act", lambda e: e.activation(out=sq.a[:, :, 0:Wo], in_=v32.a[:, :, 0:Wo], func=AF.Square), [v32], [sq])
                        for i, (n0, n) in enumerate(split(Wo, 512)):
                            pm, pq = psS[0], psS[1]
                            for kk in range(KD):
                                k.op("pe", lambda e: e.matmul(pm.a[:, 0:n], lhsT=ones_bf.a[:], rhs=vb.a[:, kk, n0:n0 + n],
                                                              start=(kk == 0), stop=(kk == KD - 1)), [ones_bf, vb], [pm], inc=(kk == KD - 1))
                            for kk in range(KD):
                                k.op("pe", lambda e: e.matmul(pq.a[:, 0:n], lhsT=ones_bf.a[:], rhs=sq.a[:, kk, n0:n0 + n],
                                                              start=(kk == 0), stop=(kk == KD - 1)), [ones_bf, sq], [pq], inc=(kk == KD - 1))
                            k.op("dve", lambda e: e.tensor_scalar(out=r2.a[:, 0, 0:n], in0=pm.a[:, 0:n], scalar1=1.0 / D, scalar2=None,
                                                                  op0=ALU.mult), [pm], [r2])
                            k.op("dve", lambda e: e.tensor_tensor(out=r2.a[:, 1, 0:n], in0=r2.a[:, 0, 0:n], in1=r2.a[:, 0, 0:n], op=ALU.mult),
                                 [r2], [r2])
                            k.op("dve", lambda e: e.scalar_tensor_tensor(out=r2.a[:, 1, 0:n], in0=pq.a[:, 0:n], scalar=1.0 / D, in1=r2.a[:, 1, 0:n],
                                                                         op0=ALU.mult, op1=ALU.subtract), [pq, r2], [r2])
                            k.op("act", lambda e: e.activation(out=r2.a[:, 1, 0:n], in_=r2.a[:, 1, 0:n], func=AF.Ln, bias=EPS, scale=1.0), [r2], [r2])
                            k.op("act", lambda e: e.activation(out=r2.a[:, 1, 0:n], in_=r2.a[:, 1, 0:n], func=AF.Exp, scale=-0.5), [r2], [r2])
                            k.op("dve", lambda e: e.scalar_tensor_tensor(out=r2.a[:, 2, 0:n], in0=r2.a[:, 0, 0:n], scalar=-1.0, in1=r2.a[:, 1, 0:n],
                                                                         op0=ALU.mult, op1=ALU.mult), [r2], [r2])
                            for kk in range(KD):
                                t_ = tmp[kk % 2]
                                k.op("dve", lambda e: e.tensor_tensor(out=t_.a[:, 0:n], in0=v32.a[:, kk, n0:n0 + n], in1=r2.a[:, 1, 0:n],
                                                                      op=ALU.mult), [v32, r2], [t_])
                                k.op("pool", lambda e: e.tensor_tensor(out=t_.a[:, 0:n], in0=t_.a[:, 0:n], in1=r2.a[:, 2, 0:n], op=ALU.add),
                                     [t_, r2], [t_])
                                k.op("act", lambda e: e.activation(out=z.a[:, kk, n0:n0 + n], in_=t_.a[:, 0:n], func=AF.Silu,
                                                                   bias=vcol(f"lnb{l}", kk), scale=vcol(f"lng{l}", kk)), [t_, V], [z])
                        it = 0
                        for oc in range(KD):
                            wb_ = w2s.get(ti * KD + oc)
                            for (n0, n) in split(Wo, 512):
                                pa = psA[it % 2]
                                it += 1
                                for kk in range(KD):
                                    k.op("pe", lambda e: e.matmul(pa.a[:, 0:n], lhsT=wb_.a[:, kk, :], rhs=z.a[:, kk, n0:n0 + n],
                                                                  start=(kk == 0), stop=(kk == KD - 1)), [wb_, z], [pa], inc=(kk == KD - 1))
                                k.op("act", lambda e: e.activation(out=ybuf.a[:, oc, n0:n0 + n], in_=pa.a[:, 0:n], func=AF.Identity,
                                                                   bias=vcol(f"b2{l}", oc), scale=1.0), [pa, V], [ybuf])
                        resid_out(ybuf, Wo, sc_off(l, 1), xin, txin, xout, txout, a, b, sq, rstd, tmp, xres, psS)
                    k.barrier()

            def ffn_part1(l, a, b, h, hid, bufs):
                (cg, cv, sl, wins, wouts, ti_, psU, psV, psY) = bufs
                Wo = b - a
                fo = VO[f"fdw{l}"]
                bo = VO[f"fdwb{l}"]
                ntl = split(Wo, 510)
                it = 0
                for i in range(KF):
                    wb_ = wins.get(ti_ * KF + i)
                    for (o0, no) in ntl:
                        n = no + 2
                        s_ = it % 2
                        pu, pv = psU[it % len(psU)], psV[it % len(psV)]
                        it += 1
                        for kk in range(KD):
                            k.op("pe", lambda e: e.matmul(pu.a[:, 0:n], lhsT=wb_.a[:, kk, 0:128], rhs=h.a[:, kk, o0:o0 + n],
                                                          start=(kk == 0), stop=(kk == KD - 1)), [wb_, h], [pu], inc=(kk == KD - 1))
                        for kk in range(KD):
                            k.op("pe", lambda e: e.matmul(pv.a[:, 0:n], lhsT=wb_.a[:, kk, 128:256], rhs=h.a[:, kk, o0:o0 + n],
                                                          start=(kk == 0), stop=(kk == KD - 1)), [wb_, h], [pv], inc=(kk == KD - 1))
                        for (cc, ch, pp_) in ((cg[s_], i, pu), (cv[s_], KF + i, pv)):
                            k.op("act", lambda e: e.activation(out=cc.a[:, 0:no], in_=pp_.a[:, 2:2 + no], func=AF.Identity,
                                                               scale=V.a[:, fo + 2 * 2 * KF + ch:fo + 2 * 2 * KF + ch + 1],
                                                               bias=V.a[:, bo + ch:bo + ch + 1]), [pp_, V], [cc])
                            k.op("dve", lambda e: e.scalar_tensor_tensor(out=cc.a[:, 0:no], in0=pp_.a[:, 1:1 + no], scalar=V.a[:, fo + 2 * KF + ch:fo + 2 * KF + ch + 1],
                                                                         in1=cc.a[:, 0:no], op0=ALU.mult, op1=ALU.add), [pp_, V, cc], [cc])
                            k.op("dve", lambda e: e.scalar_tensor_tensor(out=cc.a[:, 0:no], in0=pp_.a[:, 0:no], scalar=V.a[:, fo + ch:fo + ch + 1],
                                                                         in1=cc.a[:, 0:no], op0=ALU.mult, op1=ALU.add), [pp_, V, cc], [cc])
                        k.op("act", lambda e: e.activation(out=sl[s_].a[:, 0:no], in_=cg[s_].a[:, 0:no], func=AF.Silu), [cg[s_]], [sl[s_]])
                        k.op("pool", lambda e: e.tensor_tensor(out=hid.a[:, i, o0:o0 + no], in0=sl[s_].a[:, 0:no], in1=cv[s_].a[:, 0:no], op=ALU.mult),
                             [sl[s_], cv[s_]], [hid])

            def ffn_part2(l, a, b, hid, y, bufs):
                (cg, cv, sl, wins, wouts, ti_, psU, psV, psY) = bufs
                Wo = b - a
                it = 0
                for oc in range(KD):
                    wb_ = wouts.get(ti_ * KD + oc)
                    for (n0, n) in split(Wo, 512):
                        py = psY[it % 2]
                        it += 1
                        for kk in range(KF):
                            k.op("pe", lambda e: e.matmul(py.a[:, 0:n], lhsT=wb_.a[:, kk, :], rhs=hid.a[:, kk, n0:n0 + n],
                                                          start=(kk == 0), stop=(kk == KF - 1)), [wb_, hid], [py], inc=(kk == KF - 1))
                        k.op("act", lambda e: e.activation(out=y.a[:, oc, n0:n0 + n], in_=py.a[:, 0:n], func=AF.Copy), [py], [y])

            def ffn_sublayer(l, Hs, xin, txin, xout, txout, emit_xn=False, is_final=False):
                H = 2
                Ts = min(704, To)
                WM = tile_w(Hs, Ts) + H
                with ExitStack() as st:
                    xt = sb(st, "f_xt", [128, KD, WM], F32)
                    y = sb(st, "f_y", [128, KD, WM], F32)
                    hid = sb(st, "f_hid", [128, KF, WM], BF16)
                    sq = sb(st, "f_sq", [128, KD, WM], BF16)
                    xnb = sb(st, "f_xnb", [128, KD, WM], BF16) if emit_xn else None
                    rstd = sb(st, "f_rstd", [128, WM], F32)
                    tmp = [sb(st, f"f_tmp{i}", [128, WM], F32) for i in range(2)]
                    xres = [sb(st, f"f_xr{i}", [128, WM], F32) for i in range(2)]
                    h = sb(st, "f_h", [128, KD, WM], BF16)
                    cg = [sb(st, f"f_cg{i}", [128, 512], F32) for i in range(2)]
                    cv = [sb(st, f"f_cv{i}", [128, 512], F32) for i in range(2)]
                    sl = [sb(st, f"f_sl{i}", [128, 512], F32) for i in range(2)]
                    winb = [sb(st, f"f_win{i}", [128, KD, 256], BF16) for i in range(2)]
                    woutb = [sb(st, f"f_wout{i}", [128, KF, 128], BF16) for i in range(2)]
                    psU = [pst(st, f"f_psU{i}", [128, 512]) for i in range(3)]
                    psV = [pst(st, f"f_psV{i}", [128, 512]) for i in range(3)]
                    psS = [pst(st, f"f_psS{i}", [128, 512]) for i in range(2)]
                    psY = [psU[0], psV[0]]
                    ftiles = tiles_for(Hs, Ts)
                    wins = WStream(k, winb, [winu[l, i] for _ in ftiles for i in range(KF)])
                    wouts = WStream(k, woutb, [woutu[l, oc] for _ in ftiles for oc in range(KD)])
                    wins.get(0)
                    wouts.get(0)

                    def nin(ti):
                        a, b = ftiles[ti]
                        norm_in(st, xin, txin, a, H, b, sc_off(l, 2), l, 3, xt, sq, rstd, tmp, h, psS, (Hs + H) if ti == 0 else 0)
                    nin(0)
                    for ti, (a, b) in enumerate(ftiles):
                        bufs = (cg, cv, sl, wins, wouts, ti, psU, psV, psY)
                        Wo = b - a
                        ffn_part1(l, a, b, h, hid, bufs)
                        if ti + 1 < len(ftiles):
                            nin(ti + 1)
                        ffn_part2(l, a, b, hid, y, bufs)
                        resid_out(y, Wo, sc_off(l, 3), xin, txin, xout, txout, a, b, sq, rstd, tmp, xres, psS,
                                  emit_xn=xnb, is_final=is_final)
                    k.barrier()

            def kvq_phase(j, do_kv):
                l = NA + j
                NT = 512
                OW = OH * 128
                with ExitStack() as st:
                    xn = [sb(st, f"p_xn{i}", [128, KD, NT], BF16) for i in range(3)]
                    wf = sb(st, "p_wf", [128, KD, OW], F32)
                    wqb = sb(st, "p_wq", [128, KD, OW], BF16)
                    wkb = sb(st, "p_wk", [128, KD, OW], BF16)
                    wvb = sb(st, "p_wv", [128, KD, OW], BF16)
                    qbias = sb(st, "p_qb", [128, OH], F32)
                    qo = [sb(st, f"p_qo{i}", [128, OH, NT], BF16) for i in range(2)]
                    ko = [sb(st, f"p_ko{i}", [128, OH, NT], BF16) for i in range(2)]
                    vo = [sb(st, f"p_vo{i}", [128, NT // 128, OW], BF16) for i in range(2)]
                    pq = [pst(st, f"p_pq{i}", [128, 512]) for i in range(2)]
                    pk = [pst(st, f"p_pk{i}", [128, 512]) for i in range(2)]
                    pv = [pst(st, f"p_pv{i}", [128, 512]) for i in range(2)]
                    k.dma("sp", wf.a[:].rearrange("p k j -> p (k j)"), wq_d[j], writes=[wf])
                    for m_ in range(OH):
                        for kk in range(KD):
                            k.op("pe", lambda e: e.matmul(pq[0].a[:, m_:m_ + 1], lhsT=wf.a[:, kk, m_ * 128:(m_ + 1) * 128], rhs=modcol(l, 0, kk),
                                                          start=(kk == 0), stop=(kk == KD - 1)), [wf, modT], [pq[0]], inc=(kk == KD - 1))
                    k.op("dve", lambda e: e.tensor_copy(out=qbias.a[:], in_=pq[0].a[:, 0:OH]), [pq[0]], [qbias])
                    for kk in range(KD):
                        k.op("dve", lambda e: e.tensor_scalar(out=wqb.a[:, kk, :], in0=wf.a[:, kk, :], scalar1=scv(sc_off(l, 0), kk), scalar2=None,
                                                              op0=ALU.mult), [wf, SC], [wqb])
                    if do_kv:
                        for (wd, wb_) in ((wk_d, wkb), (wv_d, wvb)):
                            k.dma("sp", wf.a[:].rearrange("p k j -> p (k j)"), wd, writes=[wf])
                            for kk in range(KD):
                                k.op("dve", lambda e: e.tensor_scalar(out=wb_.a[:, kk, :], in0=wf.a[:, kk, :], scalar1=scv(KVG_O, kk), scalar2=None,
                                                                      op0=ALU.mult), [wf, SC], [wb_])
                    per = RC // 128
                    itq = itk = itv = 0
                    ntile = S // NT

                    def load_xn(gi):
                        g0 = gi * NT
                        rr, t0 = g0 // To, g0 % To
                        for i in range(NG):
                            k.dma("sp", xn[gi % 3].a[:, i * per:(i + 1) * per, :],
                                  xndst[i][rr * RC:(rr + 1) * RC, t0:t0 + NT].rearrange("(k p) t -> p k t", p=128),
                                  reads=[t_xndst[i]], writes=[xn[gi % 3]])
                    load_xn(0)
                    for gi in range(ntile):
                        g0 = gi * NT
                        s_ = gi % 2
                        x_ = xn[gi % 3]
                        if gi + 1 < ntile:
                            load_xn(gi + 1)
                        for m_ in range(OH):
                            p_ = pq[itq % 2]
                            itq += 1
                            for kk in range(KD):
                                k.op("pe", lambda e: e.matmul(p_.a[:, 0:NT], lhsT=wqb.a[:, kk, m_ * 128:(m_ + 1) * 128], rhs=x_.a[:, kk, :],
                                                              start=(kk == 0), stop=(kk == KD - 1)), [wqb, x_], [p_], inc=(kk == KD - 1))
                            k.op("act", lambda e: e.activation(out=qo[s_].a[:, m_, :], in_=p_.a[:, 0:NT], func=AF.Identity, bias=qbias.a[:, m_:m_ + 1], scale=1.0),
                                 [p_, qbias], [qo[s_]])
                        k.dma("sp", Qt.rearrange("h p s -> p h s")[:, :, g0:g0 + NT], qo[s_].a[:], reads=[qo[s_]], writes=[tQt])
                        if do_kv:
                            for m_ in range(OH):
                                p_ = pk[itk % 2]
                                itk += 1
                                for kk in range(KD):
                                    k.op("pe", lambda e: e.matmul(p_.a[:, 0:NT], lhsT=wkb.a[:, kk, m_ * 128:(m_ + 1) * 128], rhs=x_.a[:, kk, :],
                                                                  start=(kk == 0), stop=(kk == KD - 1)), [wkb, x_], [p_], inc=(kk == KD - 1))
                                k.op("dve", lambda e: e.tensor_copy(out=ko[s_].a[:, m_, :], in_=p_.a[:, 0:NT]), [p_], [ko[s_]])
                            k.dma("sp", Kt.rearrange("h p s -> p h s")[:, :, g0:g0 + NT], ko[s_].a[:], reads=[ko[s_]], writes=[tKt])
                            for sbk in range(NT // 128):
                                p_ = pv[itv % 2]
                                itv += 1
                                for kk in range(KD):
                                    k.op("pe", lambda e: e.matmul(p_.a[:, 0:OW], lhsT=x_.a[:, kk, sbk * 128:(sbk + 1) * 128], rhs=wvb.a[:, kk, :],
                                                                  start=(kk == 0), stop=(kk == KD - 1)), [wvb, x_], [p_], inc=(kk == KD - 1))
                                k.op("act" if sbk % 2 else "dve",
                                     (lambda e: e.activation(out=vo[s_].a[:, sbk, :], in_=p_.a[:, 0:OW], func=AF.Copy)) if sbk % 2 else
                                     (lambda e: e.tensor_copy(out=vo[s_].a[:, sbk, :], in_=p_.a[:, 0:OW])), [p_], [vo[s_]])
                            for m_ in range(OH):
                                k.dma("sp", Vd[m_][:, g0 // 128:g0 // 128 + NT // 128, :], vo[s_].a[:, :, m_ * 128:(m_ + 1) * 128],
                                      reads=[vo[s_]], writes=[tVd])
                    k.barrier()

            def attn_phase(j):
                QT = 512
                NB = S // 128
                NQ = S // QT
                XS = 704
                with ExitStack() as st:
                    strips = sb(st, "a_strips", [128, OH, 1024], F32)
                    ktb = [sb(st, f"a_kt{i}", [128, S], BF16) for i in range(2)]
                    vbb = [sb(st, f"a_vb{i}", [128, NB, 128], BF16) for i in range(2)]
                    qb = [sb(st, f"a_q{i}", [128, QT], BF16) for i in range(3)]
                    P = [sb(st, f"a_P{i}", [128, 2, QT], BF16) for i in range(4)]
                    sbias = [sb(st, f"a_sb{i}", [128, 2, QT], F32) for i in range(2)]
                    accD = [sb(st, f"a_accD{i}", [128, XS], F32) for i in range(2)]
                    accP = [sb(st, f"a_accP{i}", [128, 2 * QT - XS], F32) for i in range(2)]
                    ones_f = sb(st, "a_onesf", [128, 128], F32)
                    c1 = sb(st, "a_c1", [128, QT], F32)
                    c2 = sb(st, "a_c2", [128, QT], F32)
                    R1 = sb(st, "a_R1", [128, QT], F32)
                    R2 = sb(st, "a_R2", [128, QT], F32)
                    t1 = sb(st, "a_t1", [128, QT], F32)
                    t2 = sb(st, "a_t2", [128, QT], F32)
                    osq = sb(st, "a_osq", [128, QT], BF16)
                    rs = sb(st, "a_rs", [128, QT], F32)
                    ob = [sb(st, f"a_ob{i}", [128, QT], BF16) for i in range(2)]
                    psS = [pst(st, f"a_psS{i}", [128, 2 * QT]) for i in range(2)]
                    psO1 = pst(st, "a_psO1", [128, QT])
                    psO2 = pst(st, "a_psO2", [128, QT])
                    psZa = pst(st, "a_psZa", [128, QT])
                    psZb = pst(st, "a_psZb", [128, QT])
                    hank = sb(st, "a_hank", [128, 1024], F32)
                    k.op("dve", lambda e: e.memset(ones_f.a[:], 1.0), [], [ones_f])
                    for hl in range(OH):
                        k.dma("sp", hank.a[:], bass.AP(gvec.tensor, hl * (LG + 1), [[1, 128], [1, 1024]]), reads=[tgv], writes=[hank])
                        for half in range(2):
                            k.op("pe", lambda e: e.matmul(psO1.a[:], lhsT=antif.a[:], rhs=hank.a[:, half * 512:(half + 1) * 512], start=True, stop=True),
                                 [antif, hank], [psO1])
                            k.op("act", lambda e: e.activation(out=strips.a[:, hl, half * 512:(half + 1) * 512], in_=psO1.a[:], func=AF.Copy),
                                 [psO1], [strips])
                    nlam = scv(NLAM_O + j)
                    sgc = scv(SG_O + j)
                    jobs = [(hl, qi) for hl in range(OH) for qi in range(NQ)]
                    blocks = []
                    for ji, (hl, qi) in enumerate(jobs):
                        for kb in range(4 * qi + 4):
                            blocks.append((ji, hl, qi, kb))

                    def load_head(hl):
                        k.dma("sp", ktb[hl % 2].a[:], Kt[hl], reads=[tKt], writes=[ktb[hl % 2]])
                        k.dma("sp", vbb[hl % 2].a[:], Vd[hl], reads=[tVd], writes=[vbb[hl % 2]])

                    def load_q(ji):
                        hl, qi = jobs[ji]
                        k.dma("sp", qb[ji % 3].a[:], Qt[hl][:, qi * QT:(qi + 1) * QT], reads=[tQt], writes=[qb[ji % 3]])

                    def s_mm(bi):
                        ji, hl, qi, kb = blocks[bi]
                        ps_ = psS[bi % 2]
                        kt_, q_ = ktb[hl % 2], qb[ji % 3]
                        k.op("pe", lambda e: e.matmul(ps_.a[:, 0:QT], lhsT=kt_.a[0:64, kb * 128:(kb + 1) * 128], rhs=q_.a[0:64, :],
                                                      start=True, stop=True), [kt_, q_], [ps_], inc=False)
                        k.op("pe", lambda e: e.matmul(ps_.a[:, QT:2 * QT], lhsT=kt_.a[64:128, kb * 128:(kb + 1) * 128], rhs=q_.a[64:128, :],
                                                      start=True, stop=True), [kt_, q_], [ps_])

                    pending = []

                    def epilogue(ji, hl, qi, bi):
                        aD, aP = accD[ji % 2], accP[ji % 2]
                        o_ = ob[ji % 2]
                        k.op("dve", lambda e: e.tensor_copy(out=c1.a[:], in_=psO1.a[:]), [psO1], [c1])
                        k.op("act", lambda e: e.activation(out=c2.a[:], in_=psO2.a[:], func=AF.Copy), [psO2], [c2])

                        def st1():
                            k.op("pe", lambda e: e.matmul(psZa.a[:], lhsT=ones_f.a[:], rhs=aD.a[:, 0:QT], start=True, stop=True), [ones_f, aD], [psZa])
                            k.op("pe", lambda e: e.matmul(psZb.a[:, 0:XS - QT], lhsT=ones_f.a[:], rhs=aD.a[:, QT:XS], start=True, stop=True),
                                 [ones_f, aD], [psZb], inc=False)
                            k.op("pe", lambda e: e.matmul(psZb.a[:, XS - QT:QT], lhsT=ones_f.a[:], rhs=aP.a[:, :], start=True, stop=True),
                                 [ones_f, aP], [psZb])

                        def st2():
                            k.op("act", lambda e: e.activation(out=R1.a[:], in_=psZa.a[:], func=AF.Ln), [psZa], [R1])
                            k.op("act", lambda e: e.activation(out=R2.a[:], in_=psZb.a[:], func=AF.Ln), [psZb], [R2])
                            k.op("act", lambda e: e.activation(out=R1.a[:], in_=R1.a[:], func=AF.Exp, scale=-1.0), [R1], [R1])
                            k.op("act", lambda e: e.activation(out=R2.a[:], in_=R2.a[:], func=AF.Exp, scale=-1.0), [R2], [R2])
                            k.op("dve", lambda e: e.tensor_tensor(out=t1.a[:], in0=c1.a[:], in1=R1.a[:], op=ALU.mult), [c1, R1], [t1])
                            k.op("dve", lambda e: e.tensor_tensor(out=t2.a[:], in0=c2.a[:], in1=R2.a[:], op=ALU.mult), [c2, R2], [t2])
                            k.op("dve", lambda e: e.scalar_tensor_tensor(out=t1.a[:], in0=t2.a[:], scalar=nlam, in1=t1.a[:], op0=ALU.mult, op1=ALU.add),
                                 [t2, t1, SC], [t1])
                            k.op("act", lambda e: e.activation(out=osq.a[:], in_=t1.a[:], func=AF.Square), [t1], [osq])

                        def st3():
                            k.op("pe", lambda e: e.matmul(psZa.a[:], lhsT=ones_bf.a[:], rhs=osq.a[:], start=True, stop=True), [ones_bf, osq], [psZa])

                        def st4():
                            k.op("act", lambda e: e.activation(out=rs.a[:], in_=psZa.a[:], func=AF.Ln, bias=128.0 * EPS, scale=1.0), [psZa], [rs])
                            k.op("act", lambda e: e.activation(out=rs.a[:], in_=rs.a[:], func=AF.Exp, scale=-0.5), [rs], [rs])
                            k.op("dve", lambda e: e.tensor_tensor(out=t2.a[:], in0=t1.a[:], in1=rs.a[:], op=ALU.mult), [t1, rs], [t2])
                            k.op("act", lambda e: e.activation(out=o_.a[:], in_=t2.a[:], func=AF.Identity, scale=sgc), [t2, SC], [o_])
                            k.dma("sp", Osrc[hl][:, qi * QT:(qi + 1) * QT], o_.a[:], reads=[o_], writes=[t_osrc[hl]])
                            if qi == NQ - 1:
                                k.coll(Osrc[hl], Odst[hl], reads=[t_osrc[hl]], writes=[t_odst[hl]])
                        for d, f in enumerate((st1, st2, st3, st4)):
                            pending.append((bi + 1 + d, f))

                    load_head(0)
                    load_q(0)
                    if len(jobs) > 1:
                        load_q(1)
                    s_mm(0)
                    for bi, (ji, hl, qi, kb) in enumerate(blocks):
                        nkb = 4 * qi + 4
                        if kb == 0:
                            if ji + 2 < len(jobs):
                                load_q(ji + 2)
                            if qi == 0 and hl + 1 < OH:
                                load_head(hl + 1)
                        if bi + 1 < len(blocks):
                            s_mm(bi + 1)
                        ps_ = psS[bi % 2]
                        p_ = P[bi % 4]
                        sb_ = sbias[bi % 2]
                        vb_ = vbb[hl % 2]
                        aD, aP = accD[ji % 2], accP[ji % 2]
                        dd = (kb - 4 * qi) * 128
                        if dd >= -128:
                            c0 = 384 - dd
                            k.op("dve", lambda e: e.scalar_tensor_tensor(
                                out=sb_.a[:], in0=ps_.a[:].rearrange("p (c q) -> p c q", c=2), scalar=0.125,
                                in1=strips.a[:, hl, c0:c0 + QT].unsqueeze(1).broadcast_to([128, 2, QT]), op0=ALU.mult, op1=ALU.add),
                                [ps_, strips], [sb_])
                            k.op("act", lambda e: e.activation(out=p_.a[:], in_=sb_.a[:], func=AF.Exp), [sb_], [p_])
                        else:
                            k.op("act", lambda e: e.activation(out=p_.a[:], in_=ps_.a[:].rearrange("p (c q) -> p c q", c=2), func=AF.Exp,
                                                               scale=0.125), [ps_], [p_])
                        first, last = (kb == 0), (kb == nkb - 1)
                        k.op("pe", lambda e: e.matmul(psO1.a[:], lhsT=vb_.a[:, kb, :], rhs=p_.a[:, 0, :], start=first, stop=last),
                             [vb_, p_], [psO1], inc=False)
                        k.op("pe", lambda e: e.matmul(psO2.a[:], lhsT=vb_.a[:, kb, :], rhs=p_.a[:, 1, :], start=first, stop=last),
                             [vb_, p_], [psO2])
                        pf = p_.a[:].rearrange("p c q -> p (c q)")
                        if first:
                            k.op("dve", lambda e: e.tensor_copy(out=aD.a[:], in_=pf[:, 0:XS]), [p_], [aD])
                            k.op("pool", lambda e: e.tensor_copy(out=aP.a[:], in_=pf[:, XS:2 * QT]), [p_], [aP])
                        else:
                            k.op("dve", lambda e: e.tensor_tensor(out=aD.a[:], in0=pf[:, 0:XS], in1=aD.a[:], op=ALU.add), [p_, aD], [aD])
                            k.op("pool", lambda e: e.tensor_tensor(out=aP.a[:], in0=pf[:, XS:2 * QT], in1=aP.a[:], op=ALU.add), [p_, aP], [aP])
                        due = [f for (d, f) in pending if d <= bi]
                        pending[:] = [(d, f) for (d, f) in pending if d > bi]
                        for f in due:
                            f()
                        if last:
                            epilogue(ji, hl, qi, bi)
                    for (d, f) in sorted(pending, key=lambda t: t[0]):
                        f()
                    k.barrier()

            def attnout_sublayer(j, Hs, xin, txin, xout, txout):
                l = NA + j
                Ts = min(512, To)
                WM = tile_w(Hs, Ts)
                with ExitStack() as st:
                    oa = sb(st, "o_oa", [128, NH, WM], BF16)
                    obb = sb(st, "o_ob", [128, NH, WM], BF16)
                    osel = sb(st, "o_osel", [128, NH, WM], BF16)
                    y = sb(st, "o_y", [128, KD, WM], F32)
                    sq = sb(st, "o_sq", [128, KD, WM], BF16)
                    rstd = sb(st, "o_rstd", [128, WM], F32)
                    tmp = [sb(st, f"o_tmp{i}", [128, WM], F32) for i in range(2)]
                    xres = [sb(st, f"o_xr{i}", [128, WM], F32) for i in range(2)]
                    wob = [sb(st, f"o_wo{i}", [128, NH, 128], BF16) for i in range(2)]
                    psY = [pst(st, f"o_psY{i}", [128, 512]) for i in range(2)]
                    psS = [pst(st, f"o_psS{i}", [128, 512]) for i in range(2)]
                    otiles = tiles_for(Hs, Ts)
                    wos = WStream(k, wob, [wou[j, oc] for _ in otiles for oc in range(KD)])
                    wos.get(0)
                    for ti, (a, b) in enumerate(otiles):
                        W = b - a
                        neg = max(0, -a)
                        if neg > 0:
                            k.op("dve", lambda e: e.memset(oa.a[:, :, 0:neg], 0.0), [], [oa])
                        for hl in range(OH):
                            for rr in range(2):
                                hg = rr * OH + hl
                                k.dma("sp", oa.a[:, hg, neg:W], Odst[hl][rr * 128:(rr + 1) * 128, a + neg:b], reads=[t_odst[hl]], writes=[oa])
                                k.dma("sp", obb.a[:, hg, 0:W], Odst[hl][rr * 128:(rr + 1) * 128, To + a:To + b], reads=[t_odst[hl]], writes=[obb])
                        k.op("dve", lambda e: e.tensor_scalar(out=osel.a[:, :, 0:W], in0=oa.a[:, :, 0:W], scalar1=vcol("hmc"), scalar2=None, op0=ALU.mult),
                             [oa, V], [osel])
                        k.op("dve", lambda e: e.scalar_tensor_tensor(out=osel.a[:, :, 0:W], in0=obb.a[:, :, 0:W], scalar=vcol("hm"), in1=osel.a[:, :, 0:W],
                                                                     op0=ALU.mult, op1=ALU.add), [obb, osel, V], [osel])
                        it = 0
                        for oc in range(KD):
                            wb_ = wos.get(ti * KD + oc)
                            for (n0, n) in split(W, 512):
                                py = psY[it % 2]
                                it += 1
                                for hh in range(NH):
                                    k.op("pe", lambda e: e.matmul(py.a[:, 0:n], lhsT=wb_.a[:, hh, :], rhs=osel.a[:, hh, n0:n0 + n],
                                                                  start=(hh == 0), stop=(hh == NH - 1)), [wb_, osel], [py], inc=(hh == NH - 1))
                                k.op("act", lambda e: e.activation(out=y.a[:, oc, n0:n0 + n], in_=py.a[:, 0:n], func=AF.Copy), [py], [y])
                        resid_out(y, W, sc_off(l, 1), xin, txin, xout, txout, a, b, sq, rstd, tmp, xres, psS)
                    k.barrier()

            HS = [38, 36, 6, 4, 4, 2, 2, 0]
            seq = [(x0, tX0), (XA, tXA), (XB, tXB)]
            state = {"cur": 0}

            def nxt():
                i = state["cur"]
                src = seq[0] if i == 0 else seq[1 + (i - 1) % 2]
                dst = seq[1 + i % 2]
                state["cur"] += 1
                return src[0], src[1], dst[0], dst[1]

            def finish_from(buf, tbuf):
                k.dma("sp", yout, buf[:, HL:HL + To], reads=[tbuf], writes=[tY])

            done = False
            si = 0
            for l in range(NA):
                xi, txi, xo, txo = nxt()
                mixer_sublayer(l, HS[si], xi, txi, xo, txo)
                si += 1
                if stop_after == f"m{l}":
                    finish_from(xo, txo)
                    done = True
                    break
                xi, txi, xo, txo = nxt()
                ffn_sublayer(l, HS[si], xi, txi, xo, txo, emit_xn=(l == NA - 1))
                si += 1
                if stop_after == f"f{l}":
                    finish_from(xo, txo)
                    done = True
                    break
            if not done:
                for j in range(L - NA):
                    l = NA + j
                    exchange_xn()
                    k.barrier()
                    kvq_phase(j, do_kv=(j == 0))
                    attn_phase(j)
                    xi, txi, xo, txo = nxt()
                    attnout_sublayer(j, HS[si], xi, txi, xo, txo)
                    si += 1
                    if stop_after == f"a{l}":
                        finish_from(xo, txo)
                        done = True
                        break
                    last = (l == L - 1)
                    xi, txi, xo, txo = nxt()
                    if last:
                        ffn_sublayer(l, HS[si], xi, txi, yout, tY, emit_xn=False, is_final=True)
                    else:
                        ffn_sublayer(l, HS[si], xi, txi, xo, txo, emit_xn=True)
                    si += 1
                    if stop_after == f"f{l}" and not last:
                        finish_from(xo, txo)
                        done = True
                        break
            k.barrier()
        nc._k_nins = k.nins
    return nc


_CACHE = {}


def run(cfg, inputs, stop_after=None, trace=False):
    key = (cfg.D, cfg.S, cfg.F, stop_after)
    if key not in _CACHE:
        _CACHE[key] = build(cfg, stop_after)
    nc = _CACHE[key]
    in_maps = [prepare_core(cfg, inputs, c) for c in range(8)]
    res = run_bass_kernel_spmd(nc, in_maps, core_ids=list(range(8)), **({"trace": True} if trace else {}))
    out = np.empty((cfg.B, cfg.S, cfg.D), np.float32)
    for c in range(8):
        b, r = c // 2, c % 2
        out[b, r * cfg.To:(r + 1) * cfg.To, :] = res.results[c]["yout"].T
    return out, res


def kernel(**inputs):
    inputs = {k_: np.asarray(v) for k_, v in inputs.items()}
    cfg = Cfg()
    out, _ = run(cfg, inputs)
    return out
```

```python
import math
from contextlib import ExitStack

import numpy as np
import concourse.bass as bass
import concourse.mybir as mybir
from concourse.bass_utils import run_bass_kernel_spmd

F32 = mybir.dt.float32
BF16 = mybir.dt.bfloat16
AF = mybir.ActivationFunctionType
ALU = mybir.AluOpType
AX = mybir.AxisListType

EPS = 1e-6
CONVW = 31
LG = 1151
NEG = -30000.0
SEM_LIMIT = 20000
NDMA = 24
CC_BYTES = 2 * 1024 * 1024


class Cfg:
    def __init__(self, D=1024, S=8192, F=2816, B=4, L=4):
        self.D, self.S, self.F, self.B, self.L = D, S, F, B, L
        self.KD = D // 128
        self.KF = F // 128
        self.NH = D // 128
        self.OH = self.NH // 2
        self.To = S // 2
        self.HL = 128
        self.NA = L // 2
        rc = CC_BYTES // (self.To * 2)
        rc = min(rc, D)
        rc = (rc // 128) * 128
        self.RC = rc
        self.NG = D // rc


class T:
    __slots__ = ("w", "r")

    def __init__(self):
        self.w = None
        self.r = {}


class Buf:
    def __init__(self, a):
        self.a = a
        self.t = T()


class K:
    def __init__(self, nc, es):
        self.nc = nc
        self.es = es
        self.eng = {"pe": nc.tensor, "act": nc.scalar, "dve": nc.vector, "pool": nc.gpsimd, "sp": nc.sync}
        self.gen = {e: 0 for e in ("pe", "act", "dve", "pool")}
        self.cnt = {e: 0 for e in ("pe", "act", "dve", "pool")}
        self.semh = {}
        self.final = {}
        for e in self.gen:
            self._newsem(e)
        self.seen = {e: {} for e in self.eng}
        self.dsem = [es.enter_context(nc.semaphore(f"dq{i}")) for i in range(NDMA)]
        for i in range(NDMA):
            self.semh[f"dq{i}"] = self.dsem[i]
        self.dcnt = [0] * NDMA
        self.dnext = 0
        self.ncc = 0
        self.nins = 0

    def _newsem(self, e):
        key = f"{e}{self.gen[e]}"
        self.semh[key] = self.es.enter_context(self.nc.semaphore("s_" + key))
        self.cnt[e] = 0

    def _key(self, e):
        return f"{e}{self.gen[e]}"

    def _wait(self, e, deps):
        best = {}
        for (k, v) in deps:
            if v > best.get(k, 0):
                best[k] = v
        for k, v in best.items():
            if e == "pe" and k.startswith("pe"):
                continue
            if self.seen[e].get(k, 0) >= v:
                continue
            self.eng[e].wait_ge(self.semh[k], v)
            self.seen[e][k] = v
            self.nins += 1

    @staticmethod
    def _deps(reads, writes):
        deps = []
        for t in reads:
            if t.w:
                deps.append(t.w)
        for t in writes:
            if t.w:
                deps.append(t.w)
            deps.extend(t.r.items())
        return deps

    @staticmethod
    def _mark(tok, reads, writes):
        k, v = tok
        for t in reads:
            if t.r.get(k, 0) < v:
                t.r[k] = v
        for t in writes:
            t.w = tok
            t.r = {}

    def op(self, e, fn, reads=(), writes=(), inc=True):
        reads = [b.t if isinstance(b, Buf) else b for b in reads]
        writes = [b.t if isinstance(b, Buf) else b for b in writes]
        self._wait(e, self._deps(reads, writes))
        ins = fn(self.eng[e])
        self.nins += 1
        if inc:
            self.cnt[e] += 1
            tok = (self._key(e), self.cnt[e])
            ins.then_inc(self.semh[tok[0]], 1)
            self._mark(tok, reads, writes)
            if self.cnt[e] >= SEM_LIMIT:
                self.final[tok[0]] = self.cnt[e]
                self.gen[e] += 1
                self._newsem(e)
        else:
            tok = (self._key(e), self.cnt[e] + 1)
            self._mark(tok, reads, writes)
        return tok

    def dma(self, e, out, in_, reads=(), writes=(), **kw):
        reads = [b.t if isinstance(b, Buf) else b for b in reads]
        writes = [b.t if isinstance(b, Buf) else b for b in writes]
        i = self.dnext
        self.dnext = (self.dnext + 1) % NDMA
        key = f"dq{i}"
        deps = self._deps(reads, writes)
        if self.dcnt[i] > 0:
            deps.append((key, 16 * self.dcnt[i]))
        self._wait(e, deps)
        ins = self.eng[e].dma_start(out=out, in_=in_, **kw)
        self.nins += 1
        self.dcnt[i] += 1
        tok = (key, 16 * self.dcnt[i])
        ins.then_inc(self.dsem[i], 16)
        self._mark(tok, reads, writes)
        return tok

    def coll(self, src, dst, reads=(), writes=()):
        reads = [b.t if isinstance(b, Buf) else b for b in reads]
        writes = [b.t if isinstance(b, Buf) else b for b in writes]
        key = f"cc{self.ncc}"
        self.ncc += 1
        sem = self.es.enter_context(self.nc.semaphore("s_" + key))
        self.semh[key] = sem
        self._wait("pool", self._deps(reads, writes))
        ins = self.nc.gpsimd.collective_compute("AllGather", ALU.bypass, replica_groups=[[0, 1], [2, 3], [4, 5], [6, 7]],
                                                ins=[src], outs=[dst])
        ins.then_inc(sem, 1)
        self.nins += 1
        tok = (key, 1)
        self.final[key] = 1
        self._mark(tok, reads, writes)
        return tok

    def all_tokens(self):
        toks = [(k, v) for k, v in self.final.items()]
        toks += [(self._key(e), self.cnt[e]) for e in self.cnt if self.cnt[e] > 0]
        toks += [(f"dq{i}", 16 * self.dcnt[i]) for i in range(NDMA) if self.dcnt[i] > 0]
        return toks

    def barrier(self):
        toks = self.all_tokens()
        for e in ("pe", "act", "dve", "pool", "sp"):
            self._wait(e, toks)


class WStream:
    def __init__(self, k, bufs, srcs):
        self.k, self.bufs, self.srcs = k, bufs, srcs
        self.issued = 0

    def _issue(self):
        i = self.issued
        b = self.bufs[i % len(self.bufs)]
        self.k.dma("pool", b.a[:].rearrange("p k j -> p (k j)"), self.srcs[i], writes=[b])
        self.issued += 1

    def get(self, i):
        while self.issued <= min(i + len(self.bufs) - 1, len(self.srcs) - 1):
            self._issue()
        return self.bufs[i % len(self.bufs)]


def split(total, maxn=512):
    n = -(-total // maxn)
    base = -(-total // n)
    out = []
    s = 0
    while s < total:
        w = min(base, total - s)
        out.append((s, w))
        s += w
    return out


def _fm(v):
    v = np.asarray(v, np.float32).reshape(-1, 128)
    return np.ascontiguousarray(v.T)


def t5_bucket_np(n):
    n = np.asarray(n)
    n_f = np.maximum(n, 1).astype(np.float32)
    large = 16 + (np.log(n_f / np.float32(16)) / np.float32(math.log(128 / 16)) * np.float32(16)).astype(np.int32)
    large = np.minimum(large, 31)
    return np.where(n < 16, n, large)


def _t5_bucket_jaxlike():
    n = np.arange(0, LG, dtype=np.int32)
    return t5_bucket_np(n)


class VecTable:
    def __init__(self):
        self.cols = []
        self.off = {}
        self.n = 0

    def add(self, name, arr):
        arr = np.asarray(arr, np.float32)
        assert arr.shape[0] == 128
        self.off[name] = self.n
        self.cols.append(arr)
        self.n += arr.shape[1]

    def build(self):
        return np.ascontiguousarray(np.concatenate(self.cols, axis=1))


def vec_layout(cfg):
    vt = {}
    n = 0

    def add(name, w):
        nonlocal n
        vt[name] = n
        n += w
    KD, KF, L, NA = cfg.KD, cfg.KF, cfg.L, cfg.NA
    add("cT", KD)
    add("modb", L * 6 * KD)
    for l in range(L):
        for i in range(4):
            add(f"ng{l}{i}", KD)
    for l in range(NA):
        add(f"b1a{l}", KD)
        add(f"b1g{l}", KD)
        add(f"dw{l}", KD * CONVW)
        add(f"dwb{l}", KD)
        add(f"lng{l}", KD)
        add(f"lnb{l}", KD)
        add(f"b2{l}", KD)
    add("kvg", KD)
    for j in range(L - NA):
        add(f"lam{j}", 256)
        add(f"subg{j}", 1)
    for l in range(L):
        add(f"fdw{l}", 3 * 2 * KF)
        add(f"fdwb{l}", 2 * KF)
    add("hm", 1)
    add("hmc", 1)
    return vt, n


def prepare_core(cfg, inp, c):
    D, S, F, L, NA, KD, KF, NH, OH, To, HL = cfg.D, cfg.S, cfg.F, cfg.L, cfg.NA, cfg.KD, cfg.KF, cfg.NH, cfg.OH, cfg.To, cfg.HL
    b, r = c // 2, c % 2
    f32 = np.float32
    m = {}
    x = inp["x"][b]
    x0 = np.zeros((D, HL + To), f32)
    lo = r * To - HL
    src_lo = max(lo, 0)
    x0[:, src_lo - lo:] = x[src_lo:r * To + To].T
    m["x0"] = x0
    vt = VecTable()
    vt.add("cT", _fm(inp["c"][b]))
    vt.add("modb", np.concatenate([_fm(inp["mod_b"][l]) for l in range(L)], 1))
    for l in range(L):
        for i in range(4):
            vt.add(f"ng{l}{i}", _fm(inp["norm_g"][l, i]))
    for l in range(NA):
        vt.add(f"b1a{l}", _fm(inp["cm_b1"][l][:D]))
        vt.add(f"b1g{l}", _fm(inp["cm_b1"][l][D:]))
        dw = inp["cm_dw"][l]
        vt.add(f"dw{l}", np.ascontiguousarray(dw.reshape(CONVW, KD, 128).transpose(2, 1, 0)).reshape(128, KD * CONVW))
        vt.add(f"dwb{l}", _fm(inp["cm_dwb"][l]))
        vt.add(f"lng{l}", _fm(inp["cm_ln_g"][l]))
        vt.add(f"lnb{l}", _fm(inp["cm_ln_b"][l]))
        vt.add(f"b2{l}", _fm(inp["cm_b2"][l]))
    vt.add("kvg", _fm(inp["kv_norm_g"]))
    for j in range(L - NA):
        vt.add(f"lam{j}", np.tile(np.asarray(inp["lam"][j], f32).reshape(1, 256), (128, 1)))
        vt.add(f"subg{j}", np.asarray(inp["subln_g"][j], f32).reshape(128, 1))
    for l in range(L):
        fdw = inp["ffn_dw"][l]
        vt.add(f"fdw{l}", np.ascontiguousarray(fdw.reshape(3, 2 * KF, 128).transpose(2, 0, 1)).reshape(128, 3 * 2 * KF))
        vt.add(f"fdwb{l}", _fm(inp["ffn_dwb"][l]))
    vt.add("hm", np.full((128, 1), float(r), f32))
    vt.add("hmc", np.full((128, 1), float(1 - r), f32))
    lay, n = vec_layout(cfg)
    assert lay == vt.off and n == vt.n
    m["vecs"] = vt.build()
    w1 = np.stack([inp["cm_w1"][l].reshape(KD, 128, 2, KD, 128).transpose(3, 1, 0, 2, 4).reshape(KD, 128, KD * 256)
                   for l in range(NA)])
    m["w1u"] = np.ascontiguousarray(w1, f32)
    w2 = np.stack([inp["cm_w2"][l].reshape(KD, 128, KD, 128).transpose(2, 1, 0, 3).reshape(KD, 128, KD * 128)
                   for l in range(NA)])
    m["w2u"] = np.ascontiguousarray(w2, f32)
    wi = np.stack([inp["ffn_w_in"][l].reshape(KD, 128, 2, KF, 128).transpose(3, 1, 0, 2, 4).reshape(KF, 128, KD * 256)
                   for l in range(L)])
    m["winu"] = np.ascontiguousarray(wi, f32)
    wo = np.stack([inp["ffn_w_out"][l].reshape(KF, 128, KD, 128).transpose(2, 1, 0, 3).reshape(KD, 128, KF * 128)
                   for l in range(L)])
    m["woutu"] = np.ascontiguousarray(wo, f32)
    hs = slice(r * OH * 128, (r + 1) * OH * 128)

    def own(w):
        return np.ascontiguousarray(w.reshape(KD, 128, NH * 128)[:, :, hs].transpose(1, 0, 2).reshape(128, KD * OH * 128), f32)
    m["wk"] = own(inp["w_k"])
    m["wv"] = own(inp["w_v"])
    m["wq"] = np.stack([own(inp["w_q"][j]) for j in range(L - NA)])
    wou = np.stack([inp["w_o"][j].reshape(NH, 128, KD, 128).transpose(2, 1, 0, 3).reshape(KD, 128, NH * 128)
                    for j in range(L - NA)])
    m["wou"] = np.ascontiguousarray(wou, f32)
    m["modw"] = np.ascontiguousarray(inp["mod_w"][:, :, r * 3 * D:(r + 1) * 3 * D], f32)
    rbx = np.ones((33, OH), f32)
    rbx[:32] = inp["rel_bias"][:, r * OH:(r + 1) * OH]
    m["rbx"] = rbx
    m["ident"] = np.eye(128, dtype=f32)
    m["antiid"] = np.ascontiguousarray(np.eye(128, dtype=f32)[::-1])
    ohc = np.zeros((33, LG), f32)
    nn = np.arange(LG) - 511
    bk = t5_bucket_np(np.maximum(nn, 0))
    pos = nn >= 0
    ohc[bk[pos], np.nonzero(pos)[0]] = 1.0
    ohc[31, pos] -= 1.0
    ohc[32, ~pos] = NEG
    m["ohc"] = ohc
    return m


def build(cfg, stop_after=None):
    D, S, F, L, NA, KD, KF, NH, OH, To, HL = cfg.D, cfg.S, cfg.F, cfg.L, cfg.NA, cfg.KD, cfg.KF, cfg.NH, cfg.OH, cfg.To, cfg.HL
    RC, NG = cfg.RC, cfg.NG
    SQD = math.sqrt(D)
    nc = bass.Bass("TRN2", target_bir_lowering=False)
    VO, NV = vec_layout(cfg)

    def din(name, shape, dt=F32):
        return nc.dram_tensor(name, list(shape), dt, kind="ExternalInput").ap()

    def dint(name, shape, dt):
        return nc.dram_tensor(name, list(shape), dt).ap()

    x0 = din("x0", [D, HL + To])
    vecs_d = din("vecs", [128, NV])
    w1u = din("w1u", [NA, KD, 128, KD * 256])
    w2u = din("w2u", [NA, KD, 128, KD * 128])
    winu = din("winu", [L, KF, 128, KD * 256])
    woutu = din("woutu", [L, KD, 128, KF * 128])
    wk_d = din("wk", [128, KD * OH * 128])
    wv_d = din("wv", [128, KD * OH * 128])
    wq_d = din("wq", [L - NA, 128, KD * OH * 128])
    wou = din("wou", [L - NA, KD, 128, NH * 128])
    modw = din("modw", [L, D, 3 * D])
    rbx_d = din("rbx", [33, OH])
    ident_d = din("ident", [128, 128])
    antiid_d = din("antiid", [128, 128])
    ohc_d = din("ohc", [33, LG])
    yout = nc.dram_tensor("yout", [D, To], F32, kind="ExternalOutput").ap()

    XA = dint("XA", [D, HL + To], F32)
    XB = dint("XB", [D, HL + To], F32)
    tXA, tXB, tX0, tY = T(), T(), T(), T()
    modsrc = dint("modsrc", [L, 3 * D], F32)
    moddst = dint("moddst", [2 * L, 3 * D], F32)
    gvec = dint("gvec", [OH, LG + 1], F32)
    xnsrc = [dint(f"xnsrc{i}", [RC, To], BF16) for i in range(NG)]
    xndst = [dint(f"xndst{i}", [2 * RC, To], BF16) for i in range(NG)]
    t_xnsrc = [T() for _ in range(NG)]
    t_xndst = [T() for _ in range(NG)]
    Kt = dint("Kt", [OH, 128, S], BF16)
    Qt = dint("Qt", [OH, 128, S], BF16)
    Vd = dint("Vd", [OH, 128, S // 128, 128], BF16)
    tKt, tQt, tVd = T(), T(), T()
    Osrc = [dint(f"Osrc{h}", [128, S], BF16) for h in range(OH)]
    Odst = [dint(f"Odst{h}", [256, S], BF16) for h in range(OH)]
    t_osrc = [T() for _ in range(OH)]
    t_odst = [T() for _ in range(OH)]

    with ExitStack() as es:
        k = K(nc, es)

        uid = [0]

        def sb(st, name, shape, dt):
            uid[0] += 1
            return Buf(st.enter_context(nc.sbuf_tensor(f"sb{uid[0]}_{name}", list(shape), dt)))

        def pst(st, name, shape):
            uid[0] += 1
            return Buf(st.enter_context(nc.psum_tensor(f"ps{uid[0]}_{name}", list(shape), F32)))

        block = es.enter_context(nc.Block())

        V = sb(es, "V", [128, NV], F32)
        SC = sb(es, "SC", [128, L * 6 * KD + KD + 8], F32)
        modT = sb(es, "modT", [128, L * 6 * KD], F32)
        ident_bf = sb(es, "ident_bf", [128, 128], BF16)
        ones_bf = sb(es, "ones_bf", [128, 128], BF16)
        antif = sb(es, "antif", [128, 128], F32)

        def vcol(name, i=0, w=1):
            o = VO[name] + i
            return V.a[:, o:o + w]

        def sc_off(l, which):
            return (l * 4 + which) * KD
        KVG_O = L * 4 * KD
        NLAM_O = KVG_O + KD
        SG_O = NLAM_O + 2

        def scv(off, i=0, w=1):
            return SC.a[:, off + i:off + i + w]

        def modcol(l, which, kk):
            o = l * 6 * KD + which * KD + kk
            return modT.a[:, o:o + 1]

        @block.sync
        def _(sync):
            k.dma("sp", V.a[:], vecs_d, writes=[V])
            with ExitStack() as ps_:
                identf = sb(ps_, "identf", [128, 128], F32)
                cact = sb(ps_, "cact", [128, KD], F32)
                wt = [sb(ps_, f"mwt{i}", [128, KD, 512], F32) for i in range(2)]
                rowbuf = sb(ps_, "rowbuf", [1, L * 3 * D], F32)
                rbx = sb(ps_, "rbx", [33, OH], F32)
                ohc = sb(ps_, "ohc", [33, LG], F32)
                grow = sb(ps_, "grow", [OH, LG + 1], F32)
                lt = sb(ps_, "lt", [128, 128], F32)
                ls = sb(ps_, "ls", [128, 4], F32)
                pp = [pst(ps_, f"pp{i}", [128, 512]) for i in range(2)]
                k.dma("sp", identf.a[:], ident_d, writes=[identf])
                k.dma("sp", antif.a[:], antiid_d, writes=[antif])
                k.op("act", lambda e: e.activation(out=ident_bf.a[:], in_=identf.a[:], func=AF.Copy), [identf], [ident_bf])
                k.op("dve", lambda e: e.memset(ones_bf.a[:], 1.0), [], [ones_bf])
                k.op("act", lambda e: e.activation(out=cact.a[:], in_=vcol("cT", 0, KD), func=AF.Silu), [V], [cact])
                it = 0
                for l in range(L):
                    for (n0, n) in split(3 * D, 512):
                        w_ = wt[it % 2]
                        p_ = pp[it % 2]
                        it += 1
                        k.dma("sp", w_.a[:, :, 0:n], modw[l].rearrange("(k p) n -> p k n", p=128)[:, :, n0:n0 + n], writes=[w_])
                        for kk in range(KD):
                            k.op("pe", lambda e: e.matmul(p_.a[0:1, 0:n], lhsT=cact.a[:, kk:kk + 1], rhs=w_.a[:, kk, 0:n],
                                                          start=(kk == 0), stop=(kk == KD - 1)),
                                 [cact, w_], [p_], inc=(kk == KD - 1))
                        o = l * 3 * D + n0
                        k.op("act", lambda e: e.activation(out=rowbuf.a[0:1, o:o + n], in_=p_.a[0:1, 0:n], func=AF.Copy),
                             [p_], [rowbuf])
                tms, tmd = T(), T()
                k.dma("sp", modsrc.rearrange("(o l) n -> o (l n)", o=1), rowbuf.a[0:1, :], reads=[rowbuf], writes=[tms])
                k.coll(modsrc, moddst, reads=[tms], writes=[tmd])
                for l in range(L):
                    for rr in range(2):
                        o = l * 6 * KD + rr * 3 * KD
                        k.dma("sp", modT.a[:, o:o + 3 * KD], moddst[rr * L + l].rearrange("(j p) -> p j", p=128),
                              reads=[tmd], writes=[modT], allow_slow_non_contiguous=True)
                k.op("dve", lambda e: e.tensor_tensor(out=modT.a[:], in0=modT.a[:], in1=vcol("modb", 0, L * 6 * KD), op=ALU.add),
                     [modT, V], [modT])
                for l in range(L):
                    for (which, scw, gw, ngA, ngG) in ((0, 1, 2, 0, 1), (2, 4, 5, 2, 3)):
                        oA = sc_off(l, which)
                        oG = sc_off(l, which + 1)
                        mo = l * 6 * KD
                        k.op("dve", lambda e: e.tensor_scalar(out=SC.a[:, oA:oA + KD], in0=modT.a[:, mo + scw * KD:mo + (scw + 1) * KD],
                                                              scalar1=1.0, scalar2=SQD, op0=ALU.add, op1=ALU.mult), [modT], [SC])
                        k.op("dve", lambda e: e.tensor_tensor(out=SC.a[:, oA:oA + KD], in0=SC.a[:, oA:oA + KD],
                                                              in1=vcol(f"ng{l}{ngA}", 0, KD), op=ALU.mult), [SC, V], [SC])
                        k.op("dve", lambda e: e.tensor_scalar(out=SC.a[:, oG:oG + KD], in0=modT.a[:, mo + gw * KD:mo + (gw + 1) * KD],
                                                              scalar1=SQD, scalar2=None, op0=ALU.mult), [modT], [SC])
                        k.op("dve", lambda e: e.tensor_tensor(out=SC.a[:, oG:oG + KD], in0=SC.a[:, oG:oG + KD],
                                                              in1=vcol(f"ng{l}{ngG}", 0, KD), op=ALU.mult), [SC, V], [SC])
                k.op("dve", lambda e: e.tensor_scalar(out=SC.a[:, KVG_O:KVG_O + KD], in0=vcol("kvg", 0, KD), scalar1=SQD, scalar2=None,
                                                      op0=ALU.mult), [V], [SC])
                for j in range(L - NA):
                    l = NA + j
                    lam_init = 0.8 - 0.6 * math.exp(-0.3 * l)
                    lo = VO[f"lam{j}"]
                    k.op("dve", lambda e: e.tensor_tensor(out=lt.a[:, 0:64], in0=V.a[:, lo:lo + 64], in1=V.a[:, lo + 64:lo + 128], op=ALU.mult),
                         [V], [lt])
                    k.op("dve", lambda e: e.tensor_tensor(out=lt.a[:, 64:128], in0=V.a[:, lo + 128:lo + 192], in1=V.a[:, lo + 192:lo + 256],
                                                          op=ALU.mult), [V, lt], [lt])
                    k.op("dve", lambda e: e.tensor_reduce(out=ls.a[:, 0:1], in_=lt.a[:, 0:64], axis=AX.X, op=ALU.add), [lt], [ls])
                    k.op("dve", lambda e: e.tensor_reduce(out=ls.a[:, 1:2], in_=lt.a[:, 64:128], axis=AX.X, op=ALU.add), [lt, ls], [ls])
                    k.op("act", lambda e: e.activation(out=ls.a[:, 2:4], in_=ls.a[:, 0:2], func=AF.Exp), [ls], [ls])
                    k.op("dve", lambda e: e.tensor_tensor(out=ls.a[:, 0:1], in0=ls.a[:, 3:4], in1=ls.a[:, 2:3], op=ALU.subtract), [ls], [ls])
                    k.op("dve", lambda e: e.tensor_scalar(out=SC.a[:, NLAM_O + j:NLAM_O + j + 1], in0=ls.a[:, 0:1], scalar1=-lam_init,
                                                          scalar2=None, op0=ALU.add), [ls], [SC])
                    k.op("dve", lambda e: e.tensor_scalar(out=SC.a[:, SG_O + j:SG_O + j + 1], in0=vcol(f"subg{j}"),
                                                          scalar1=(1.0 - lam_init) * math.sqrt(128.0), scalar2=None, op0=ALU.mult), [V], [SC])
                k.dma("sp", rbx.a[:], rbx_d, writes=[rbx])
                k.dma("sp", ohc.a[:], ohc_d, writes=[ohc])
                for (n0, n) in split(LG, 512):
                    p_ = pp[it % 2]
                    it += 1
                    k.op("pe", lambda e: e.matmul(p_.a[0:OH, 0:n], lhsT=rbx.a[:, :], rhs=ohc.a[:, n0:n0 + n], start=True, stop=True),
                         [rbx, ohc], [p_])
                    k.op("act", lambda e: e.activation(out=grow.a[:, n0:n0 + n], in_=p_.a[0:OH, 0:n], func=AF.Copy), [p_], [grow])
                tgv = T()
                k.dma("sp", gvec[:, 0:LG], grow.a[:, 0:LG], reads=[grow], writes=[tgv])
                k.barrier()

            def rms_rstd(st_sq, W, rstd, psb, extra_eps=D * EPS):
                for i, (n0, n) in enumerate(split(W, 512)):
                    p_ = psb[i % len(psb)]
                    for kk in range(KD):
                        k.op("pe", lambda e: e.matmul(p_.a[:, 0:n], lhsT=ones_bf.a[:], rhs=st_sq.a[:, kk, n0:n0 + n],
                                                      start=(kk == 0), stop=(kk == KD - 1)), [ones_bf, st_sq], [p_], inc=(kk == KD - 1))
                    k.op("act", lambda e: e.activation(out=rstd.a[:, n0:n0 + n], in_=p_.a[:, 0:n], func=AF.Ln, bias=extra_eps, scale=1.0),
                         [p_], [rstd])
                    k.op("act", lambda e: e.activation(out=rstd.a[:, n0:n0 + n], in_=rstd.a[:, n0:n0 + n], func=AF.Exp, scale=-0.5),
                         [rstd], [rstd])

            def xview(xd, c0, c1):
                return xd.rearrange("(k p) t -> p k t", p=128)[:, :, c0:c1]

            def tiles_for(Hs, Ts):
                nt = -(-To // Ts)
                step = -(-(To // nt) // 64) * 64
                bounds = [-Hs] + [min(To, step * i) for i in range(1, nt)] + [To]
                return [(bounds[i], bounds[i + 1]) for i in range(len(bounds) - 1)]

            def tile_w(Hs, Ts):
                return max(b - a for (a, b) in tiles_for(Hs, Ts))

            def norm_in(st, xin, txin, a, H, b, A_off, B_l, B_which, xt, sq, rstd, tmp, h, psb, mask_h):
                W = b - a + H
                k.dma("sp", xt.a[:, :, 0:W], xview(xin, HL + a - H, HL + b), reads=[txin], writes=[xt])
                k.op("act", lambda e: e.activation(out=sq.a[:, :, 0:W], in_=xt.a[:, :, 0:W], func=AF.Square), [xt], [sq])
                rms_rstd(sq, W, rstd, psb)
                for kk in range(KD):
                    t_ = tmp[kk % 2]
                    k.op("dve", lambda e: e.scalar_tensor_tensor(out=t_.a[:, 0:W], in0=xt.a[:, kk, 0:W], scalar=scv(A_off, kk),
                                                                 in1=rstd.a[:, 0:W], op0=ALU.mult, op1=ALU.mult), [xt, SC, rstd], [t_])
                    k.op("act", lambda e: e.activation(out=h.a[:, kk, 0:W], in_=t_.a[:, 0:W], func=AF.Identity,
                                                       bias=modcol(B_l, B_which, kk), scale=1.0), [t_, modT], [h])
                if mask_h > 0:
                    k.op("dve", lambda e: e.tensor_scalar(out=h.a[:, :, 0:mask_h], in0=h.a[:, :, 0:mask_h], scalar1=vcol("hm"), scalar2=None,
                                                          op0=ALU.mult), [h, V], [h])

            def resid_out(y, Wo, G_off, xin, txin, xout, txout, a, b, sq, rstd, tmp, xres, psb, emit_xn=None, is_final=False):
                k.op("act", lambda e: e.activation(out=sq.a[:, :, 0:Wo], in_=y.a[:, :, 0:Wo], func=AF.Square), [y], [sq])
                rms_rstd(sq, Wo, rstd, psb)
                for kk in range(KD):
                    xr = xres[kk % 2]
                    t_ = tmp[kk % 2]
                    k.dma("sp", xr.a[:, 0:Wo], xin[kk * 128:(kk + 1) * 128, HL + a:HL + b], reads=[txin], writes=[xr])
                    k.op("dve", lambda e: e.tensor_tensor(out=t_.a[:, 0:Wo], in0=y.a[:, kk, 0:Wo], in1=rstd.a[:, 0:Wo], op=ALU.mult),
                         [y, rstd], [t_])
                    k.op("act", lambda e: e.activation(out=t_.a[:, 0:Wo], in_=t_.a[:, 0:Wo], func=AF.Identity, scale=scv(G_off, kk)),
                         [t_, SC], [t_])
                    k.op("pool", lambda e: e.tensor_tensor(out=y.a[:, kk, 0:Wo], in0=t_.a[:, 0:Wo], in1=xr.a[:, 0:Wo], op=ALU.add),
                         [t_, xr], [y])
                if is_final:
                    o0 = max(0, -a)
                    k.dma("sp", xview(xout, a + o0, b), y.a[:, :, o0:Wo], reads=[y], writes=[txout])
                else:
                    k.dma("sp", xview(xout, HL + a, HL + b), y.a[:, :, 0:Wo], reads=[y], writes=[txout])
                if emit_xn is not None:
                    hbuf = emit_xn
                    k.op("act", lambda e: e.activation(out=sq.a[:, :, 0:Wo], in_=y.a[:, :, 0:Wo], func=AF.Square), [y], [sq])
                    rms_rstd(sq, Wo, rstd, psb)
                    o0 = max(0, -a)
                    for kk in range(KD):
                        k.op("dve", lambda e: e.tensor_tensor(out=hbuf.a[:, kk, 0:Wo], in0=y.a[:, kk, 0:Wo], in1=rstd.a[:, 0:Wo], op=ALU.mult),
                             [y, rstd], [hbuf])
                    per = RC // 128
                    for i in range(NG):
                        k.dma("sp", xnsrc[i].rearrange("(k p) t -> p k t", p=128)[:, :, a + o0:b],
                              hbuf.a[:, i * per:(i + 1) * per, o0:Wo], reads=[hbuf], writes=[t_xnsrc[i]])

            def exchange_xn():
                for i in range(NG):
                    k.coll(xnsrc[i], xndst[i], reads=[t_xnsrc[i]], writes=[t_xndst[i]])

            def mixer_sublayer(l, Hs, xin, txin, xout, txout):
                H = CONVW - 1
                Ts = min(512, To)
                WO = tile_w(Hs, Ts)
                WM = WO + H
                with ExitStack() as st:
                    xt = sb(st, "m_xt", [128, KD, WM], F32)
                    ybuf = sb(st, "m_y", [128, KD, WO], F32)
                    sq = sb(st, "m_sq", [128, KD, WM], BF16)
                    rstd = sb(st, "m_rstd", [128, WM], F32)
                    r2 = sb(st, "m_r2", [128, 3, 512], F32)
                    tmp = [sb(st, f"m_tmp{i}", [128, WM], F32) for i in range(2)]
                    xres = [sb(st, f"m_xr{i}", [128, WO], F32) for i in range(2)]
                    h = sb(st, "m_h", [128, KD, WM], BF16)
                    u = sb(st, "m_u", [128, KD, WM], BF16)
                    v32 = sb(st, "m_v", [128, KD, WO], F32)
                    vb = sb(st, "m_vb", [128, KD, WO], BF16)
                    z = sb(st, "m_z", [128, KD, WO], BF16)
                    sig = [sb(st, f"m_sig{i}", [128, 512], F32) for i in range(2)]
                    w1b = [sb(st, f"m_w1{i}", [128, KD, 256], BF16) for i in range(2)]
                    w2b = [sb(st, f"m_w2{i}", [128, KD, 128], BF16) for i in range(2)]
                    dg = [sb(st, f"m_dg{i}", [128, CONVW, 128], BF16) for i in range(2)]
                    psA = [pst(st, f"m_psA{i}", [128, 512]) for i in range(2)]
                    psG = [pst(st, f"m_psG{i}", [128, 512]) for i in range(2)]
                    psC = [pst(st, f"m_psC{i}", [128, 512]) for i in range(2)]
                    psS = [pst(st, f"m_psS{i}", [128, 512]) for i in range(2)]
                    wi = 0
                    mtiles = tiles_for(Hs, Ts)
                    w1s = WStream(k, w1b, [w1u[l, oc] for _ in mtiles for oc in range(KD)])
                    w2s = WStream(k, w2b, [w2u[l, oc] for _ in mtiles for oc in range(KD)])
                    w1s.get(0)
                    w2s.get(0)
                    def nin(ti):
                        a, b = mtiles[ti]
                        norm_in(st, xin, txin, a, H, b, sc_off(l, 0), l, 0, xt, sq, rstd, tmp, h, psS, 0)
                    nin(0)
                    for ti, (a, b) in enumerate(mtiles):
                        W = b - a + H
                        Wo = b - a
                        it = 0
                        for oc in range(KD):
                            wb_ = w1s.get(ti * KD + oc)
                            for (n0, n) in split(W, 512):
                                pa, pg, sg_ = psA[it % 2], psG[it % 2], sig[it % 2]
                                it += 1
                                for kk in range(KD):
                                    k.op("pe", lambda e: e.matmul(pa.a[:, 0:n], lhsT=wb_.a[:, kk, 0:128], rhs=h.a[:, kk, n0:n0 + n],
                                                                  start=(kk == 0), stop=(kk == KD - 1)), [wb_, h], [pa], inc=(kk == KD - 1))
                                for kk in range(KD):
                                    k.op("pe", lambda e: e.matmul(pg.a[:, 0:n], lhsT=wb_.a[:, kk, 128:256], rhs=h.a[:, kk, n0:n0 + n],
                                                                  start=(kk == 0), stop=(kk == KD - 1)), [wb_, h], [pg], inc=(kk == KD - 1))
                                k.op("act", lambda e: e.activation(out=sg_.a[:, 0:n], in_=pg.a[:, 0:n], func=AF.Sigmoid,
                                                                   bias=vcol(f"b1g{l}", oc), scale=1.0), [pg, V], [sg_])
                                k.op("dve", lambda e: e.scalar_tensor_tensor(out=u.a[:, oc, n0:n0 + n], in0=pa.a[:, 0:n], scalar=vcol(f"b1a{l}", oc),
                                                                             in1=sg_.a[:, 0:n], op0=ALU.add, op1=ALU.mult), [pa, sg_, V], [u])
                        if ti == 0:
                            nn = Hs + H
                            k.op("dve", lambda e: e.tensor_scalar(out=u.a[:, :, 0:nn], in0=u.a[:, :, 0:nn], scalar1=vcol("hm"), scalar2=None,
                                                                  op0=ALU.mult), [u, V], [u])
                        if ti + 1 < len(mtiles):
                            nin(ti + 1)
                        it = 0
                        for c in range(KD):
                            d_ = dg[c % 2]
                            do = VO[f"dw{l}"] + c * CONVW
                            k.op("pool", lambda e: e.tensor_tensor(out=d_.a[:], in0=ident_bf.a[:].unsqueeze(1).broadcast_to([128, CONVW, 128]),
                                                                   in1=V.a[:, do:do + CONVW].unsqueeze(2).broadcast_to([128, CONVW, 128]),
                                                                   op=ALU.mult), [ident_bf, V], [d_])
                            for (n0, n) in split(Wo, 512):
                                pc = psC[it % 2]
                                it += 1
                                for j in range(CONVW):
                                    k.op("pe", lambda e: e.matmul(pc.a[:, 0:n], lhsT=d_.a[:, j, :], rhs=u.a[:, c, n0 + j:n0 + j + n],
                                                                  start=(j == 0), stop=(j == CONVW - 1)), [d_, u], [pc], inc=(j == CONVW - 1))
                                k.op("act", lambda e: e.activation(out=v32.a[:, c, n0:n0 + n], in_=pc.a[:, 0:n], func=AF.Identity,
                                                                   bias=vcol(f"dwb{l}", c), scale=1.0), [pc, V], [v32])
                        k.op("pool", lambda e: e.tensor_copy(out=vb.a[:, :, 0:Wo], in_=v32.a[:, :, 0:Wo]), [v32], [vb])
                        k.op("act", lambda e: e.activation(out=sq.a[:, :, 0:Wo], in_=v32.a[:, :, 0:Wo], func=AF.Square), [v32], [sq])
                        for i, (n0, n) in enumerate(split(Wo, 512)):
                            pm, pq = psS[0], psS[1]
                            for kk in range(KD):
                                k.op("pe", lambda e: e.matmul(pm.a[:, 0:n], lhsT=ones_bf.a[:], rhs=vb.a[:, kk, n0:n0 + n],
                                                              start=(kk == 0), stop=(kk == KD - 1)), [ones_bf, vb], [pm], inc=(kk == KD - 1))
                            for kk in range(KD):
                                k.op("pe", lambda e: e.matmul(pq.a[:, 0:n], lhsT=ones_bf.a[:], rhs=sq.a[:, kk, n0:n0 + n],
                                                              start=(kk == 0), stop=(kk == KD - 1)), [ones_bf, sq], [pq], inc=(kk == KD - 1))
                            k.op("dve", lambda e: e.tensor_scalar(out=r2.a[:, 0, 0:n], in0=pm.a[:, 0:n], scalar1=1.0 / D, scalar2=None,
                                                                  op0=ALU.mult), [pm], [r2])
                            k.op("dve", lambda e: e.tensor_tensor(out=r2.a[:, 1, 0:n], in0=r2.a[:, 0, 0:n], in1=r2.a[:, 0, 0:n], op=ALU.mult),
                                 [r2], [r2])
                            k.op("dve", lambda e: e.scalar_tensor_tensor(out=r2.a[:, 1, 0:n], in0=pq.a[:, 0:n], scalar=1.0 / D, in1=r2.a[:, 1, 0:n],
                                                                         op0=ALU.mult, op1=ALU.subtract), [pq, r2], [r2])
                            k.op("act", lambda e: e.activation(out=r2.a[:, 1, 0:n], in_=r2.a[:, 1, 0:n], func=AF.Ln, bias=EPS, scale=1.0), [r2], [r2])
                            k.op("act", lambda e: e.activation(out=r2.a[:, 1, 0:n], in_=r2.a[:, 1, 0:n], func=AF.Exp, scale=-0.5), [r2], [r2])
                            k.op("dve", lambda e: e.scalar_tensor_tensor(out=r2.a[:, 2, 0:n], in0=r2.a[:, 0, 0:n], scalar=-1.0, in1=r2.a[:, 1, 0:n],
                                                                         op0=ALU.mult, op1=ALU.mult), [r2], [r2])
                            for kk in range(KD):
                                t_ = tmp[kk % 2]
                                k.op("dve", lambda e: e.tensor_tensor(out=t_.a[:, 0:n], in0=v32.a[:, kk, n0:n0 + n], in1=r2.a[:, 1, 0:n],
                                                                      op=ALU.mult), [v32, r2], [t_])
                                k.op("pool", lambda e: e.tensor_tensor(out=t_.a[:, 0:n], in0=t_.a[:, 0:n], in1=r2.a[:, 2, 0:n], op=ALU.add),
                                     [t_, r2], [t_])
                                k.op("act", lambda e: e.activation(out=z.a[:, kk, n0:n0 + n], in_=t_.a[:, 0:n], func=AF.Silu,
                                                                   bias=vcol(f"lnb{l}", kk), scale=vcol(f"lng{l}", kk)), [t_, V], [z])
                        it = 0
                        for oc in range(KD):
                            wb_ = w2s.get(ti * KD + oc)
                            for (n0, n) in split(Wo, 512):
                                pa = psA[it % 2]
                                it += 1
                                for kk in range(KD):
                                    k.op("pe", lambda e: e.matmul(pa.a[:, 0:n], lhsT=wb_.a[:, kk, :], rhs=z.a[:, kk, n0:n0 + n],
                                                                  start=(kk == 0), stop=(kk == KD - 1)), [wb_, z], [pa], inc=(kk == KD - 1))
                                k.op("act", lambda e: e.activation(out=ybuf.a[:, oc, n0:n0 + n], in_=pa.a[:, 0:n], func=AF.Identity,
                                                                   bias=vcol(f"b2{l}", oc), scale=1.0), [pa, V], [ybuf])
                        resid_out(ybuf, Wo, sc_off(l, 1), xin, txin, xout, txout, a, b, sq, rstd, tmp, xres, psS)
                    k.barrier()

            def ffn_part1(l, a, b, h, hid, bufs):
                (cg, cv, sl, wins, wouts, ti_, psU, psV, psY) = bufs
                Wo = b - a
                fo = VO[f"fdw{l}"]
                bo = VO[f"fdwb{l}"]
                ntl = split(Wo, 510)
                it = 0
                for i in range(KF):
                    wb_ = wins.get(ti_ * KF + i)
                    for (o0, no) in ntl:
                        n = no + 2
                        s_ = it % 2
                        pu, pv = psU[it % len(psU)], psV[it % len(psV)]
                        it += 1
                        for kk in range(KD):
                            k.op("pe", lambda e: e.matmul(pu.a[:, 0:n], lhsT=wb_.a[:, kk, 0:128], rhs=h.a[:, kk, o0:o0 + n],
                                                          start=(kk == 0), stop=(kk == KD - 1)), [wb_, h], [pu], inc=(kk == KD - 1))
                        for kk in range(KD):
                            k.op("pe", lambda e: e.matmul(pv.a[:, 0:n], lhsT=wb_.a[:, kk, 128:256], rhs=h.a[:, kk, o0:o0 + n],
                                                          start=(kk == 0), stop=(kk == KD - 1)), [wb_, h], [pv], inc=(kk == KD - 1))
                        for (cc, ch, pp_) in ((cg[s_], i, pu), (cv[s_], KF + i, pv)):
                            k.op("act", lambda e: e.activation(out=cc.a[:, 0:no], in_=pp_.a[:, 2:2 + no], func=AF.Identity,
                                                               scale=V.a[:, fo + 2 * 2 * KF + ch:fo + 2 * 2 * KF + ch + 1],
                                                               bias=V.a[:, bo + ch:bo + ch + 1]), [pp_, V], [cc])
                            k.op("dve", lambda e: e.scalar_tensor_tensor(out=cc.a[:, 0:no], in0=pp_.a[:, 1:1 + no], scalar=V.a[:, fo + 2 * KF + ch:fo + 2 * KF + ch + 1],
                                                                         in1=cc.a[:, 0:no], op0=ALU.mult, op1=ALU.add), [pp_, V, cc], [cc])
                            k.op("dve", lambda e: e.scalar_tensor_tensor(out=cc.a[:, 0:no], in0=pp_.a[:, 0:no], scalar=V.a[:, fo + ch:fo + ch + 1],
                                                                         in1=cc.a[:, 0:no], op0=ALU.mult, op1=ALU.add), [pp_, V, cc], [cc])
                        k.op("act", lambda e: e.activation(out=sl[s_].a[:, 0:no], in_=cg[s_].a[:, 0:no], func=AF.Silu), [cg[s_]], [sl[s_]])
                        k.op("pool", lambda e: e.tensor_tensor(out=hid.a[:, i, o0:o0 + no], in0=sl[s_].a[:, 0:no], in1=cv[s_].a[:, 0:no], op=ALU.mult),
                             [sl[s_], cv[s_]], [hid])

            def ffn_part2(l, a, b, hid, y, bufs):
                (cg, cv, sl, wins, wouts, ti_, psU, psV, psY) = bufs
                Wo = b - a
                it = 0
                for oc in range(KD):
                    wb_ = wouts.get(ti_ * KD + oc)
                    for (n0, n) in split(Wo, 512):
                        py = psY[it % 2]
                        it += 1
                        for kk in range(KF):
                            k.op("pe", lambda e: e.matmul(py.a[:, 0:n], lhsT=wb_.a[:, kk, :], rhs=hid.a[:, kk, n0:n0 + n],
                                                          start=(kk == 0), stop=(kk == KF - 1)), [wb_, hid], [py], inc=(kk == KF - 1))
                        k.op("act", lambda e: e.activation(out=y.a[:, oc, n0:n0 + n], in_=py.a[:, 0:n], func=AF.Copy), [py], [y])

            def ffn_sublayer(l, Hs, xin, txin, xout, txout, emit_xn=False, is_final=False):
                H = 2
                Ts = min(704, To)
                WM = tile_w(Hs, Ts) + H
                with ExitStack() as st:
                    xt = sb(st, "f_xt", [128, KD, WM], F32)
                    y = sb(st, "f_y", [128, KD, WM], F32)
                    hid = sb(st, "f_hid", [128, KF, WM], BF16)
                    sq = sb(st, "f_sq", [128, KD, WM], BF16)
                    xnb = sb(st, "f_xnb", [128, KD, WM], BF16) if emit_xn else None
                    rstd = sb(st, "f_rstd", [128, WM], F32)
                    tmp = [sb(st, f"f_tmp{i}", [128, WM], F32) for i in range(2)]
                    xres = [sb(st, f"f_xr{i}", [128, WM], F32) for i in range(2)]
                    h = sb(st, "f_h", [128, KD, WM], BF16)
                    cg = [sb(st, f"f_cg{i}", [128, 512], F32) for i in range(2)]
                    cv = [sb(st, f"f_cv{i}", [128, 512], F32) for i in range(2)]
                    sl = [sb(st, f"f_sl{i}", [128, 512], F32) for i in range(2)]
                    winb = [sb(st, f"f_win{i}", [128, KD, 256], BF16) for i in range(2)]
                    woutb = [sb(st, f"f_wout{i}", [128, KF, 128], BF16) for i in range(2)]
                    psU = [pst(st, f"f_psU{i}", [128, 512]) for i in range(3)]
                    psV = [pst(st, f"f_psV{i}", [128, 512]) for i in range(3)]
                    psS = [pst(st, f"f_psS{i}", [128, 512]) for i in range(2)]
                    psY = [psU[0], psV[0]]
                    ftiles = tiles_for(Hs, Ts)
                    wins = WStream(k, winb, [winu[l, i] for _ in ftiles for i in range(KF)])
                    wouts = WStream(k, woutb, [woutu[l, oc] for _ in ftiles for oc in range(KD)])
                    wins.get(0)
                    wouts.get(0)

                    def nin(ti):
                        a, b = ftiles[ti]
                        norm_in(st, xin, txin, a, H, b, sc_off(l, 2), l, 3, xt, sq, rstd, tmp, h, psS, (Hs + H) if ti == 0 else 0)
                    nin(0)
                    for ti, (a, b) in enumerate(ftiles):
                        bufs = (cg, cv, sl, wins, wouts, ti, psU, psV, psY)
                        Wo = b - a
                        ffn_part1(l, a, b, h, hid, bufs)
                        if ti + 1 < len(ftiles):
                            nin(ti + 1)
                        ffn_part2(l, a, b, hid, y, bufs)
                        resid_out(y, Wo, sc_off(l, 3), xin, txin, xout, txout, a, b, sq, rstd, tmp, xres, psS,
                                  emit_xn=xnb, is_final=is_final)
                    k.barrier()

            def kvq_phase(j, do_kv):
                l = NA + j
                NT = 512
                OW = OH * 128
                with ExitStack() as st:
                    xn = [sb(st, f"p_xn{i}", [128, KD, NT], BF16) for i in range(3)]
                    wf = sb(st, "p_wf", [128, KD, OW], F32)
                    wqb = sb(st, "p_wq", [128, KD, OW], BF16)
                    wkb = sb(st, "p_wk", [128, KD, OW], BF16)
                    wvb = sb(st, "p_wv", [128, KD, OW], BF16)
                    qbias = sb(st, "p_qb", [128, OH], F32)
                    qo = [sb(st, f"p_qo{i}", [128, OH, NT], BF16) for i in range(2)]
                    ko = [sb(st, f"p_ko{i}", [128, OH, NT], BF16) for i in range(2)]
                    vo = [sb(st, f"p_vo{i}", [128, NT // 128, OW], BF16) for i in range(2)]
                    pq = [pst(st, f"p_pq{i}", [128, 512]) for i in range(2)]
                    pk = [pst(st, f"p_pk{i}", [128, 512]) for i in range(2)]
                    pv = [pst(st, f"p_pv{i}", [128, 512]) for i in range(2)]
                    k.dma("sp", wf.a[:].rearrange("p k j -> p (k j)"), wq_d[j], writes=[wf])
                    for m_ in range(OH):
                        for kk in range(KD):
                            k.op("pe", lambda e: e.matmul(pq[0].a[:, m_:m_ + 1], lhsT=wf.a[:, kk, m_ * 128:(m_ + 1) * 128], rhs=modcol(l, 0, kk),
                                                          start=(kk == 0), stop=(kk == KD - 1)), [wf, modT], [pq[0]], inc=(kk == KD - 1))
                    k.op("dve", lambda e: e.tensor_copy(out=qbias.a[:], in_=pq[0].a[:, 0:OH]), [pq[0]], [qbias])
                    for kk in range(KD):
                        k.op("dve", lambda e: e.tensor_scalar(out=wqb.a[:, kk, :], in0=wf.a[:, kk, :], scalar1=scv(sc_off(l, 0), kk), scalar2=None,
                                                              op0=ALU.mult), [wf, SC], [wqb])
                    if do_kv:
                        for (wd, wb_) in ((wk_d, wkb), (wv_d, wvb)):
                            k.dma("sp", wf.a[:].rearrange("p k j -> p (k j)"), wd, writes=[wf])
                            for kk in range(KD):
                                k.op("dve", lambda e: e.tensor_scalar(out=wb_.a[:, kk, :], in0=wf.a[:, kk, :], scalar1=scv(KVG_O, kk), scalar2=None,
                                                                      op0=ALU.mult), [wf, SC], [wb_])
                    per = RC // 128
                    itq = itk = itv = 0
                    ntile = S // NT

                    def load_xn(gi):
                        g0 = gi * NT
                        rr, t0 = g0 // To, g0 % To
                        for i in range(NG):
                            k.dma("sp", xn[gi % 3].a[:, i * per:(i + 1) * per, :],
                                  xndst[i][rr * RC:(rr + 1) * RC, t0:t0 + NT].rearrange("(k p) t -> p k t", p=128),
                                  reads=[t_xndst[i]], writes=[xn[gi % 3]])
                    load_xn(0)
                    for gi in range(ntile):
                        g0 = gi * NT
                        s_ = gi % 2
                        x_ = xn[gi % 3]
                        if gi + 1 < ntile:
                            load_xn(gi + 1)
                        for m_ in range(OH):
                            p_ = pq[itq % 2]
                            itq += 1
                            for kk in range(KD):
                                k.op("pe", lambda e: e.matmul(p_.a[:, 0:NT], lhsT=wqb.a[:, kk, m_ * 128:(m_ + 1) * 128], rhs=x_.a[:, kk, :],
                                                              start=(kk == 0), stop=(kk == KD - 1)), [wqb, x_], [p_], inc=(kk == KD - 1))
                            k.op("act", lambda e: e.activation(out=qo[s_].a[:, m_, :], in_=p_.a[:, 0:NT], func=AF.Identity, bias=qbias.a[:, m_:m_ + 1], scale=1.0),
                                 [p_, qbias], [qo[s_]])
                        k.dma("sp", Qt.rearrange("h p s -> p h s")[:, :, g0:g0 + NT], qo[s_].a[:], reads=[qo[s_]], writes=[tQt])
                        if do_kv:
                            for m_ in range(OH):
                                p_ = pk[itk % 2]
                                itk += 1
                                for kk in range(KD):
                                    k.op("pe", lambda e: e.matmul(p_.a[:, 0:NT], lhsT=wkb.a[:, kk, m_ * 128:(m_ + 1) * 128], rhs=x_.a[:, kk, :],
                                                                  start=(kk == 0), stop=(kk == KD - 1)), [wkb, x_], [p_], inc=(kk == KD - 1))
                                k.op("dve", lambda e: e.tensor_copy(out=ko[s_].a[:, m_, :], in_=p_.a[:, 0:NT]), [p_], [ko[s_]])
                            k.dma("sp", Kt.rearrange("h p s -> p h s")[:, :, g0:g0 + NT], ko[s_].a[:], reads=[ko[s_]], writes=[tKt])
                            for sbk in range(NT // 128):
                                p_ = pv[itv % 2]
                                itv += 1
                                for kk in range(KD):
                                    k.op("pe", lambda e: e.matmul(p_.a[:, 0:OW], lhsT=x_.a[:, kk, sbk * 128:(sbk + 1) * 128], rhs=wvb.a[:, kk, :],
                                                                  start=(kk == 0), stop=(kk == KD - 1)), [wvb, x_], [p_], inc=(kk == KD - 1))
                                k.op("act" if sbk % 2 else "dve",
                                     (lambda e: e.activation(out=vo[s_].a[:, sbk, :], in_=p_.a[:, 0:OW], func=AF.Copy)) if sbk % 2 else
                                     (lambda e: e.tensor_copy(out=vo[s_].a[:, sbk, :], in_=p_.a[:, 0:OW])), [p_], [vo[s_]])
                            for m_ in range(OH):
                                k.dma("sp", Vd[m_][:, g0 // 128:g0 // 128 + NT // 128, :], vo[s_].a[:, :, m_ * 128:(m_ + 1) * 128],
                                      reads=[vo[s_]], writes=[tVd])
                    k.barrier()

            def attn_phase(j):
                QT = 512
                NB = S // 128
                NQ = S // QT
                XS = 2 * QT
                with ExitStack() as st:
                    strips = sb(st, "a_strips", [128, OH, 1024], F32)
                    ktb = [sb(st, f"a_kt{i}", [128, S], BF16) for i in range(2)]
                    vbb = [sb(st, f"a_vb{i}", [128, NB, 128], BF16) for i in range(2)]
                    qb = [sb(st, f"a_q{i}", [128, QT], BF16) for i in range(3)]
                    P = [sb(st, f"a_P{i}", [128, 2, QT], BF16) for i in range(4)]
                    sbias = [sb(st, f"a_sb{i}", [128, 2, QT], F32) for i in range(2)]
                    accD = [sb(st, f"a_accD{i}", [128, XS], F32) for i in range(2)]
                    accP = [sb(st, f"a_accP{i}", [128, max(2 * QT - XS, 1)], F32) for i in range(2)]
                    ones_f = sb(st, "a_onesf", [128, 128], F32)
                    c1 = sb(st, "a_c1", [128, QT], F32)
                    c2 = sb(st, "a_c2", [128, QT], F32)
                    R1 = sb(st, "a_R1", [128, QT], F32)
                    R2 = sb(st, "a_R2", [128, QT], F32)
                    t1 = sb(st, "a_t1", [128, QT], F32)
                    t2 = sb(st, "a_t2", [128, QT], F32)
                    osq = sb(st, "a_osq", [128, QT], BF16)
                    rs = sb(st, "a_rs", [128, QT], F32)
                    ob = [sb(st, f"a_ob{i}", [128, QT], BF16) for i in range(2)]
                    psS = [pst(st, f"a_psS{i}", [128, 2 * QT]) for i in range(2)]
                    psO1 = pst(st, "a_psO1", [128, QT])
                    psO2 = pst(st, "a_psO2", [128, QT])
                    psZa = pst(st, "a_psZa", [128, QT])
                    psZb = pst(st, "a_psZb", [128, QT])
                    hank = sb(st, "a_hank", [128, 1024], F32)
                    k.op("dve", lambda e: e.memset(ones_f.a[:], 1.0), [], [ones_f])
                    for hl in range(OH):
                        k.dma("sp", hank.a[:], bass.AP(gvec.tensor, hl * (LG + 1), [[1, 128], [1, 1024]]), reads=[tgv], writes=[hank])
                        for half in range(2):
                            k.op("pe", lambda e: e.matmul(psO1.a[:], lhsT=antif.a[:], rhs=hank.a[:, half * 512:(half + 1) * 512], start=True, stop=True),
                                 [antif, hank], [psO1])
                            k.op("act", lambda e: e.activation(out=strips.a[:, hl, half * 512:(half + 1) * 512], in_=psO1.a[:], func=AF.Copy),
                                 [psO1], [strips])
                    nlam = scv(NLAM_O + j)
                    sgc = scv(SG_O + j)
                    jobs = [(hl, qi) for hl in range(OH) for qi in range(NQ)]
                    blocks = []
                    for ji, (hl, qi) in enumerate(jobs):
                        for kb in range(4 * qi + 4):
                            blocks.append((ji, hl, qi, kb))

                    def load_head(hl):
                        k.dma("sp", ktb[hl % 2].a[:], Kt[hl], reads=[tKt], writes=[ktb[hl % 2]])
                        k.dma("sp", vbb[hl % 2].a[:], Vd[hl], reads=[tVd], writes=[vbb[hl % 2]])

                    def load_q(ji):
                        hl, qi = jobs[ji]
                        k.dma("sp", qb[ji % 3].a[:], Qt[hl][:, qi * QT:(qi + 1) * QT], reads=[tQt], writes=[qb[ji % 3]])

                    def s_mm(bi):
                        ji, hl, qi, kb = blocks[bi]
                        ps_ = psS[bi % 2]
                        kt_, q_ = ktb[hl % 2], qb[ji % 3]
                        k.op("pe", lambda e: e.matmul(ps_.a[:, 0:QT], lhsT=kt_.a[0:64, kb * 128:(kb + 1) * 128], rhs=q_.a[0:64, :],
                                                      start=True, stop=True), [kt_, q_], [ps_], inc=False)
                        k.op("pe", lambda e: e.matmul(ps_.a[:, QT:2 * QT], lhsT=kt_.a[64:128, kb * 128:(kb + 1) * 128], rhs=q_.a[64:128, :],
                                                      start=True, stop=True), [kt_, q_], [ps_])

                    pending = []

                    def epilogue(ji, hl, qi, bi):
                        aD, aP = accD[ji % 2], accP[ji % 2]
                        o_ = ob[ji % 2]
                        k.op("dve", lambda e: e.tensor_copy(out=c1.a[:], in_=psO1.a[:]), [psO1], [c1])
                        k.op("act", lambda e: e.activation(out=c2.a[:], in_=psO2.a[:], func=AF.Copy), [psO2], [c2])

                        def st1():
                            k.op("pe", lambda e: e.matmul(psZa.a[:], lhsT=ones_f.a[:], rhs=aD.a[:, 0:QT], start=True, stop=True), [ones_f, aD], [psZa])
                            if XS < 2 * QT:
                                k.op("pe", lambda e: e.matmul(psZb.a[:, 0:XS - QT], lhsT=ones_f.a[:], rhs=aD.a[:, QT:XS], start=True, stop=True),
                                     [ones_f, aD], [psZb], inc=False)
                                k.op("pe", lambda e: e.matmul(psZb.a[:, XS - QT:QT], lhsT=ones_f.a[:], rhs=aP.a[:, :], start=True, stop=True),
                                     [ones_f, aP], [psZb])
                            else:
                                k.op("pe", lambda e: e.matmul(psZb.a[:], lhsT=ones_f.a[:], rhs=aD.a[:, QT:2 * QT], start=True, stop=True),
                                     [ones_f, aD], [psZb])

                        def st2():
                            k.op("act", lambda e: e.activation(out=R1.a[:], in_=psZa.a[:], func=AF.Ln), [psZa], [R1])
                            k.op("act", lambda e: e.activation(out=R2.a[:], in_=psZb.a[:], func=AF.Ln), [psZb], [R2])
                            k.op("act", lambda e: e.activation(out=R1.a[:], in_=R1.a[:], func=AF.Exp, scale=-1.0), [R1], [R1])
                            k.op("act", lambda e: e.activation(out=R2.a[:], in_=R2.a[:], func=AF.Exp, scale=-1.0), [R2], [R2])
                            k.op("dve", lambda e: e.tensor_tensor(out=t1.a[:], in0=c1.a[:], in1=R1.a[:], op=ALU.mult), [c1, R1], [t1])
                            k.op("dve", lambda e: e.tensor_tensor(out=t2.a[:], in0=c2.a[:], in1=R2.a[:], op=ALU.mult), [c2, R2], [t2])
                            k.op("dve", lambda e: e.scalar_tensor_tensor(out=t1.a[:], in0=t2.a[:], scalar=nlam, in1=t1.a[:], op0=ALU.mult, op1=ALU.add),
                                 [t2, t1, SC], [t1])
                            k.op("act", lambda e: e.activation(out=osq.a[:], in_=t1.a[:], func=AF.Square), [t1], [osq])

                        def st3():
                            k.op("pe", lambda e: e.matmul(psZa.a[:], lhsT=ones_bf.a[:], rhs=osq.a[:], start=True, stop=True), [ones_bf, osq], [psZa])

                        def st4():
                            k.op("act", lambda e: e.activation(out=rs.a[:], in_=psZa.a[:], func=AF.Ln, bias=128.0 * EPS, scale=1.0), [psZa], [rs])
                            k.op("act", lambda e: e.activation(out=rs.a[:], in_=rs.a[:], func=AF.Exp, scale=-0.5), [rs], [rs])
                            k.op("dve", lambda e: e.tensor_tensor(out=t2.a[:], in0=t1.a[:], in1=rs.a[:], op=ALU.mult), [t1, rs], [t2])
                            k.op("act", lambda e: e.activation(out=o_.a[:], in_=t2.a[:], func=AF.Identity, scale=sgc), [t2, SC], [o_])
                            k.dma("sp", Osrc[hl][:, qi * QT:(qi + 1) * QT], o_.a[:], reads=[o_], writes=[t_osrc[hl]])
                            if qi == NQ - 1:
                                k.coll(Osrc[hl], Odst[hl], reads=[t_osrc[hl]], writes=[t_odst[hl]])
                        for d, f in enumerate((st1, st2, st3, st4)):
                            pending.append((bi + 1 + d, f))

                    load_head(0)
                    load_q(0)
                    if len(jobs) > 1:
                        load_q(1)
                    s_mm(0)
                    for bi, (ji, hl, qi, kb) in enumerate(blocks):
                        nkb = 4 * qi + 4
                        if kb == 0:
                            if ji + 2 < len(jobs):
                                load_q(ji + 2)
                            if qi == 0 and hl + 1 < OH:
                                load_head(hl + 1)
                        if bi + 1 < len(blocks):
                            s_mm(bi + 1)
                        ps_ = psS[bi % 2]
                        p_ = P[bi % 4]
                        sb_ = sbias[bi % 2]
                        vb_ = vbb[hl % 2]
                        aD, aP = accD[ji % 2], accP[ji % 2]
                        dd = (kb - 4 * qi) * 128
                        if dd >= -128:
                            c0 = 384 - dd
                            k.op("dve", lambda e: e.scalar_tensor_tensor(
                                out=sb_.a[:], in0=ps_.a[:].rearrange("p (c q) -> p c q", c=2), scalar=0.125,
                                in1=strips.a[:, hl, c0:c0 + QT].unsqueeze(1).broadcast_to([128, 2, QT]), op0=ALU.mult, op1=ALU.add),
                                [ps_, strips], [sb_])
                            k.op("act", lambda e: e.activation(out=p_.a[:], in_=sb_.a[:], func=AF.Exp), [sb_], [p_])
                        else:
                            k.op("act", lambda e: e.activation(out=p_.a[:], in_=ps_.a[:].rearrange("p (c q) -> p c q", c=2), func=AF.Exp,
                                                               scale=0.125), [ps_], [p_])
                        first, last = (kb == 0), (kb == nkb - 1)
                        k.op("pe", lambda e: e.matmul(psO1.a[:], lhsT=vb_.a[:, kb, :], rhs=p_.a[:, 0, :], start=first, stop=last),
                             [vb_, p_], [psO1], inc=False)
                        k.op("pe", lambda e: e.matmul(psO2.a[:], lhsT=vb_.a[:, kb, :], rhs=p_.a[:, 1, :], start=first, stop=last),
                             [vb_, p_], [psO2])
                        pf = p_.a[:].rearrange("p c q -> p (c q)")
                        if first:
                            k.op("dve", lambda e: e.tensor_copy(out=aD.a[:], in_=pf[:, 0:XS]), [p_], [aD])
                            if XS < 2 * QT:
                                k.op("pool", lambda e: e.tensor_copy(out=aP.a[:], in_=pf[:, XS:2 * QT]), [p_], [aP])
                        else:
                            k.op("dve", lambda e: e.tensor_tensor(out=aD.a[:], in0=pf[:, 0:XS], in1=aD.a[:], op=ALU.add), [p_, aD], [aD])
                            if XS < 2 * QT:
                                k.op("pool", lambda e: e.tensor_tensor(out=aP.a[:], in0=pf[:, XS:2 * QT], in1=aP.a[:], op=ALU.add), [p_, aP], [aP])
                        due = [f for (d, f) in pending if d <= bi]
                        pending[:] = [(d, f) for (d, f) in pending if d > bi]
                        for f in due:
                            f()
                        if last:
                            epilogue(ji, hl, qi, bi)
                    for (d, f) in sorted(pending, key=lambda t: t[0]):
                        f()
                    k.barrier()

            def attnout_sublayer(j, Hs, xin, txin, xout, txout):
                l = NA + j
                Ts = min(512, To)
                WM = tile_w(Hs, Ts)
                with ExitStack() as st:
                    oa2 = [sb(st, f"o_oa{i}", [128, NH, WM], BF16) for i in range(2)]
                    ob2 = [sb(st, f"o_ob{i}", [128, NH, WM], BF16) for i in range(2)]
                    osel = sb(st, "o_osel", [128, NH, WM], BF16)
                    y = sb(st, "o_y", [128, KD, WM], F32)
                    sq = sb(st, "o_sq", [128, KD, WM], BF16)
                    rstd = sb(st, "o_rstd", [128, WM], F32)
                    tmp = [sb(st, f"o_tmp{i}", [128, WM], F32) for i in range(2)]
                    xres = [sb(st, f"o_xr{i}", [128, WM], F32) for i in range(2)]
                    wob = [sb(st, f"o_wo{i}", [128, NH, 128], BF16) for i in range(2)]
                    psY = [pst(st, f"o_psY{i}", [128, 512]) for i in range(2)]
                    psS = [pst(st, f"o_psS{i}", [128, 512]) for i in range(2)]
                    otiles = tiles_for(Hs, Ts)
                    wos = WStream(k, wob, [wou[j, oc] for _ in otiles for oc in range(KD)])
                    wos.get(0)
                    def load_o(ti):
                        a, b = otiles[ti]
                        oa, obb = oa2[ti % 2], ob2[ti % 2]
                        W = b - a
                        neg = max(0, -a)
                        if neg > 0:
                            k.op("dve", lambda e: e.memset(oa.a[:, :, 0:neg], 0.0), [], [oa])
                        for hl in range(OH):
                            for rr in range(2):
                                hg = rr * OH + hl
                                k.dma("sp", oa.a[:, hg, neg:W], Odst[hl][rr * 128:(rr + 1) * 128, a + neg:b], reads=[t_odst[hl]], writes=[oa])
                                k.dma("act", obb.a[:, hg, 0:W], Odst[hl][rr * 128:(rr + 1) * 128, To + a:To + b], reads=[t_odst[hl]], writes=[obb])
                    load_o(0)
                    for ti, (a, b) in enumerate(otiles):
                        W = b - a
                        neg = max(0, -a)
                        oa, obb = oa2[ti % 2], ob2[ti % 2]
                        if ti + 1 < len(otiles):
                            load_o(ti + 1)
                        k.op("dve", lambda e: e.tensor_scalar(out=osel.a[:, :, 0:W], in0=oa.a[:, :, 0:W], scalar1=vcol("hmc"), scalar2=None, op0=ALU.mult),
                             [oa, V], [osel])
                        k.op("dve", lambda e: e.scalar_tensor_tensor(out=osel.a[:, :, 0:W], in0=obb.a[:, :, 0:W], scalar=vcol("hm"), in1=osel.a[:, :, 0:W],
                                                                     op0=ALU.mult, op1=ALU.add), [obb, osel, V], [osel])
                        it = 0
                        for oc in range(KD):
                            wb_ = wos.get(ti * KD + oc)
                            for (n0, n) in split(W, 512):
                                py = psY[it % 2]
                                it += 1
                                for hh in range(NH):
                                    k.op("pe", lambda e: e.matmul(py.a[:, 0:n], lhsT=wb_.a[:, hh, :], rhs=osel.a[:, hh, n0:n0 + n],
                                                                  start=(hh == 0), stop=(hh == NH - 1)), [wb_, osel], [py], inc=(hh == NH - 1))
                                k.op("act", lambda e: e.activation(out=y.a[:, oc, n0:n0 + n], in_=py.a[:, 0:n], func=AF.Copy), [py], [y])
                        resid_out(y, W, sc_off(l, 1), xin, txin, xout, txout, a, b, sq, rstd, tmp, xres, psS)
                    k.barrier()

            HS = [38, 36, 6, 4, 4, 2, 2, 0]
            seq = [(x0, tX0), (XA, tXA), (XB, tXB)]
            state = {"cur": 0}

            def nxt():
                i = state["cur"]
                src = seq[0] if i == 0 else seq[1 + (i - 1) % 2]
                dst = seq[1 + i % 2]
                state["cur"] += 1
                return src[0], src[1], dst[0], dst[1]

            def finish_from(buf, tbuf):
                k.dma("sp", yout, buf[:, HL:HL + To], reads=[tbuf], writes=[tY])

            done = False
            si = 0
            for l in range(NA):
                xi, txi, xo, txo = nxt()
                mixer_sublayer(l, HS[si], xi, txi, xo, txo)
                si += 1
                if stop_after == f"m{l}":
                    finish_from(xo, txo)
                    done = True
                    break
                xi, txi, xo, txo = nxt()
                ffn_sublayer(l, HS[si], xi, txi, xo, txo, emit_xn=(l == NA - 1))
                si += 1
                if stop_after == f"f{l}":
                    finish_from(xo, txo)
                    done = True
                    break
            if not done:
                for j in range(L - NA):
                    l = NA + j
                    exchange_xn()
                    k.barrier()
                    kvq_phase(j, do_kv=(j == 0))
                    attn_phase(j)
                    xi, txi, xo, txo = nxt()
                    attnout_sublayer(j, HS[si], xi, txi, xo, txo)
                    si += 1
                    if stop_after == f"a{l}":
                        finish_from(xo, txo)
                        done = True
                        break
                    last = (l == L - 1)
                    xi, txi, xo, txo = nxt()
                    if last:
                        ffn_sublayer(l, HS[si], xi, txi, yout, tY, emit_xn=False, is_final=True)
                    else:
                        ffn_sublayer(l, HS[si], xi, txi, xo, txo, emit_xn=True)
                    si += 1
                    if stop_after == f"f{l}" and not last:
                        finish_from(xo, txo)
                        done = True
                        break
            k.barrier()
        nc._k_nins = k.nins
    return nc


_CACHE = {}


def run(cfg, inputs, stop_after=None, trace=False):
    key = (cfg.D, cfg.S, cfg.F, stop_after)
    if key not in _CACHE:
        _CACHE[key] = build(cfg, stop_after)
    nc = _CACHE[key]
    in_maps = [prepare_core(cfg, inputs, c) for c in range(8)]
    res = run_bass_kernel_spmd(nc, in_maps, core_ids=list(range(8)), **({"trace": True} if trace else {}))
    out = np.empty((cfg.B, cfg.S, cfg.D), np.float32)
    for c in range(8):
        b, r = c // 2, c % 2
        out[b, r * cfg.To:(r + 1) * cfg.To, :] = res.results[c]["yout"].T
    return out, res


def kernel(**inputs):
    inputs = {k_: np.asarray(v) for k_, v in inputs.items()}
    cfg = Cfg()
    out, _ = run(cfg, inputs)
    return out
```

```python
import math
from contextlib import ExitStack

import numpy as np
import concourse.bass as bass
import concourse.mybir as mybir
from concourse.bass_utils import run_bass_kernel_spmd

F32 = mybir.dt.float32
BF16 = mybir.dt.bfloat16
AF = mybir.ActivationFunctionType
ALU = mybir.AluOpType
AX = mybir.AxisListType

EPS = 1e-6
CONVW = 31
LG = 1151
NEG = -30000.0
SEM_LIMIT = 20000
NDMA = 24
CC_BYTES = 2 * 1024 * 1024


class Cfg:
    def __init__(self, D=1024, S=8192, F=2816, B=4, L=4):
        self.D, self.S, self.F, self.B, self.L = D, S, F, B, L
        self.KD = D // 128
        self.KF = F // 128
        self.NH = D // 128
        self.OH = self.NH // 2
        self.To = S // 2
        self.HL = 128
        self.NA = L // 2
        rc = CC_BYTES // (self.To * 2)
        rc = min(rc, D)
        rc = (rc // 128) * 128
        self.RC = rc
        self.NG = D // rc


class T:
    __slots__ = ("w", "r")

    def __init__(self):
        self.w = None
        self.r = {}


class Buf:
    def __init__(self, a):
        self.a = a
        self.t = T()


class K:
    def __init__(self, nc, es):
        self.nc = nc
        self.es = es
        self.eng = {"pe": nc.tensor, "act": nc.scalar, "dve": nc.vector, "pool": nc.gpsimd, "sp": nc.sync}
        self.gen = {e: 0 for e in ("pe", "act", "dve", "pool")}
        self.cnt = {e: 0 for e in ("pe", "act", "dve", "pool")}
        self.semh = {}
        self.final = {}
        for e in self.gen:
            self._newsem(e)
        self.seen = {e: {} for e in self.eng}
        self.dsem = [es.enter_context(nc.semaphore(f"dq{i}")) for i in range(NDMA)]
        for i in range(NDMA):
            self.semh[f"dq{i}"] = self.dsem[i]
        self.dcnt = [0] * NDMA
        self.dnext = 0
        self.ncc = 0
        self.nins = 0

    def _newsem(self, e):
        key = f"{e}{self.gen[e]}"
        self.semh[key] = self.es.enter_context(self.nc.semaphore("s_" + key))
        self.cnt[e] = 0

    def _key(self, e):
        return f"{e}{self.gen[e]}"

    def _wait(self, e, deps):
        best = {}
        for (k, v) in deps:
            if v > best.get(k, 0):
                best[k] = v
        for k, v in best.items():
            if e == "pe" and k.startswith("pe"):
                continue
            if self.seen[e].get(k, 0) >= v:
                continue
            self.eng[e].wait_ge(self.semh[k], v)
            self.seen[e][k] = v
            self.nins += 1

    @staticmethod
    def _deps(reads, writes):
        deps = []
        for t in reads:
            if t.w:
                deps.append(t.w)
        for t in writes:
            if t.w:
                deps.append(t.w)
            deps.extend(t.r.items())
        return deps

    @staticmethod
    def _mark(tok, reads, writes):
        k, v = tok
        for t in reads:
            if t.r.get(k, 0) < v:
                t.r[k] = v
        for t in writes:
            t.w = tok
            t.r = {}

    def op(self, e, fn, reads=(), writes=(), inc=True):
        reads = [b.t if isinstance(b, Buf) else b for b in reads]
        writes = [b.t if isinstance(b, Buf) else b for b in writes]
        self._wait(e, self._deps(reads, writes))
        ins = fn(self.eng[e])
        self.nins += 1
        if inc:
            self.cnt[e] += 1
            tok = (self._key(e), self.cnt[e])
            ins.then_inc(self.semh[tok[0]], 1)
            self._mark(tok, reads, writes)
            if self.cnt[e] >= SEM_LIMIT:
                self.final[tok[0]] = self.cnt[e]
                self.gen[e] += 1
                self._newsem(e)
        else:
            tok = (self._key(e), self.cnt[e] + 1)
            self._mark(tok, reads, writes)
        return tok

    def dma(self, e, out, in_, reads=(), writes=(), **kw):
        reads = [b.t if isinstance(b, Buf) else b for b in reads]
        writes = [b.t if isinstance(b, Buf) else b for b in writes]
        i = self.dnext
        self.dnext = (self.dnext + 1) % NDMA
        key = f"dq{i}"
        deps = self._deps(reads, writes)
        if self.dcnt[i] > 0:
            deps.append((key, 16 * self.dcnt[i]))
        self._wait(e, deps)
        ins = self.eng[e].dma_start(out=out, in_=in_, **kw)
        self.nins += 1
        self.dcnt[i] += 1
        tok = (key, 16 * self.dcnt[i])
        ins.then_inc(self.dsem[i], 16)
        self._mark(tok, reads, writes)
        return tok

    def coll(self, src, dst, reads=(), writes=()):
        reads = [b.t if isinstance(b, Buf) else b for b in reads]
        writes = [b.t if isinstance(b, Buf) else b for b in writes]
        key = f"cc{self.ncc}"
        self.ncc += 1
        sem = self.es.enter_context(self.nc.semaphore("s_" + key))
        self.semh[key] = sem
        self._wait("pool", self._deps(reads, writes))
        ins = self.nc.gpsimd.collective_compute("AllGather", ALU.bypass, replica_groups=[[0, 1], [2, 3], [4, 5], [6, 7]],
                                                ins=[src], outs=[dst])
        ins.then_inc(sem, 1)
        self.nins += 1
        tok = (key, 1)
        self.final[key] = 1
        self._mark(tok, reads, writes)
        return tok

    def all_tokens(self):
        toks = [(k, v) for k, v in self.final.items()]
        toks += [(self._key(e), self.cnt[e]) for e in self.cnt if self.cnt[e] > 0]
        toks += [(f"dq{i}", 16 * self.dcnt[i]) for i in range(NDMA) if self.dcnt[i] > 0]
        return toks

    def barrier(self):
        toks = self.all_tokens()
        for e in ("pe", "act", "dve", "pool", "sp"):
            self._wait(e, toks)


class WStream:
    def __init__(self, k, bufs, srcs):
        self.k, self.bufs, self.srcs = k, bufs, srcs
        self.issued = 0

    def _issue(self):
        i = self.issued
        b = self.bufs[i % len(self.bufs)]
        self.k.dma("pool", b.a[:].rearrange("p k j -> p (k j)"), self.srcs[i], writes=[b])
        self.issued += 1

    def get(self, i):
        while self.issued <= min(i + len(self.bufs) - 1, len(self.srcs) - 1):
            self._issue()
        return self.bufs[i % len(self.bufs)]


def split(total, maxn=512):
    n = -(-total // maxn)
    base = -(-total // n)
    out = []
    s = 0
    while s < total:
        w = min(base, total - s)
        out.append((s, w))
        s += w
    return out


def _fm(v):
    v = np.asarray(v, np.float32).reshape(-1, 128)
    return np.ascontiguousarray(v.T)


def t5_bucket_np(n):
    n = np.asarray(n)
    n_f = np.maximum(n, 1).astype(np.float32)
    large = 16 + (np.log(n_f / np.float32(16)) / np.float32(math.log(128 / 16)) * np.float32(16)).astype(np.int32)
    large = np.minimum(large, 31)
    return np.where(n < 16, n, large)


def _t5_bucket_jaxlike():
    n = np.arange(0, LG, dtype=np.int32)
    return t5_bucket_np(n)


class VecTable:
    def __init__(self):
        self.cols = []
        self.off = {}
        self.n = 0

    def add(self, name, arr):
        arr = np.asarray(arr, np.float32)
        assert arr.shape[0] == 128
        self.off[name] = self.n
        self.cols.append(arr)
        self.n += arr.shape[1]

    def build(self):
        return np.ascontiguousarray(np.concatenate(self.cols, axis=1))


def vec_layout(cfg):
    vt = {}
    n = 0

    def add(name, w):
        nonlocal n
        vt[name] = n
        n += w
    KD, KF, L, NA = cfg.KD, cfg.KF, cfg.L, cfg.NA
    add("cT", KD)
    add("modb", L * 6 * KD)
    for l in range(L):
        for i in range(4):
            add(f"ng{l}{i}", KD)
    for l in range(NA):
        add(f"b1a{l}", KD)
        add(f"b1g{l}", KD)
        add(f"dw{l}", KD * CONVW)
        add(f"dwb{l}", KD)
        add(f"lng{l}", KD)
        add(f"lnb{l}", KD)
        add(f"b2{l}", KD)
    add("kvg", KD)
    for j in range(L - NA):
        add(f"lam{j}", 256)
        add(f"subg{j}", 1)
    for l in range(L):
        add(f"fdw{l}", 3 * 2 * KF)
        add(f"fdwb{l}", 2 * KF)
    add("hm", 1)
    add("hmc", 1)
    return vt, n


def prepare_core(cfg, inp, c):
    D, S, F, L, NA, KD, KF, NH, OH, To, HL = cfg.D, cfg.S, cfg.F, cfg.L, cfg.NA, cfg.KD, cfg.KF, cfg.NH, cfg.OH, cfg.To, cfg.HL
    b, r = c // 2, c % 2
    f32 = np.float32
    m = {}
    x = inp["x"][b]
    x0 = np.zeros((D, HL + To), f32)
    lo = r * To - HL
    src_lo = max(lo, 0)
    x0[:, src_lo - lo:] = x[src_lo:r * To + To].T
    m["x0"] = x0
    vt = VecTable()
    vt.add("cT", _fm(inp["c"][b]))
    vt.add("modb", np.concatenate([_fm(inp["mod_b"][l]) for l in range(L)], 1))
    for l in range(L):
        for i in range(4):
            vt.add(f"ng{l}{i}", _fm(inp["norm_g"][l, i]))
    for l in range(NA):
        vt.add(f"b1a{l}", _fm(inp["cm_b1"][l][:D]))
        vt.add(f"b1g{l}", _fm(inp["cm_b1"][l][D:]))
        dw = inp["cm_dw"][l]
        vt.add(f"dw{l}", np.ascontiguousarray(dw.reshape(CONVW, KD, 128).transpose(2, 1, 0)).reshape(128, KD * CONVW))
        vt.add(f"dwb{l}", _fm(inp["cm_dwb"][l]))
        vt.add(f"lng{l}", _fm(inp["cm_ln_g"][l]))
        vt.add(f"lnb{l}", _fm(inp["cm_ln_b"][l]))
        vt.add(f"b2{l}", _fm(inp["cm_b2"][l]))
    vt.add("kvg", _fm(inp["kv_norm_g"]))
    for j in range(L - NA):
        vt.add(f"lam{j}", np.tile(np.asarray(inp["lam"][j], f32).reshape(1, 256), (128, 1)))
        vt.add(f"subg{j}", np.asarray(inp["subln_g"][j], f32).reshape(128, 1))
    for l in range(L):
        fdw = inp["ffn_dw"][l]
        vt.add(f"fdw{l}", np.ascontiguousarray(fdw.reshape(3, 2 * KF, 128).transpose(2, 0, 1)).reshape(128, 3 * 2 * KF))
        vt.add(f"fdwb{l}", _fm(inp["ffn_dwb"][l]))
    vt.add("hm", np.full((128, 1), float(r), f32))
    vt.add("hmc", np.full((128, 1), float(1 - r), f32))
    lay, n = vec_layout(cfg)
    assert lay == vt.off and n == vt.n
    m["vecs"] = vt.build()
    w1 = np.stack([inp["cm_w1"][l].reshape(KD, 128, 2, KD, 128).transpose(3, 1, 0, 2, 4).reshape(KD, 128, KD * 256)
                   for l in range(NA)])
    m["w1u"] = np.ascontiguousarray(w1, f32)
    w2 = np.stack([inp["cm_w2"][l].reshape(KD, 128, KD, 128).transpose(2, 1, 0, 3).reshape(KD, 128, KD * 128)
                   for l in range(NA)])
    m["w2u"] = np.ascontiguousarray(w2, f32)
    wi = np.stack([inp["ffn_w_in"][l].reshape(KD, 128, 2, KF, 128).transpose(3, 1, 0, 2, 4).reshape(KF, 128, KD * 256)
                   for l in range(L)])
    m["winu"] = np.ascontiguousarray(wi, f32)
    wo = np.stack([inp["ffn_w_out"][l].reshape(KF, 128, KD, 128).transpose(2, 1, 0, 3).reshape(KD, 128, KF * 128)
                   for l in range(L)])
    m["woutu"] = np.ascontiguousarray(wo, f32)
    hs = slice(r * OH * 128, (r + 1) * OH * 128)

    def own(w):
        return np.ascontiguousarray(w.reshape(KD, 128, NH * 128)[:, :, hs].transpose(1, 0, 2).reshape(128, KD * OH * 128), f32)
    m["wk"] = own(inp["w_k"])
    m["wv"] = own(inp["w_v"])
    m["wq"] = np.stack([own(inp["w_q"][j]) for j in range(L - NA)])
    wou = np.stack([inp["w_o"][j].reshape(NH, 128, KD, 128).transpose(2, 1, 0, 3).reshape(KD, 128, NH * 128)
                    for j in range(L - NA)])
    m["wou"] = np.ascontiguousarray(wou, f32)
    m["modw"] = np.ascontiguousarray(inp["mod_w"][:, :, r * 3 * D:(r + 1) * 3 * D], f32)
    rbx = np.ones((33, OH), f32)
    rbx[:32] = inp["rel_bias"][:, r * OH:(r + 1) * OH]
    m["rbx"] = rbx
    m["ident"] = np.eye(128, dtype=f32)
    m["antiid"] = np.ascontiguousarray(np.eye(128, dtype=f32)[::-1])
    ohc = np.zeros((33, LG), f32)
    nn = np.arange(LG) - 511
    bk = t5_bucket_np(np.maximum(nn, 0))
    pos = nn >= 0
    ohc[bk[pos], np.nonzero(pos)[0]] = 1.0
    ohc[31, pos] -= 1.0
    ohc[32, ~pos] = NEG
    m["ohc"] = ohc
    return m


def build(cfg, stop_after=None):
    D, S, F, L, NA, KD, KF, NH, OH, To, HL = cfg.D, cfg.S, cfg.F, cfg.L, cfg.NA, cfg.KD, cfg.KF, cfg.NH, cfg.OH, cfg.To, cfg.HL
    RC, NG = cfg.RC, cfg.NG
    SQD = math.sqrt(D)
    nc = bass.Bass("TRN2", target_bir_lowering=False)
    VO, NV = vec_layout(cfg)

    def din(name, shape, dt=F32):
        return nc.dram_tensor(name, list(shape), dt, kind="ExternalInput").ap()

    def dint(name, shape, dt):
        return nc.dram_tensor(name, list(shape), dt).ap()

    x0 = din("x0", [D, HL + To])
    vecs_d = din("vecs", [128, NV])
    w1u = din("w1u", [NA, KD, 128, KD * 256])
    w2u = din("w2u", [NA, KD, 128, KD * 128])
    winu = din("winu", [L, KF, 128, KD * 256])
    woutu = din("woutu", [L, KD, 128, KF * 128])
    wk_d = din("wk", [128, KD * OH * 128])
    wv_d = din("wv", [128, KD * OH * 128])
    wq_d = din("wq", [L - NA, 128, KD * OH * 128])
    wou = din("wou", [L - NA, KD, 128, NH * 128])
    modw = din("modw", [L, D, 3 * D])
    rbx_d = din("rbx", [33, OH])
    ident_d = din("ident", [128, 128])
    antiid_d = din("antiid", [128, 128])
    ohc_d = din("ohc", [33, LG])
    yout = nc.dram_tensor("yout", [D, To], F32, kind="ExternalOutput").ap()

    XA = dint("XA", [D, HL + To], F32)
    XB = dint("XB", [D, HL + To], F32)
    tXA, tXB, tX0, tY = T(), T(), T(), T()
    modsrc = dint("modsrc", [L, 3 * D], F32)
    moddst = dint("moddst", [2 * L, 3 * D], F32)
    gvec = dint("gvec", [OH, LG + 1], F32)
    xnsrc = [dint(f"xnsrc{i}", [RC, To], BF16) for i in range(NG)]
    xndst = [dint(f"xndst{i}", [2 * RC, To], BF16) for i in range(NG)]
    t_xnsrc = [T() for _ in range(NG)]
    t_xndst = [T() for _ in range(NG)]
    Kt = dint("Kt", [OH, 128, S], BF16)
    Qt = dint("Qt", [OH, 128, S], BF16)
    Vd = dint("Vd", [OH, 128, S // 128, 128], BF16)
    tKt, tQt, tVd = T(), T(), T()
    Osrc = [dint(f"Osrc{h}", [128, S], BF16) for h in range(OH)]
    Odst = [dint(f"Odst{h}", [256, S], BF16) for h in range(OH)]
    t_osrc = [T() for _ in range(OH)]
    t_odst = [T() for _ in range(OH)]

    with ExitStack() as es:
        k = K(nc, es)

        uid = [0]

        def sb(st, name, shape, dt):
            uid[0] += 1
            return Buf(st.enter_context(nc.sbuf_tensor(f"sb{uid[0]}_{name}", list(shape), dt)))

        def pst(st, name, shape):
            uid[0] += 1
            return Buf(st.enter_context(nc.psum_tensor(f"ps{uid[0]}_{name}", list(shape), F32)))

        block = es.enter_context(nc.Block())

        V = sb(es, "V", [128, NV], F32)
        SC = sb(es, "SC", [128, L * 6 * KD + KD + 8], F32)
        modT = sb(es, "modT", [128, L * 6 * KD], F32)
        ident_bf = sb(es, "ident_bf", [128, 128], BF16)
        ones_bf = sb(es, "ones_bf", [128, 128], BF16)
        antif = sb(es, "antif", [128, 128], F32)

        def vcol(name, i=0, w=1):
            o = VO[name] + i
            return V.a[:, o:o + w]

        def sc_off(l, which):
            return (l * 4 + which) * KD
        KVG_O = L * 4 * KD
        NLAM_O = KVG_O + KD
        SG_O = NLAM_O + 2

        def scv(off, i=0, w=1):
            return SC.a[:, off + i:off + i + w]

        def modcol(l, which, kk):
            o = l * 6 * KD + which * KD + kk
            return modT.a[:, o:o + 1]

        @block.sync
        def _(sync):
            k.dma("sp", V.a[:], vecs_d, writes=[V])
            with ExitStack() as ps_:
                identf = sb(ps_, "identf", [128, 128], F32)
                cact = sb(ps_, "cact", [128, KD], F32)
                wt = [sb(ps_, f"mwt{i}", [128, KD, 512], F32) for i in range(2)]
                rowbuf = sb(ps_, "rowbuf", [1, L * 3 * D], F32)
                rbx = sb(ps_, "rbx", [33, OH], F32)
                ohc = sb(ps_, "ohc", [33, LG], F32)
                grow = sb(ps_, "grow", [OH, LG + 1], F32)
                lt = sb(ps_, "lt", [128, 128], F32)
                ls = sb(ps_, "ls", [128, 4], F32)
                pp = [pst(ps_, f"pp{i}", [128, 512]) for i in range(2)]
                k.dma("sp", identf.a[:], ident_d, writes=[identf])
                k.dma("sp", antif.a[:], antiid_d, writes=[antif])
                k.op("act", lambda e: e.activation(out=ident_bf.a[:], in_=identf.a[:], func=AF.Copy), [identf], [ident_bf])
                k.op("dve", lambda e: e.memset(ones_bf.a[:], 1.0), [], [ones_bf])
                k.op("act", lambda e: e.activation(out=cact.a[:], in_=vcol("cT", 0, KD), func=AF.Silu), [V], [cact])
                it = 0
                for l in range(L):
                    for (n0, n) in split(3 * D, 512):
                        w_ = wt[it % 2]
                        p_ = pp[it % 2]
                        it += 1
                        k.dma("sp", w_.a[:, :, 0:n], modw[l].rearrange("(k p) n -> p k n", p=128)[:, :, n0:n0 + n], writes=[w_])
                        for kk in range(KD):
                            k.op("pe", lambda e: e.matmul(p_.a[0:1, 0:n], lhsT=cact.a[:, kk:kk + 1], rhs=w_.a[:, kk, 0:n],
                                                          start=(kk == 0), stop=(kk == KD - 1)),
                                 [cact, w_], [p_], inc=(kk == KD - 1))
                        o = l * 3 * D + n0
                        k.op("act", lambda e: e.activation(out=rowbuf.a[0:1, o:o + n], in_=p_.a[0:1, 0:n], func=AF.Copy),
                             [p_], [rowbuf])
                tms, tmd = T(), T()
                k.dma("sp", modsrc.rearrange("(o l) n -> o (l n)", o=1), rowbuf.a[0:1, :], reads=[rowbuf], writes=[tms])
                k.coll(modsrc, moddst, reads=[tms], writes=[tmd])
                for l in range(L):
                    for rr in range(2):
                        o = l * 6 * KD + rr * 3 * KD
                        k.dma("sp", modT.a[:, o:o + 3 * KD], moddst[rr * L + l].rearrange("(j p) -> p j", p=128),
                              reads=[tmd], writes=[modT], allow_slow_non_contiguous=True)
                k.op("dve", lambda e: e.tensor_tensor(out=modT.a[:], in0=modT.a[:], in1=vcol("modb", 0, L * 6 * KD), op=ALU.add),
                     [modT, V], [modT])
                for l in range(L):
                    for (which, scw, gw, ngA, ngG) in ((0, 1, 2, 0, 1), (2, 4, 5, 2, 3)):
                        oA = sc_off(l, which)
                        oG = sc_off(l, which + 1)
                        mo = l * 6 * KD
                        k.op("dve", lambda e: e.tensor_scalar(out=SC.a[:, oA:oA + KD], in0=modT.a[:, mo + scw * KD:mo + (scw + 1) * KD],
                                                              scalar1=1.0, scalar2=SQD, op0=ALU.add, op1=ALU.mult), [modT], [SC])
                        k.op("dve", lambda e: e.tensor_tensor(out=SC.a[:, oA:oA + KD], in0=SC.a[:, oA:oA + KD],
                                                              in1=vcol(f"ng{l}{ngA}", 0, KD), op=ALU.mult), [SC, V], [SC])
                        k.op("dve", lambda e: e.tensor_scalar(out=SC.a[:, oG:oG + KD], in0=modT.a[:, mo + gw * KD:mo + (gw + 1) * KD],
                                                              scalar1=SQD, scalar2=None, op0=ALU.mult), [modT], [SC])
                        k.op("dve", lambda e: e.tensor_tensor(out=SC.a[:, oG:oG + KD], in0=SC.a[:, oG:oG + KD],
                                                              in1=vcol(f"ng{l}{ngG}", 0, KD), op=ALU.mult), [SC, V], [SC])
                k.op("dve", lambda e: e.tensor_scalar(out=SC.a[:, KVG_O:KVG_O + KD], in0=vcol("kvg", 0, KD), scalar1=SQD, scalar2=None,
                                                      op0=ALU.mult), [V], [SC])
                for j in range(L - NA):
                    l = NA + j
                    lam_init = 0.8 - 0.6 * math.exp(-0.3 * l)
                    lo = VO[f"lam{j}"]
                    k.op("dve", lambda e: e.tensor_tensor(out=lt.a[:, 0:64], in0=V.a[:, lo:lo + 64], in1=V.a[:, lo + 64:lo + 128], op=ALU.mult),
                         [V], [lt])
                    k.op("dve", lambda e: e.tensor_tensor(out=lt.a[:, 64:128], in0=V.a[:, lo + 128:lo + 192], in1=V.a[:, lo + 192:lo + 256],
                                                          op=ALU.mult), [V, lt], [lt])
                    k.op("dve", lambda e: e.tensor_reduce(out=ls.a[:, 0:1], in_=lt.a[:, 0:64], axis=AX.X, op=ALU.add), [lt], [ls])
                    k.op("dve", lambda e: e.tensor_reduce(out=ls.a[:, 1:2], in_=lt.a[:, 64:128], axis=AX.X, op=ALU.add), [lt, ls], [ls])
                    k.op("act", lambda e: e.activation(out=ls.a[:, 2:4], in_=ls.a[:, 0:2], func=AF.Exp), [ls], [ls])
                    k.op("dve", lambda e: e.tensor_tensor(out=ls.a[:, 0:1], in0=ls.a[:, 3:4], in1=ls.a[:, 2:3], op=ALU.subtract), [ls], [ls])
                    k.op("dve", lambda e: e.tensor_scalar(out=SC.a[:, NLAM_O + j:NLAM_O + j + 1], in0=ls.a[:, 0:1], scalar1=-lam_init,
                                                          scalar2=None, op0=ALU.add), [ls], [SC])
                    k.op("dve", lambda e: e.tensor_scalar(out=SC.a[:, SG_O + j:SG_O + j + 1], in0=vcol(f"subg{j}"),
                                                          scalar1=(1.0 - lam_init) * math.sqrt(128.0), scalar2=None, op0=ALU.mult), [V], [SC])
                k.dma("sp", rbx.a[:], rbx_d, writes=[rbx])
                k.dma("sp", ohc.a[:], ohc_d, writes=[ohc])
                for (n0, n) in split(LG, 512):
                    p_ = pp[it % 2]
                    it += 1
                    k.op("pe", lambda e: e.matmul(p_.a[0:OH, 0:n], lhsT=rbx.a[:, :], rhs=ohc.a[:, n0:n0 + n], start=True, stop=True),
                         [rbx, ohc], [p_])
                    k.op("act", lambda e: e.activation(out=grow.a[:, n0:n0 + n], in_=p_.a[0:OH, 0:n], func=AF.Copy), [p_], [grow])
                tgv = T()
                k.dma("sp", gvec[:, 0:LG], grow.a[:, 0:LG], reads=[grow], writes=[tgv])
                k.barrier()

            def rms_rstd(st_sq, W, rstd, psb, extra_eps=D * EPS):
                for i, (n0, n) in enumerate(split(W, 512)):
                    p_ = psb[i % len(psb)]
                    for kk in range(KD):
                        k.op("pe", lambda e: e.matmul(p_.a[:, 0:n], lhsT=ones_bf.a[:], rhs=st_sq.a[:, kk, n0:n0 + n],
                                                      start=(kk == 0), stop=(kk == KD - 1)), [ones_bf, st_sq], [p_], inc=(kk == KD - 1))
                    k.op("act", lambda e: e.activation(out=rstd.a[:, n0:n0 + n], in_=p_.a[:, 0:n], func=AF.Ln, bias=extra_eps, scale=1.0),
                         [p_], [rstd])
                    k.op("act", lambda e: e.activation(out=rstd.a[:, n0:n0 + n], in_=rstd.a[:, n0:n0 + n], func=AF.Exp, scale=-0.5),
                         [rstd], [rstd])

            def xview(xd, c0, c1):
                return xd.rearrange("(k p) t -> p k t", p=128)[:, :, c0:c1]

            def tiles_for(Hs, Ts):
                nt = -(-To // Ts)
                step = -(-(To // nt) // 64) * 64
                bounds = [-Hs] + [min(To, step * i) for i in range(1, nt)] + [To]
                return [(bounds[i], bounds[i + 1]) for i in range(len(bounds) - 1)]

            def tile_w(Hs, Ts):
                return max(b - a for (a, b) in tiles_for(Hs, Ts))

            def norm_in(st, xin, txin, a, H, b, A_off, B_l, B_which, xt, sq, rstd, tmp, h, psb, mask_h):
                W = b - a + H
                k.dma("sp", xt.a[:, :, 0:W], xview(xin, HL + a - H, HL + b), reads=[txin], writes=[xt])
                k.op("act", lambda e: e.activation(out=sq.a[:, :, 0:W], in_=xt.a[:, :, 0:W], func=AF.Square), [xt], [sq])
                rms_rstd(sq, W, rstd, psb)
                for kk in range(KD):
                    t_ = tmp[kk % 2]
                    k.op("dve", lambda e: e.scalar_tensor_tensor(out=t_.a[:, 0:W], in0=xt.a[:, kk, 0:W], scalar=scv(A_off, kk),
                                                                 in1=rstd.a[:, 0:W], op0=ALU.mult, op1=ALU.mult), [xt, SC, rstd], [t_])
                    k.op("act", lambda e: e.activation(out=h.a[:, kk, 0:W], in_=t_.a[:, 0:W], func=AF.Identity,
                                                       bias=modcol(B_l, B_which, kk), scale=1.0), [t_, modT], [h])
                if mask_h > 0:
                    k.op("dve", lambda e: e.tensor_scalar(out=h.a[:, :, 0:mask_h], in0=h.a[:, :, 0:mask_h], scalar1=vcol("hm"), scalar2=None,
                                                          op0=ALU.mult), [h, V], [h])

            def resid_out(y, Wo, G_off, xin, txin, xout, txout, a, b, sq, rstd, tmp, xres, psb, emit_xn=None, is_final=False):
                k.op("act", lambda e: e.activation(out=sq.a[:, :, 0:Wo], in_=y.a[:, :, 0:Wo], func=AF.Square), [y], [sq])
                rms_rstd(sq, Wo, rstd, psb)
                for kk in range(KD):
                    xr = xres[kk % 2]
                    t_ = tmp[kk % 2]
                    k.dma("sp", xr.a[:, 0:Wo], xin[kk * 128:(kk + 1) * 128, HL + a:HL + b], reads=[txin], writes=[xr])
                    k.op("dve", lambda e: e.tensor_tensor(out=t_.a[:, 0:Wo], in0=y.a[:, kk, 0:Wo], in1=rstd.a[:, 0:Wo], op=ALU.mult),
                         [y, rstd], [t_])
                    k.op("act", lambda e: e.activation(out=t_.a[:, 0:Wo], in_=t_.a[:, 0:Wo], func=AF.Identity, scale=scv(G_off, kk)),
                         [t_, SC], [t_])
                    k.op("pool", lambda e: e.tensor_tensor(out=y.a[:, kk, 0:Wo], in0=t_.a[:, 0:Wo], in1=xr.a[:, 0:Wo], op=ALU.add),
                         [t_, xr], [y])
                if is_final:
                    o0 = max(0, -a)
                    k.dma("sp", xview(xout, a + o0, b), y.a[:, :, o0:Wo], reads=[y], writes=[txout])
                else:
                    k.dma("sp", xview(xout, HL + a, HL + b), y.a[:, :, 0:Wo], reads=[y], writes=[txout])
                if emit_xn is not None:
                    hbuf = emit_xn
                    k.op("act", lambda e: e.activation(out=sq.a[:, :, 0:Wo], in_=y.a[:, :, 0:Wo], func=AF.Square), [y], [sq])
                    rms_rstd(sq, Wo, rstd, psb)
                    o0 = max(0, -a)
                    for kk in range(KD):
                        k.op("dve", lambda e: e.tensor_tensor(out=hbuf.a[:, kk, 0:Wo], in0=y.a[:, kk, 0:Wo], in1=rstd.a[:, 0:Wo], op=ALU.mult),
                             [y, rstd], [hbuf])
                    per = RC // 128
                    for i in range(NG):
                        k.dma("sp", xnsrc[i].rearrange("(k p) t -> p k t", p=128)[:, :, a + o0:b],
                              hbuf.a[:, i * per:(i + 1) * per, o0:Wo], reads=[hbuf], writes=[t_xnsrc[i]])

            def exchange_xn():
                for i in range(NG):
                    k.coll(xnsrc[i], xndst[i], reads=[t_xnsrc[i]], writes=[t_xndst[i]])

            def mixer_sublayer(l, Hs, xin, txin, xout, txout):
                H = CONVW - 1
                Ts = min(512, To)
                WO = tile_w(Hs, Ts)
                WM = WO + H
                with ExitStack() as st:
                    xt = sb(st, "m_xt", [128, KD, WM], F32)
                    ybuf = sb(st, "m_y", [128, KD, WO], F32)
                    sq = sb(st, "m_sq", [128, KD, WM], BF16)
                    rstd = sb(st, "m_rstd", [128, WM], F32)
                    r2 = sb(st, "m_r2", [128, 3, 512], F32)
                    tmp = [sb(st, f"m_tmp{i}", [128, WM], F32) for i in range(2)]
                    xres = [sb(st, f"m_xr{i}", [128, WO], F32) for i in range(2)]
                    h = sb(st, "m_h", [128, KD, WM], BF16)
                    u = sb(st, "m_u", [128, KD, WM], BF16)
                    v32 = sb(st, "m_v", [128, KD, WO], F32)
                    vb = sb(st, "m_vb", [128, KD, WO], BF16)
                    z = sb(st, "m_z", [128, KD, WO], BF16)
                    sig = [sb(st, f"m_sig{i}", [128, 512], F32) for i in range(2)]
                    w1b = [sb(st, f"m_w1{i}", [128, KD, 256], BF16) for i in range(2)]
                    w2b = [sb(st, f"m_w2{i}", [128, KD, 128], BF16) for i in range(2)]
                    dg = [sb(st, f"m_dg{i}", [128, CONVW, 128], BF16) for i in range(2)]
                    psA = [pst(st, f"m_psA{i}", [128, 512]) for i in range(2)]
                    psG = [pst(st, f"m_psG{i}", [128, 512]) for i in range(2)]
                    psC = [pst(st, f"m_psC{i}", [128, 512]) for i in range(2)]
                    psS = [pst(st, f"m_psS{i}", [128, 512]) for i in range(2)]
                    wi = 0
                    mtiles = tiles_for(Hs, Ts)
                    w1s = WStream(k, w1b, [w1u[l, oc] for _ in mtiles for oc in range(KD)])
                    w2s = WStream(k, w2b, [w2u[l, oc] for _ in mtiles for oc in range(KD)])
                    w1s.get(0)
                    w2s.get(0)
                    def nin(ti):
                        a, b = mtiles[ti]
                        norm_in(st, xin, txin, a, H, b, sc_off(l, 0), l, 0, xt, sq, rstd, tmp, h, psS, 0)
                    nin(0)
                    for ti, (a, b) in enumerate(mtiles):
                        W = b - a + H
                        Wo = b - a
                        it = 0
                        for oc in range(KD):
                            wb_ = w1s.get(ti * KD + oc)
                            for (n0, n) in split(W, 512):
                                pa, pg, sg_ = psA[it % 2], psG[it % 2], sig[it % 2]
                                it += 1
                                for kk in range(KD):
                                    k.op("pe", lambda e: e.matmul(pa.a[:, 0:n], lhsT=wb_.a[:, kk, 0:128], rhs=h.a[:, kk, n0:n0 + n],
                                                                  start=(kk == 0), stop=(kk == KD - 1)), [wb_, h], [pa], inc=(kk == KD - 1))
                                for kk in range(KD):
                                    k.op("pe", lambda e: e.matmul(pg.a[:, 0:n], lhsT=wb_.a[:, kk, 128:256], rhs=h.a[:, kk, n0:n0 + n],
                                                                  start=(kk == 0), stop=(kk == KD - 1)), [wb_, h], [pg], inc=(kk == KD - 1))
                                k.op("act", lambda e: e.activation(out=sg_.a[:, 0:n], in_=pg.a[:, 0:n], func=AF.Sigmoid,
                                                                   bias=vcol(f"b1g{l}", oc), scale=1.0), [pg, V], [sg_])
                                k.op("dve", lambda e: e.scalar_tensor_tensor(out=u.a[:, oc, n0:n0 + n], in0=pa.a[:, 0:n], scalar=vcol(f"b1a{l}", oc),
                                                                             in1=sg_.a[:, 0:n], op0=ALU.add, op1=ALU.mult), [pa, sg_, V], [u])
                        if ti == 0:
                            nn = Hs + H
                            k.op("dve", lambda e: e.tensor_scalar(out=u.a[:, :, 0:nn], in0=u.a[:, :, 0:nn], scalar1=vcol("hm"), scalar2=None,
                                                                  op0=ALU.mult), [u, V], [u])
                        if ti + 1 < len(mtiles):
                            nin(ti + 1)
                        it = 0
                        for c in range(KD):
                            d_ = dg[c % 2]
                            do = VO[f"dw{l}"] + c * CONVW
                            k.op("pool", lambda e: e.tensor_tensor(out=d_.a[:], in0=ident_bf.a[:].unsqueeze(1).broadcast_to([128, CONVW, 128]),
                                                                   in1=V.a[:, do:do + CONVW].unsqueeze(2).broadcast_to([128, CONVW, 128]),
                                                                   op=ALU.mult), [ident_bf, V], [d_])
                            for (n0, n) in split(Wo, 512):
                                pc = psC[it % 2]
                                it += 1
                                for j in range(CONVW):
                                    k.op("pe", lambda e: e.matmul(pc.a[:, 0:n], lhsT=d_.a[:, j, :], rhs=u.a[:, c, n0 + j:n0 + j + n],
                                                                  start=(j == 0), stop=(j == CONVW - 1)), [d_, u], [pc], inc=(j == CONVW - 1))
                                k.op("act", lambda e: e.activation(out=v32.a[:, c, n0:n0 + n], in_=pc.a[:, 0:n], func=AF.Identity,
                                                                   bias=vcol(f"dwb{l}", c), scale=1.0), [pc, V], [v32])
                        k.op("pool", lambda e: e.tensor_copy(out=vb.a[:, :, 0:Wo], in_=v32.a[:, :, 0:Wo]), [v32], [vb])
                        k.op("act", lambda e: e.activation(out=sq.a[:, :, 0:Wo], in_=v32.a[:, :, 0:Wo], func=AF.Square), [v32], [sq])
                        for i, (n0, n) in enumerate(split(Wo, 512)):
                            pm, pq = psS[0], psS[1]
                            for kk in range(KD):
                                k.op("pe", lambda e: e.matmul(pm.a[:, 0:n], lhsT=ones_bf.a[:], rhs=vb.a[:, kk, n0:n0 + n],
                                                              start=(kk == 0), stop=(kk == KD - 1)), [ones_bf, vb], [pm], inc=(kk == KD - 1))
                            for kk in range(KD):
                                k.op("pe", lambda e: e.matmul(pq.a[:, 0:n], lhsT=ones_bf.a[:], rhs=sq.a[:, kk, n0:n0 + n],
                                                              start=(kk == 0), stop=(kk == KD - 1)), [ones_bf, sq], [pq], inc=(kk == KD - 1))
                            k.op("dve", lambda e: e.tensor_scalar(out=r2.a[:, 0, 0:n], in0=pm.a[:, 0:n], scalar1=1.0 / D, scalar2=None,
                                                                  op0=ALU.mult), [pm], [r2])
                            k.op("dve", lambda e: e.tensor_tensor(out=r2.a[:, 1, 0:n], in0=r2.a[:, 0, 0:n], in1=r2.a[:, 0, 0:n], op=ALU.mult),
                                 [r2], [r2])
                            k.op("dve", lambda e: e.scalar_tensor_tensor(out=r2.a[:, 1, 0:n], in0=pq.a[:, 0:n], scalar=1.0 / D, in1=r2.a[:, 1, 0:n],
                                                                         op0=ALU.mult, op1=ALU.subtract), [pq, r2], [r2])
                            k.op("act", lambda e: e.activation(out=r2.a[:, 1, 0:n], in_=r2.a[:, 1, 0:n], func=AF.Ln, bias=EPS, scale=1.0), [r2], [r2])
                            k.op("act", lambda e: e.activation(out=r2.a[:, 1, 0:n], in_=r2.a[:, 1, 0:n], func=AF.Exp, scale=-0.5), [r2], [r2])
                            k.op("dve", lambda e: e.scalar_tensor_tensor(out=r2.a[:, 2, 0:n], in0=r2.a[:, 0, 0:n], scalar=-1.0, in1=r2.a[:, 1, 0:n],
                                                                         op0=ALU.mult, op1=ALU.mult), [r2], [r2])
                            for kk in range(KD):
                                t_ = tmp[kk % 2]
                                k.op("dve", lambda e: e.tensor_tensor(out=t_.a[:, 0:n], in0=v32.a[:, kk, n0:n0 + n], in1=r2.a[:, 1, 0:n],
                                                                      op=ALU.mult), [v32, r2], [t_])
                                k.op("pool", lambda e: e.tensor_tensor(out=t_.a[:, 0:n], in0=t_.a[:, 0:n], in1=r2.a[:, 2, 0:n], op=ALU.add),
                                     [t_, r2], [t_])
                                k.op("act", lambda e: e.activation(out=z.a[:, kk, n0:n0 + n], in_=t_.a[:, 0:n], func=AF.Silu,
                                                                   bias=vcol(f"lnb{l}", kk), scale=vcol(f"lng{l}", kk)), [t_, V], [z])
                        it = 0
                        for oc in range(KD):
                            wb_ = w2s.get(ti * KD + oc)
                            for (n0, n) in split(Wo, 512):
                                pa = psA[it % 2]
                                it += 1
                                for kk in range(KD):
                                    k.op("pe", lambda e: e.matmul(pa.a[:, 0:n], lhsT=wb_.a[:, kk, :], rhs=z.a[:, kk, n0:n0 + n],
                                                                  start=(kk == 0), stop=(kk == KD - 1)), [wb_, z], [pa], inc=(kk == KD - 1))
                                k.op("act", lambda e: e.activation(out=ybuf.a[:, oc, n0:n0 + n], in_=pa.a[:, 0:n], func=AF.Identity,
                                                                   bias=vcol(f"b2{l}", oc), scale=1.0), [pa, V], [ybuf])
                        resid_out(ybuf, Wo, sc_off(l, 1), xin, txin, xout, txout, a, b, sq, rstd, tmp, xres, psS)
                    k.barrier()

            def ffn_part1(l, a, b, h, hid, bufs):
                (cg, cv, sl, wins, wouts, ti_, psU, psV, psY) = bufs
                Wo = b - a
                fo = VO[f"fdw{l}"]
                bo = VO[f"fdwb{l}"]
                ntl = split(Wo, 510)
                it = 0
                for i in range(KF):
                    wb_ = wins.get(ti_ * KF + i)
                    for (o0, no) in ntl:
                        n = no + 2
                        s_ = it % 2
                        pu, pv = psU[it % len(psU)], psV[it % len(psV)]
                        it += 1
                        for kk in range(KD):
                            k.op("pe", lambda e: e.matmul(pu.a[:, 0:n], lhsT=wb_.a[:, kk, 0:128], rhs=h.a[:, kk, o0:o0 + n],
                                                          start=(kk == 0), stop=(kk == KD - 1)), [wb_, h], [pu], inc=(kk == KD - 1))
                        for kk in range(KD):
                            k.op("pe", lambda e: e.matmul(pv.a[:, 0:n], lhsT=wb_.a[:, kk, 128:256], rhs=h.a[:, kk, o0:o0 + n],
                                                          start=(kk == 0), stop=(kk == KD - 1)), [wb_, h], [pv], inc=(kk == KD - 1))
                        for (cc, ch, pp_) in ((cg[s_], i, pu), (cv[s_], KF + i, pv)):
                            k.op("act", lambda e: e.activation(out=cc.a[:, 0:no], in_=pp_.a[:, 2:2 + no], func=AF.Identity,
                                                               scale=V.a[:, fo + 2 * 2 * KF + ch:fo + 2 * 2 * KF + ch + 1],
                                                               bias=V.a[:, bo + ch:bo + ch + 1]), [pp_, V], [cc])
                            k.op("dve", lambda e: e.scalar_tensor_tensor(out=cc.a[:, 0:no], in0=pp_.a[:, 1:1 + no], scalar=V.a[:, fo + 2 * KF + ch:fo + 2 * KF + ch + 1],
                                                                         in1=cc.a[:, 0:no], op0=ALU.mult, op1=ALU.add), [pp_, V, cc], [cc])
                            k.op("dve", lambda e: e.scalar_tensor_tensor(out=cc.a[:, 0:no], in0=pp_.a[:, 0:no], scalar=V.a[:, fo + ch:fo + ch + 1],
                                                                         in1=cc.a[:, 0:no], op0=ALU.mult, op1=ALU.add), [pp_, V, cc], [cc])
                        k.op("act", lambda e: e.activation(out=sl[s_].a[:, 0:no], in_=cg[s_].a[:, 0:no], func=AF.Silu), [cg[s_]], [sl[s_]])
                        k.op("pool", lambda e: e.tensor_tensor(out=hid.a[:, i, o0:o0 + no], in0=sl[s_].a[:, 0:no], in1=cv[s_].a[:, 0:no], op=ALU.mult),
                             [sl[s_], cv[s_]], [hid])

            def ffn_part2(l, a, b, hid, y, bufs):
                (cg, cv, sl, wins, wouts, ti_, psU, psV, psY) = bufs
                Wo = b - a
                it = 0
                for oc in range(KD):
                    wb_ = wouts.get(ti_ * KD + oc)
                    for (n0, n) in split(Wo, 512):
                        py = psY[it % 2]
                        it += 1
                        for kk in range(KF):
                            k.op("pe", lambda e: e.matmul(py.a[:, 0:n], lhsT=wb_.a[:, kk, :], rhs=hid.a[:, kk, n0:n0 + n],
                                                          start=(kk == 0), stop=(kk == KF - 1)), [wb_, hid], [py], inc=(kk == KF - 1))
                        k.op("act", lambda e: e.activation(out=y.a[:, oc, n0:n0 + n], in_=py.a[:, 0:n], func=AF.Copy), [py], [y])

            def ffn_sublayer(l, Hs, xin, txin, xout, txout, emit_xn=False, is_final=False):
                H = 2
                Ts = min(704, To)
                WM = tile_w(Hs, Ts) + H
                with ExitStack() as st:
                    xt = sb(st, "f_xt", [128, KD, WM], F32)
                    y = sb(st, "f_y", [128, KD, WM], F32)
                    hid = sb(st, "f_hid", [128, KF, WM], BF16)
                    sq = sb(st, "f_sq", [128, KD, WM], BF16)
                    xnb = sb(st, "f_xnb", [128, KD, WM], BF16) if emit_xn else None
                    rstd = sb(st, "f_rstd", [128, WM], F32)
                    tmp = [sb(st, f"f_tmp{i}", [128, WM], F32) for i in range(2)]
                    xres = [sb(st, f"f_xr{i}", [128, WM], F32) for i in range(2)]
                    h = sb(st, "f_h", [128, KD, WM], BF16)
                    cg = [sb(st, f"f_cg{i}", [128, 512], F32) for i in range(2)]
                    cv = [sb(st, f"f_cv{i}", [128, 512], F32) for i in range(2)]
                    sl = [sb(st, f"f_sl{i}", [128, 512], F32) for i in range(2)]
                    winb = [sb(st, f"f_win{i}", [128, KD, 256], BF16) for i in range(2)]
                    woutb = [sb(st, f"f_wout{i}", [128, KF, 128], BF16) for i in range(2)]
                    psU = [pst(st, f"f_psU{i}", [128, 512]) for i in range(3)]
                    psV = [pst(st, f"f_psV{i}", [128, 512]) for i in range(3)]
                    psS = [pst(st, f"f_psS{i}", [128, 512]) for i in range(2)]
                    psY = [psU[0], psV[0]]
                    ftiles = tiles_for(Hs, Ts)
                    wins = WStream(k, winb, [winu[l, i] for _ in ftiles for i in range(KF)])
                    wouts = WStream(k, woutb, [woutu[l, oc] for _ in ftiles for oc in range(KD)])
                    wins.get(0)
                    wouts.get(0)

                    def nin(ti):
                        a, b = ftiles[ti]
                        norm_in(st, xin, txin, a, H, b, sc_off(l, 2), l, 3, xt, sq, rstd, tmp, h, psS, (Hs + H) if ti == 0 else 0)
                    nin(0)
                    for ti, (a, b) in enumerate(ftiles):
                        bufs = (cg, cv, sl, wins, wouts, ti, psU, psV, psY)
                        Wo = b - a
                        ffn_part1(l, a, b, h, hid, bufs)
                        if ti + 1 < len(ftiles):
                            nin(ti + 1)
                        ffn_part2(l, a, b, hid, y, bufs)
                        resid_out(y, Wo, sc_off(l, 3), xin, txin, xout, txout, a, b, sq, rstd, tmp, xres, psS,
                                  emit_xn=xnb, is_final=is_final)
                    k.barrier()

            def kvq_phase(j, do_kv):
                l = NA + j
                NT = 512
                OW = OH * 128
                with ExitStack() as st:
                    xn = [sb(st, f"p_xn{i}", [128, KD, NT], BF16) for i in range(3)]
                    wf = sb(st, "p_wf", [128, KD, OW], F32)
                    wqb = sb(st, "p_wq", [128, KD, OW], BF16)
                    wkb = sb(st, "p_wk", [128, KD, OW], BF16)
                    wvb = sb(st, "p_wv", [128, KD, OW], BF16)
                    qbias = sb(st, "p_qb", [128, OH], F32)
                    qo = [sb(st, f"p_qo{i}", [128, OH, NT], BF16) for i in range(2)]
                    ko = [sb(st, f"p_ko{i}", [128, OH, NT], BF16) for i in range(2)]
                    vo = [sb(st, f"p_vo{i}", [128, NT // 128, OW], BF16) for i in range(2)]
                    pq = [pst(st, f"p_pq{i}", [128, 512]) for i in range(2)]
                    pk = [pst(st, f"p_pk{i}", [128, 512]) for i in range(2)]
                    pv = [pst(st, f"p_pv{i}", [128, 512]) for i in range(2)]
                    k.dma("sp", wf.a[:].rearrange("p k j -> p (k j)"), wq_d[j], writes=[wf])
                    for m_ in range(OH):
                        for kk in range(KD):
                            k.op("pe", lambda e: e.matmul(pq[0].a[:, m_:m_ + 1], lhsT=wf.a[:, kk, m_ * 128:(m_ + 1) * 128], rhs=modcol(l, 0, kk),
                                                          start=(kk == 0), stop=(kk == KD - 1)), [wf, modT], [pq[0]], inc=(kk == KD - 1))
                    k.op("dve", lambda e: e.tensor_copy(out=qbias.a[:], in_=pq[0].a[:, 0:OH]), [pq[0]], [qbias])
                    for kk in range(KD):
                        k.op("dve", lambda e: e.tensor_scalar(out=wqb.a[:, kk, :], in0=wf.a[:, kk, :], scalar1=scv(sc_off(l, 0), kk), scalar2=None,
                                                              op0=ALU.mult), [wf, SC], [wqb])
                    if do_kv:
                        for (wd, wb_) in ((wk_d, wkb), (wv_d, wvb)):
                            k.dma("sp", wf.a[:].rearrange("p k j -> p (k j)"), wd, writes=[wf])
                            for kk in range(KD):
                                k.op("dve", lambda e: e.tensor_scalar(out=wb_.a[:, kk, :], in0=wf.a[:, kk, :], scalar1=scv(KVG_O, kk), scalar2=None,
                                                                      op0=ALU.mult), [wf, SC], [wb_])
                    per = RC // 128
                    itq = itk = itv = 0
                    ntile = S // NT

                    def load_xn(gi):
                        g0 = gi * NT
                        rr, t0 = g0 // To, g0 % To
                        for i in range(NG):
                            k.dma("sp", xn[gi % 3].a[:, i * per:(i + 1) * per, :],
                                  xndst[i][rr * RC:(rr + 1) * RC, t0:t0 + NT].rearrange("(k p) t -> p k t", p=128),
                                  reads=[t_xndst[i]], writes=[xn[gi % 3]])
                    load_xn(0)
                    for gi in range(ntile):
                        g0 = gi * NT
                        s_ = gi % 2
                        x_ = xn[gi % 3]
                        if gi + 1 < ntile:
                            load_xn(gi + 1)
                        for m_ in range(OH):
                            p_ = pq[itq % 2]
                            itq += 1
                            for kk in range(KD):
                                k.op("pe", lambda e: e.matmul(p_.a[:, 0:NT], lhsT=wqb.a[:, kk, m_ * 128:(m_ + 1) * 128], rhs=x_.a[:, kk, :],
                                                              start=(kk == 0), stop=(kk == KD - 1)), [wqb, x_], [p_], inc=(kk == KD - 1))
                            k.op("act", lambda e: e.activation(out=qo[s_].a[:, m_, :], in_=p_.a[:, 0:NT], func=AF.Identity, bias=qbias.a[:, m_:m_ + 1], scale=1.0),
                                 [p_, qbias], [qo[s_]])
                        k.dma("sp", Qt.rearrange("h p s -> p h s")[:, :, g0:g0 + NT], qo[s_].a[:], reads=[qo[s_]], writes=[tQt])
                        if do_kv:
                            for m_ in range(OH):
                                p_ = pk[itk % 2]
                                itk += 1
                                for kk in range(KD):
                                    k.op("pe", lambda e: e.matmul(p_.a[:, 0:NT], lhsT=wkb.a[:, kk, m_ * 128:(m_ + 1) * 128], rhs=x_.a[:, kk, :],
                                                                  start=(kk == 0), stop=(kk == KD - 1)), [wkb, x_], [p_], inc=(kk == KD - 1))
                                k.op("dve", lambda e: e.tensor_copy(out=ko[s_].a[:, m_, :], in_=p_.a[:, 0:NT]), [p_], [ko[s_]])
                            k.dma("sp", Kt.rearrange("h p s -> p h s")[:, :, g0:g0 + NT], ko[s_].a[:], reads=[ko[s_]], writes=[tKt])
                            for sbk in range(NT // 128):
                                p_ = pv[itv % 2]
                                itv += 1
                                for kk in range(KD):
                                    k.op("pe", lambda e: e.matmul(p_.a[:, 0:OW], lhsT=x_.a[:, kk, sbk * 128:(sbk + 1) * 128], rhs=wvb.a[:, kk, :],
                                                                  start=(kk == 0), stop=(kk == KD - 1)), [wvb, x_], [p_], inc=(kk == KD - 1))
                                k.op("act" if sbk % 2 else "dve",
                                     (lambda e: e.activation(out=vo[s_].a[:, sbk, :], in_=p_.a[:, 0:OW], func=AF.Copy)) if sbk % 2 else
                                     (lambda e: e.tensor_copy(out=vo[s_].a[:, sbk, :], in_=p_.a[:, 0:OW])), [p_], [vo[s_]])
                            for m_ in range(OH):
                                k.dma("sp", Vd[m_][:, g0 // 128:g0 // 128 + NT // 128, :], vo[s_].a[:, :, m_ * 128:(m_ + 1) * 128],
                                      reads=[vo[s_]], writes=[tVd])
                    k.barrier()

            def attn_phase(j):
                QT = 512
                NB = S // 128
                NQ = S // QT
                XS = 2 * QT
                with ExitStack() as st:
                    strips = sb(st, "a_strips", [128, OH, 1024], F32)
                    ktb = [sb(st, f"a_kt{i}", [128, S], BF16) for i in range(2)]
                    vbb = [sb(st, f"a_vb{i}", [128, NB, 128], BF16) for i in range(2)]
                    qb = [sb(st, f"a_q{i}", [128, QT], BF16) for i in range(3)]
                    P = [sb(st, f"a_P{i}", [128, 2, QT], BF16) for i in range(4)]
                    sbias = [sb(st, f"a_sb{i}", [128, 2, QT], F32) for i in range(2)]
                    accD = [sb(st, f"a_accD{i}", [128, XS], F32) for i in range(2)]
                    accP = [sb(st, f"a_accP{i}", [128, max(2 * QT - XS, 1)], F32) for i in range(2)]
                    ones_f = sb(st, "a_onesf", [128, 128], F32)
                    c1 = sb(st, "a_c1", [128, QT], F32)
                    c2 = sb(st, "a_c2", [128, QT], F32)
                    R1 = sb(st, "a_R1", [128, QT], F32)
                    R2 = sb(st, "a_R2", [128, QT], F32)
                    t1 = sb(st, "a_t1", [128, QT], F32)
                    t2 = sb(st, "a_t2", [128, QT], F32)
                    osq = sb(st, "a_osq", [128, QT], BF16)
                    rs = sb(st, "a_rs", [128, QT], F32)
                    ob = [sb(st, f"a_ob{i}", [128, QT], BF16) for i in range(2)]
                    psS = [pst(st, f"a_psS{i}", [128, 2 * QT]) for i in range(2)]
                    psO1 = pst(st, "a_psO1", [128, QT])
                    psO2 = pst(st, "a_psO2", [128, QT])
                    psZa = pst(st, "a_psZa", [128, QT])
                    psZb = pst(st, "a_psZb", [128, QT])
                    hank = sb(st, "a_hank", [128, 1024], F32)
                    k.op("dve", lambda e: e.memset(ones_f.a[:], 1.0), [], [ones_f])
                    for hl in range(OH):
                        k.dma("sp", hank.a[:], bass.AP(gvec.tensor, hl * (LG + 1), [[1, 128], [1, 1024]]), reads=[tgv], writes=[hank])
                        for half in range(2):
                            k.op("pe", lambda e: e.matmul(psO1.a[:], lhsT=antif.a[:], rhs=hank.a[:, half * 512:(half + 1) * 512], start=True, stop=True),
                                 [antif, hank], [psO1])
                            k.op("act", lambda e: e.activation(out=strips.a[:, hl, half * 512:(half + 1) * 512], in_=psO1.a[:], func=AF.Copy),
                                 [psO1], [strips])
                    nlam = scv(NLAM_O + j)
                    sgc = scv(SG_O + j)
                    jobs = [(hl, qi) for hl in range(OH) for qi in range(NQ)]
                    blocks = []
                    for ji, (hl, qi) in enumerate(jobs):
                        for kb in range(4 * qi + 4):
                            blocks.append((ji, hl, qi, kb))

                    def load_head(hl):
                        k.dma("sp", ktb[hl % 2].a[:], Kt[hl], reads=[tKt], writes=[ktb[hl % 2]])
                        k.dma("sp", vbb[hl % 2].a[:], Vd[hl], reads=[tVd], writes=[vbb[hl % 2]])

                    def load_q(ji):
                        hl, qi = jobs[ji]
                        k.dma("sp", qb[ji % 3].a[:], Qt[hl][:, qi * QT:(qi + 1) * QT], reads=[tQt], writes=[qb[ji % 3]])

                    def s_mm(bi):
                        ji, hl, qi, kb = blocks[bi]
                        ps_ = psS[bi % 2]
                        kt_, q_ = ktb[hl % 2], qb[ji % 3]
                        k.op("pe", lambda e: e.matmul(ps_.a[:, 0:QT], lhsT=kt_.a[0:64, kb * 128:(kb + 1) * 128], rhs=q_.a[0:64, :],
                                                      start=True, stop=True), [kt_, q_], [ps_], inc=False)
                        k.op("pe", lambda e: e.matmul(ps_.a[:, QT:2 * QT], lhsT=kt_.a[64:128, kb * 128:(kb + 1) * 128], rhs=q_.a[64:128, :],
                                                      start=True, stop=True), [kt_, q_], [ps_])

                    def bias_op(bi):
                        ji, hl, qi, kb = blocks[bi]
                        dd = (kb - 4 * qi) * 128
                        if dd < -128:
                            return
                        c0 = 384 - dd
                        ps_, sb_ = psS[bi % 2], sbias[bi % 2]
                        k.op("dve", lambda e: e.scalar_tensor_tensor(
                            out=sb_.a[:], in0=ps_.a[:].rearrange("p (c q) -> p c q", c=2), scalar=0.125,
                            in1=strips.a[:, hl, c0:c0 + QT].unsqueeze(1).broadcast_to([128, 2, QT]), op0=ALU.mult, op1=ALU.add),
                            [ps_, strips], [sb_])

                    pending = []

                    def epilogue(ji, hl, qi, bi):
                        aD, aP = accD[ji % 2], accP[ji % 2]
                        o_ = ob[ji % 2]
                        k.op("dve", lambda e: e.tensor_copy(out=c1.a[:], in_=psO1.a[:]), [psO1], [c1])
                        k.op("act", lambda e: e.activation(out=c2.a[:], in_=psO2.a[:], func=AF.Copy), [psO2], [c2])

                        def st1():
                            k.op("pe", lambda e: e.matmul(psZa.a[:], lhsT=ones_f.a[:], rhs=aD.a[:, 0:QT], start=True, stop=True), [ones_f, aD], [psZa])
                            if XS < 2 * QT:
                                k.op("pe", lambda e: e.matmul(psZb.a[:, 0:XS - QT], lhsT=ones_f.a[:], rhs=aD.a[:, QT:XS], start=True, stop=True),
                                     [ones_f, aD], [psZb], inc=False)
                                k.op("pe", lambda e: e.matmul(psZb.a[:, XS - QT:QT], lhsT=ones_f.a[:], rhs=aP.a[:, :], start=True, stop=True),
                                     [ones_f, aP], [psZb])
                            else:
                                k.op("pe", lambda e: e.matmul(psZb.a[:], lhsT=ones_f.a[:], rhs=aD.a[:, QT:2 * QT], start=True, stop=True),
                                     [ones_f, aD], [psZb])

                        def st2():
                            k.op("act", lambda e: e.activation(out=R1.a[:], in_=psZa.a[:], func=AF.Ln), [psZa], [R1])
                            k.op("act", lambda e: e.activation(out=R2.a[:], in_=psZb.a[:], func=AF.Ln), [psZb], [R2])
                            k.op("act", lambda e: e.activation(out=R1.a[:], in_=R1.a[:], func=AF.Exp, scale=-1.0), [R1], [R1])
                            k.op("act", lambda e: e.activation(out=R2.a[:], in_=R2.a[:], func=AF.Exp, scale=-1.0), [R2], [R2])
                            k.op("dve", lambda e: e.tensor_tensor(out=t1.a[:], in0=c1.a[:], in1=R1.a[:], op=ALU.mult), [c1, R1], [t1])
                            k.op("dve", lambda e: e.tensor_tensor(out=t2.a[:], in0=c2.a[:], in1=R2.a[:], op=ALU.mult), [c2, R2], [t2])
                            k.op("dve", lambda e: e.scalar_tensor_tensor(out=t1.a[:], in0=t2.a[:], scalar=nlam, in1=t1.a[:], op0=ALU.mult, op1=ALU.add),
                                 [t2, t1, SC], [t1])
                            k.op("act", lambda e: e.activation(out=osq.a[:], in_=t1.a[:], func=AF.Square), [t1], [osq])

                        def st3():
                            k.op("pe", lambda e: e.matmul(psZa.a[:], lhsT=ones_bf.a[:], rhs=osq.a[:], start=True, stop=True), [ones_bf, osq], [psZa])

                        def st4():
                            k.op("act", lambda e: e.activation(out=rs.a[:], in_=psZa.a[:], func=AF.Ln, bias=128.0 * EPS, scale=1.0), [psZa], [rs])
                            k.op("act", lambda e: e.activation(out=rs.a[:], in_=rs.a[:], func=AF.Exp, scale=-0.5), [rs], [rs])
                            k.op("dve", lambda e: e.tensor_tensor(out=t2.a[:], in0=t1.a[:], in1=rs.a[:], op=ALU.mult), [t1, rs], [t2])
                            k.op("act", lambda e: e.activation(out=o_.a[:], in_=t2.a[:], func=AF.Identity, scale=sgc), [t2, SC], [o_])
                            k.dma("sp", Osrc[hl][:, qi * QT:(qi + 1) * QT], o_.a[:], reads=[o_], writes=[t_osrc[hl]])
                            if qi == NQ - 1:
                                k.coll(Osrc[hl], Odst[hl], reads=[t_osrc[hl]], writes=[t_odst[hl]])
                        for d, f in enumerate((st1, st2, st3, st4)):
                            pending.append((bi + 1 + d, f))

                    load_head(0)
                    load_q(0)
                    if len(jobs) > 1:
                        load_q(1)
                    s_mm(0)
                    bias_op(0)
                    for bi, (ji, hl, qi, kb) in enumerate(blocks):
                        nkb = 4 * qi + 4
                        if kb == 0:
                            if ji + 2 < len(jobs):
                                load_q(ji + 2)
                            if qi == 0 and hl + 1 < OH:
                                load_head(hl + 1)
                        if bi + 1 < len(blocks):
                            s_mm(bi + 1)
                            bias_op(bi + 1)
                        ps_ = psS[bi % 2]
                        p_ = P[bi % 4]
                        sb_ = sbias[bi % 2]
                        vb_ = vbb[hl % 2]
                        aD, aP = accD[ji % 2], accP[ji % 2]
                        dd = (kb - 4 * qi) * 128
                        if dd >= -128:
                            k.op("act", lambda e: e.activation(out=p_.a[:], in_=sb_.a[:], func=AF.Exp), [sb_], [p_])
                        else:
                            k.op("act", lambda e: e.activation(out=p_.a[:], in_=ps_.a[:].rearrange("p (c q) -> p c q", c=2), func=AF.Exp,
                                                               scale=0.125), [ps_], [p_])
                        first, last = (kb == 0), (kb == nkb - 1)
                        k.op("pe", lambda e: e.matmul(psO1.a[:], lhsT=vb_.a[:, kb, :], rhs=p_.a[:, 0, :], start=first, stop=last),
                             [vb_, p_], [psO1], inc=False)
                        k.op("pe", lambda e: e.matmul(psO2.a[:], lhsT=vb_.a[:, kb, :], rhs=p_.a[:, 1, :], start=first, stop=last),
                             [vb_, p_], [psO2])
                        pf = p_.a[:].rearrange("p c q -> p (c q)")
                        if first:
                            k.op("dve", lambda e: e.tensor_copy(out=aD.a[:], in_=pf[:, 0:XS]), [p_], [aD])
                            if XS < 2 * QT:
                                k.op("pool", lambda e: e.tensor_copy(out=aP.a[:], in_=pf[:, XS:2 * QT]), [p_], [aP])
                        else:
                            k.op("dve", lambda e: e.tensor_tensor(out=aD.a[:], in0=pf[:, 0:XS], in1=aD.a[:], op=ALU.add), [p_, aD], [aD])
                            if XS < 2 * QT:
                                k.op("pool", lambda e: e.tensor_tensor(out=aP.a[:], in0=pf[:, XS:2 * QT], in1=aP.a[:], op=ALU.add), [p_, aP], [aP])
                        due = [f for (d, f) in pending if d <= bi]
                        pending[:] = [(d, f) for (d, f) in pending if d > bi]
                        for f in due:
                            f()
                        if last:
                            epilogue(ji, hl, qi, bi)
                    for (d, f) in sorted(pending, key=lambda t: t[0]):
                        f()
                    k.barrier()

            def attnout_sublayer(j, Hs, xin, txin, xout, txout):
                l = NA + j
                Ts = min(512, To)
                WM = tile_w(Hs, Ts)
                with ExitStack() as st:
                    oa2 = [sb(st, f"o_oa{i}", [128, NH, WM], BF16) for i in range(2)]
                    ob2 = [sb(st, f"o_ob{i}", [128, NH, WM], BF16) for i in range(2)]
                    osel = sb(st, "o_osel", [128, NH, WM], BF16)
                    y = sb(st, "o_y", [128, KD, WM], F32)
                    sq = sb(st, "o_sq", [128, KD, WM], BF16)
                    rstd = sb(st, "o_rstd", [128, WM], F32)
                    tmp = [sb(st, f"o_tmp{i}", [128, WM], F32) for i in range(2)]
                    xres = [sb(st, f"o_xr{i}", [128, WM], F32) for i in range(2)]
                    wob = [sb(st, f"o_wo{i}", [128, NH, 128], BF16) for i in range(2)]
                    psY = [pst(st, f"o_psY{i}", [128, 512]) for i in range(2)]
                    psS = [pst(st, f"o_psS{i}", [128, 512]) for i in range(2)]
                    otiles = tiles_for(Hs, Ts)
                    wos = WStream(k, wob, [wou[j, oc] for _ in otiles for oc in range(KD)])
                    wos.get(0)
                    def load_o(ti):
                        a, b = otiles[ti]
                        oa, obb = oa2[ti % 2], ob2[ti % 2]
                        W = b - a
                        neg = max(0, -a)
                        if neg > 0:
                            k.op("dve", lambda e: e.memset(oa.a[:, :, 0:neg], 0.0), [], [oa])
                        for hl in range(OH):
                            for rr in range(2):
                                hg = rr * OH + hl
                                k.dma("sp", oa.a[:, hg, neg:W], Odst[hl][rr * 128:(rr + 1) * 128, a + neg:b], reads=[t_odst[hl]], writes=[oa])
                                k.dma("act", obb.a[:, hg, 0:W], Odst[hl][rr * 128:(rr + 1) * 128, To + a:To + b], reads=[t_odst[hl]], writes=[obb])
                    load_o(0)
                    for ti, (a, b) in enumerate(otiles):
                        W = b - a
                        neg = max(0, -a)
                        oa, obb = oa2[ti % 2], ob2[ti % 2]
                        if ti + 1 < len(otiles):
                            load_o(ti + 1)
                        k.op("dve", lambda e: e.tensor_scalar(out=osel.a[:, :, 0:W], in0=oa.a[:, :, 0:W], scalar1=vcol("hmc"), scalar2=None, op0=ALU.mult),
                             [oa, V], [osel])
                        k.op("dve", lambda e: e.scalar_tensor_tensor(out=osel.a[:, :, 0:W], in0=obb.a[:, :, 0:W], scalar=vcol("hm"), in1=osel.a[:, :, 0:W],
                                                                     op0=ALU.mult, op1=ALU.add), [obb, osel, V], [osel])
                        it = 0
                        for oc in range(KD):
                            wb_ = wos.get(ti * KD + oc)
                            for (n0, n) in split(W, 512):
                                py = psY[it % 2]
                                it += 1
                                for hh in range(NH):
                                    k.op("pe", lambda e: e.matmul(py.a[:, 0:n], lhsT=wb_.a[:, hh, :], rhs=osel.a[:, hh, n0:n0 + n],
                                                                  start=(hh == 0), stop=(hh == NH - 1)), [wb_, osel], [py], inc=(hh == NH - 1))
                                k.op("act", lambda e: e.activation(out=y.a[:, oc, n0:n0 + n], in_=py.a[:, 0:n], func=AF.Copy), [py], [y])
                        resid_out(y, W, sc_off(l, 1), xin, txin, xout, txout, a, b, sq, rstd, tmp, xres, psS)
                    k.barrier()

            HS = [38, 36, 6, 4, 4, 2, 2, 0]
            seq = [(x0, tX0), (XA, tXA), (XB, tXB)]
            state = {"cur": 0}

            def nxt():
                i = state["cur"]
                src = seq[0] if i == 0 else seq[1 + (i - 1) % 2]
                dst = seq[1 + i % 2]
                state["cur"] += 1
                return src[0], src[1], dst[0], dst[1]

            def finish_from(buf, tbuf):
                k.dma("sp", yout, buf[:, HL:HL + To], reads=[tbuf], writes=[tY])

            done = False
            si = 0
            for l in range(NA):
                xi, txi, xo, txo = nxt()
                mixer_sublayer(l, HS[si], xi, txi, xo, txo)
                si += 1
                if stop_after == f"m{l}":
                    finish_from(xo, txo)
                    done = True
                    break
                xi, txi, xo, txo = nxt()
                ffn_sublayer(l, HS[si], xi, txi, xo, txo, emit_xn=(l == NA - 1))
                si += 1
                if stop_after == f"f{l}":
                    finish_from(xo, txo)
                    done = True
                    break
            if not done:
                for j in range(L - NA):
                    l = NA + j
                    exchange_xn()
                    k.barrier()
                    kvq_phase(j, do_kv=(j == 0))
                    attn_phase(j)
                    xi, txi, xo, txo = nxt()
                    attnout_sublayer(j, HS[si], xi, txi, xo, txo)
                    si += 1
                    if stop_after == f"a{l}":
                        finish_from(xo, txo)
                        done = True
                        break
                    last = (l == L - 1)
                    xi, txi, xo, txo = nxt()
                    if last:
                        ffn_sublayer(l, HS[si], xi, txi, yout, tY, emit_xn=False, is_final=True)
                    else:
                        ffn_sublayer(l, HS[si], xi, txi, xo, txo, emit_xn=True)
                    si += 1
                    if stop_after == f"f{l}" and not last:
                        finish_from(xo, txo)
                        done = True
                        break
            k.barrier()
        nc._k_nins = k.nins
    return nc


_CACHE = {}


def run(cfg, inputs, stop_after=None, trace=False):
    key = (cfg.D, cfg.S, cfg.F, stop_after)
    if key not in _CACHE:
        _CACHE[key] = build(cfg, stop_after)
    nc = _CACHE[key]
    in_maps = [prepare_core(cfg, inputs, c) for c in range(8)]
    res = run_bass_kernel_spmd(nc, in_maps, core_ids=list(range(8)), **({"trace": True} if trace else {}))
    out = np.empty((cfg.B, cfg.S, cfg.D), np.float32)
    for c in range(8):
        b, r = c // 2, c % 2
        out[b, r * cfg.To:(r + 1) * cfg.To, :] = res.results[c]["yout"].T
    return out, res


def kernel(**inputs):
    inputs = {k_: np.asarray(v) for k_, v in inputs.items()}
    cfg = Cfg()
    out, _ = run(cfg, inputs)
    return out
```
